# Optimizing a Trainium2 kernel written in Bass

```python
import math
import jax, jax.numpy as jnp
from jax import lax
import numpy as np

D_MODEL = 4096
BATCH = 4
SEQ = 2048
DEPTH = 1

MIX_WIDTH = 2 * D_MODEL
D_SSM = MIX_WIDTH // 2
SSM_HEADDIM = 64
SSM_HEADS = D_SSM // SSM_HEADDIM
SSM_GROUPS = 8
SSM_HEADS_PER_GROUP = SSM_HEADS // SSM_GROUPS
SSM_STATE = 128
SSM_CHUNK = 128
D_XBC = D_SSM + 2 * SSM_GROUPS * SSM_STATE
D_LRU = MIX_WIDTH - D_SSM
LRU_BLOCK = 256
LRU_HEADS = D_LRU // LRU_BLOCK
LRU_C = 8.0
CONV_WIDTH = 4
NORM_EPS = 1e-5
DEEPNORM_ALPHA = (2.0 * DEPTH) ** 0.25
DEEPNORM_BETA = (8.0 * DEPTH) ** -0.25
SPLIT_SIZES = (D_SSM, D_XBC, SSM_HEADS, D_LRU, D_LRU)
SPLIT_POINTS = tuple(int(v) for v in np.cumsum(SPLIT_SIZES)[:-1])
D_IN_PROJ = sum(SPLIT_SIZES)

kernel_name = "hymba_style_ssd_rglru_deepnorm"


def _causal_depthwise_conv(u, w, b):
    ch = u.shape[-1]
    y = lax.conv_general_dilated(
        u, w[:, None, :].astype(u.dtype), window_strides=(1,),
        padding=[(w.shape[0] - 1, 0)],
        dimension_numbers=("NWC", "WIO", "NWC"), feature_group_count=ch)
    return y + b.astype(u.dtype)


def _segsum(a):
    t = a.shape[-1]
    a_rep = jnp.broadcast_to(a[..., None], a.shape + (t,))
    strict = jnp.tril(jnp.ones((t, t), dtype=bool), -1)
    seg = jnp.cumsum(jnp.where(strict, a_rep, 0.0), axis=-2)
    lower = jnp.tril(jnp.ones((t, t), dtype=bool))
    return jnp.where(lower, seg, -jnp.inf)


def _ssd_chunked(xh, dt, a, bm, cm):
    bsz, s = xh.shape[0], xh.shape[1]
    nc = s // SSM_CHUNK
    g, r, p, n = SSM_GROUPS, SSM_HEADS_PER_GROUP, SSM_HEADDIM, SSM_STATE
    x = (xh * dt[..., None]).reshape(bsz, nc, SSM_CHUNK, g, r, p)
    a_dt = (dt * a).reshape(bsz, nc, SSM_CHUNK, g, r).transpose(0, 3, 4, 1, 2)
    bm = bm.reshape(bsz, nc, SSM_CHUNK, g, n)
    cm = cm.reshape(bsz, nc, SSM_CHUNK, g, n)
    a_cs = jnp.cumsum(a_dt, axis=-1)
    lmat = jnp.exp(_segsum(a_dt))
    cb = jnp.einsum("bclgn,bcsgn->bgcls", cm, bm)
    y_diag = jnp.einsum("bgcls,bgrcls,bcsgrp->bclgrp", cb, lmat, x)
    decay_states = jnp.exp(a_cs[..., -1:] - a_cs)
    states = jnp.einsum("bclgn,bgrcl,bclgrp->bcgrpn", bm, decay_states, x)
    chunk_tot = jnp.pad(a_cs[..., -1], [(0, 0), (0, 0), (0, 0), (1, 0)])
    decay_chunk = jnp.exp(_segsum(chunk_tot))
    states = jnp.concatenate([jnp.zeros_like(states[:, :1]), states], axis=1)
    prev_states = jnp.einsum("bgrzc,bcgrpn->bzgrpn", decay_chunk, states)[:, :-1]
    y_off = jnp.einsum("bclgn,bcgrpn,bgrcl->bclgrp", cm, prev_states, jnp.exp(a_cs))
    return (y_diag + y_off).reshape(bsz, s, g, r, p)


def _lin_combine(c1, c2):
    a1, b1 = c1
    a2, b2 = c2
    return (a1 * a2, a2 * b1 + b2)


def _hybrid_layer(x, w_in, ssd_conv_w, ssd_conv_b, ssd_dt_bias, ssd_a_log, ssd_d,
                  ssd_norm_w, lru_conv_w, lru_conv_b, lru_wa, lru_ba, lru_wx, lru_bx,
                  lru_lambda, w_out, ln_g, ln_b):
    dtype = x.dtype
    bsz, s, _ = x.shape
    f32 = jnp.float32
    proj = x @ w_in.astype(dtype)
    z, xbc, dt_raw, lx, lg = jnp.split(proj, SPLIT_POINTS, axis=-1)

    xbc = jax.nn.silu(_causal_depthwise_conv(xbc, ssd_conv_w, ssd_conv_b))
    xs, bm, cm = jnp.split(xbc, (D_SSM, D_SSM + SSM_GROUPS * SSM_STATE), axis=-1)
    g, r = SSM_GROUPS, SSM_HEADS_PER_GROUP
    xs_h = xs.astype(f32).reshape(bsz, s, g, r, SSM_HEADDIM)
    bm = bm.astype(f32).reshape(bsz, s, g, SSM_STATE)
    cm = cm.astype(f32).reshape(bsz, s, g, SSM_STATE)
    dt = jax.nn.softplus(dt_raw.astype(f32) + ssd_dt_bias.astype(f32)).reshape(bsz, s, g, r)
    a = -jnp.exp(ssd_a_log.astype(f32)).reshape(g, r)
    y = _ssd_chunked(xs_h, dt, a, bm, cm) + ssd_d.astype(f32).reshape(g, r)[..., None] * xs_h
    y = y.reshape(bsz, s, D_SSM) * jax.nn.silu(z.astype(f32))
    yg = y.reshape(bsz, s, SSM_GROUPS, D_SSM // SSM_GROUPS)
    yg = yg * lax.rsqrt(jnp.mean(yg * yg, axis=-1, keepdims=True) + NORM_EPS)
    ssd_out = (yg.reshape(bsz, s, D_SSM) * ssd_norm_w.astype(f32)).astype(dtype)

    u = _causal_depthwise_conv(lx, lru_conv_w, lru_conv_b)
    ub = u.reshape(bsz, s, LRU_HEADS, LRU_BLOCK)
    gate_r = jax.nn.sigmoid(jnp.einsum("bshi,hij->bshj", ub, lru_wa.astype(dtype)) + lru_ba.astype(dtype))
    gate_i = jax.nn.sigmoid(jnp.einsum("bshi,hij->bshj", ub, lru_wx.astype(dtype)) + lru_bx.astype(dtype))
    gate_r = gate_r.reshape(bsz, s, D_LRU).astype(f32)
    gate_i = gate_i.reshape(bsz, s, D_LRU).astype(f32)
    log_a = -LRU_C * jax.nn.softplus(-lru_lambda.astype(f32)) * gate_r
    a_t = jnp.exp(log_a)
    b_t = jnp.sqrt(-jnp.expm1(2.0 * log_a)) * (gate_i * u.astype(f32))
    _, h = lax.associative_scan(_lin_combine, (a_t, b_t), axis=1)
    lru_out = (h * jax.nn.silu(lg.astype(f32))).astype(dtype)

    mix = jnp.concatenate([ssd_out, lru_out], axis=-1)
    res = (DEEPNORM_ALPHA * x + mix @ w_out.astype(dtype)).astype(f32)
    mu = jnp.mean(res, axis=-1, keepdims=True)
    var = jnp.mean(jnp.square(res - mu), axis=-1, keepdims=True)
    out = (res - mu) * lax.rsqrt(var + NORM_EPS) * ln_g.astype(f32) + ln_b.astype(f32)
    return out.astype(dtype)


def setup_inputs(seed: int = 0) -> dict:
    key = jax.random.key(seed)
    ks = jax.random.split(key, 20)
    f32 = jnp.float32
    L = DEPTH
    x = jax.random.normal(ks[0], (BATCH, SEQ, D_MODEL), f32)
    w_in = jax.random.normal(ks[1], (L, D_MODEL, D_IN_PROJ), f32) * D_MODEL ** -0.5
    ssd_conv_w = jax.random.normal(ks[2], (L, CONV_WIDTH, D_XBC), f32) * CONV_WIDTH ** -0.5
    ssd_conv_b = jax.random.normal(ks[3], (L, D_XBC), f32) * 0.01
    dt0 = jnp.exp(jax.random.uniform(ks[4], (L, SSM_HEADS), f32)
                  * (math.log(0.1) - math.log(0.001)) + math.log(0.001))
    ssd_dt_bias = dt0 + jnp.log(-jnp.expm1(-dt0))
    ssd_a_log = jnp.log(jax.random.uniform(ks[5], (L, SSM_HEADS), f32, 1.0, 16.0))
    ssd_d = 1.0 + 0.01 * jax.random.normal(ks[6], (L, SSM_HEADS), f32)
    ssd_norm_w = 1.0 + 0.01 * jax.random.normal(ks[7], (L, D_SSM), f32)
    lru_conv_w = jax.random.normal(ks[8], (L, CONV_WIDTH, D_LRU), f32) * CONV_WIDTH ** -0.5
    lru_conv_b = jax.random.normal(ks[9], (L, D_LRU), f32) * 0.01
    lru_wa = jax.random.normal(ks[10], (L, LRU_HEADS, LRU_BLOCK, LRU_BLOCK), f32) * LRU_BLOCK ** -0.5
    lru_ba = jax.random.normal(ks[11], (L, LRU_HEADS, LRU_BLOCK), f32) * 0.01
    lru_wx = jax.random.normal(ks[12], (L, LRU_HEADS, LRU_BLOCK, LRU_BLOCK), f32) * LRU_BLOCK ** -0.5
    lru_bx = jax.random.normal(ks[13], (L, LRU_HEADS, LRU_BLOCK), f32) * 0.01
    a_c = jax.random.uniform(ks[14], (L, D_LRU), f32, 0.9, 0.999)
    a_base = a_c ** (1.0 / LRU_C)
    lru_lambda = jnp.log(a_base) - jnp.log1p(-a_base)
    w_out = jax.random.normal(ks[15], (L, MIX_WIDTH, D_MODEL), f32) * (MIX_WIDTH ** -0.5) * DEEPNORM_BETA
    ln_g = 1.0 + 0.01 * jax.random.normal(ks[16], (L, D_MODEL), f32)
    ln_b = 0.01 * jax.random.normal(ks[17], (L, D_MODEL), f32)
    return {"x": x, "w_in": w_in, "ssd_conv_w": ssd_conv_w, "ssd_conv_b": ssd_conv_b,
            "ssd_dt_bias": ssd_dt_bias, "ssd_a_log": ssd_a_log, "ssd_d": ssd_d,
            "ssd_norm_w": ssd_norm_w, "lru_conv_w": lru_conv_w, "lru_conv_b": lru_conv_b,
            "lru_wa": lru_wa, "lru_ba": lru_ba, "lru_wx": lru_wx, "lru_bx": lru_bx,
            "lru_lambda": lru_lambda, "w_out": w_out, "ln_g": ln_g, "ln_b": ln_b}


def reference(x, w_in, ssd_conv_w, ssd_conv_b, ssd_dt_bias, ssd_a_log, ssd_d, ssd_norm_w,
              lru_conv_w, lru_conv_b, lru_wa, lru_ba, lru_wx, lru_bx, lru_lambda,
              w_out, ln_g, ln_b):
    h = x
    for layer in range(DEPTH):
        h = _hybrid_layer(h, w_in[layer], ssd_conv_w[layer], ssd_conv_b[layer],
                          ssd_dt_bias[layer], ssd_a_log[layer], ssd_d[layer],
                          ssd_norm_w[layer], lru_conv_w[layer], lru_conv_b[layer],
                          lru_wa[layer], lru_ba[layer], lru_wx[layer], lru_bx[layer],
                          lru_lambda[layer], w_out[layer], ln_g[layer], ln_b[layer])
    return h
```

```python
import numpy as np
import ml_dtypes
import concourse.bass as bass
import concourse.mybir as mybir
from concourse.bass_utils import run_bass_kernel_spmd

F32, BF16 = mybir.dt.float32, mybir.dt.bfloat16
AF = mybir.ActivationFunctionType
ALU = mybir.AluOpType
AX = mybir.AxisListType

D = 4096
NTOK = 1024
NCH = 8
KC = 32
G = 8
NLT = 32
ALPHA = 2.0 ** 0.25
EPS = 1e-5
ENGS = ("pe", "act", "dve", "pool", "sp")

P_DTB, P_ALOG, P_DH, P_CONV, P_NORMW, P_LBA, P_LBX, P_LAM, P_FLAG, NPAR = 0, 64, 128, 192, 592, 624, 656, 688, 720, 721

def _blocks():
    blks = [("dt", list(range(10240, 10304)))]
    for g in range(G):
        blks.append((f"B{g}", list(range(8192 + 128 * g, 8192 + 128 * (g + 1)))))
        blks.append((f"C{g}", list(range(9216 + 128 * g, 9216 + 128 * (g + 1)))))
        for hb in range(2):
            blks.append((f"xs{g}_{hb}", list(range(4096 + 512 * g + 256 * hb, 4096 + 512 * g + 256 * (hb + 1)))))
        for hb in range(2):
            blks.append((f"z{g}_{hb}", list(range(512 * g + 256 * hb, 512 * g + 256 * (hb + 1)))))
    for k in range(16):
        blks.append((f"lx{k}", list(range(10304 + 256 * k, 10304 + 256 * (k + 1)))))
        blks.append((f"lg{k}", list(range(14400 + 256 * k, 14400 + 256 * (k + 1)))))
    return blks

BLOCKS = _blocks()
BLK_OFF = {}
_o = 0
for _n, _c in BLOCKS:
    BLK_OFF[_n] = (_o, len(_c))
    _o += KC * len(_c)
WIN_COLS = _o


class Buf:
    __slots__ = ("ap", "w", "r", "sem")

    def __init__(self, ap, sem=None):
        self.ap, self.w, self.r, self.sem = ap, None, [], sem


class Prog:
    def __init__(self):
        self.ops = {e: [] for e in ENGS}
        self.cnt = {e: 0 for e in ENGS}
        self.dcnt = {}
        self.seq = 0
        self.limit = None
        self.marks = []

    def _hz(self, reads, writes, extra):
        waits = [b.w for b in reads if b.w is not None]
        for b in writes:
            waits += b.r
            if b.w is not None:
                waits.append(b.w)
        waits += [t for t in extra if t is not None]
        return waits

    def op(self, eng, fn, reads=(), writes=(), extra=()):
        self.seq += 1
        if self.limit is not None and self.seq > self.limit:
            return None
        waits = self._hz(reads, writes, extra)
        self.cnt[eng] += 1
        tok = (eng, self.cnt[eng])
        self.ops[eng].append((fn, waits, tok))
        for b in reads:
            b.r.append(tok)
        for b in writes:
            b.w, b.r = tok, []
        return tok

    def dma(self, eng, fn, sem, reads=(), writes=(), extra=()):
        self.seq += 1
        if self.limit is not None and self.seq > self.limit:
            return None
        waits = self._hz(reads, writes, extra)
        self.dcnt[sem] = self.dcnt.get(sem, 0) + 16
        tok = ("d:" + sem, self.dcnt[sem])
        self.ops[eng].append((fn, waits, tok))
        for b in reads:
            b.r.append(tok)
        for b in writes:
            b.w, b.r = tok, []
        return tok


def build_nc(limit=None):
    nc = bass.Bass("TRN2", target_bir_lowering=False)
    dram = {}

    def din(name, shape, dt=F32):
        dram[name] = nc.dram_tensor(name, list(shape), dt, kind="ExternalInput").ap()
        return dram[name]

    xT = din("xT", [2, 128, KC * NTOK])
    xres = din("xres", [NTOK, D])
    win = din("win", [128, WIN_COLS])
    wout = din("wout", [16, 128, 64 * 256])
    lruw = din("lruw", [16, 128, 1024])
    params = din("params", [128, NPAR])
    consts = din("consts", [128, 512])
    lngb = din("lngb", [128, 2 * D])
    out = nc.dram_tensor("out", [NTOK, D], F32, kind="ExternalOutput").ap()
    mixT = nc.dram_tensor("mixT_scr", [64, 128, NTOK], BF16, kind="Internal").ap()
    pre_scr = nc.dram_tensor("pre_scr", [NTOK, D], F32, kind="Internal").ap()

    ARENA = 101120
    arena_t = nc.alloc_sbuf_tensor("arena", [128, ARENA], BF16)
    small_t = nc.alloc_sbuf_tensor("small", [128, 1728], F32)
    idb_t = nc.alloc_sbuf_tensor("idb", [128, 128], BF16)
    ps_t = [nc.alloc_psum_tensor(f"ps{i}", [128, 512], F32) for i in range(8)]
    arena = arena_t[:]
    small = small_t[:]

    class Carve:
        def __init__(self, base=0):
            self.o = base

        def bf(self, n):
            a = arena[:, self.o:self.o + n]
            self.o += n
            return a

        def f32(self, n):
            a = arena[:, self.o:self.o + 2 * n].bitcast(F32)
            self.o += 2 * n
            return a

    ca = Carve()
    xT_sb = Buf(ca.bf(KC * NTOK), "xT")
    ws = [Buf(ca.bf(KC * 256), f"ws{i}") for i in range(2)]
    pre = [Buf(ca.f32(1028)) for _ in range(2)]
    acc = [Buf(ca.f32(NTOK)) for _ in range(2)]
    mix_base = ca.o
    xsT = [Buf(ca.bf(NTOK)) for _ in range(4)]
    BT = Buf(ca.bf(NTOK))
    CT = Buf(ca.bf(NTOK))
    szc = [Buf(ca.bf(512)) for _ in range(NCH)]
    mixstage = Buf(ca.bf(4 * NTOK), "mixst")
    rhs2 = Buf(ca.f32(1024))
    Eb = Buf(ca.f32(1024))
    Mb = Buf(ca.bf(1024))
    t1 = Buf(ca.f32(512))
    t2 = Buf(ca.f32(512))
    xD2 = [Buf(ca.f32(512)) for _ in range(2)]
    x_tok = Buf(ca.bf(512))
    xdt = Buf(ca.bf(512))
    xdd = Buf(ca.bf(512))
    ob = Buf(ca.bf(512))
    Mcb = Buf(ca.f32(128))
    B_tok = Buf(ca.bf(128))
    S_bf = Buf(ca.bf(512))
    ssd_end = ca.o
    cl = Carve(mix_base)
    u = [Buf(cl.f32(NTOK)) for _ in range(2)]
    u_bf = [Buf(cl.bf(NTOK)) for _ in range(2)]
    gr2 = [Buf(cl.f32(NTOK)) for _ in range(2)]
    gi2 = [Buf(cl.f32(NTOK)) for _ in range(2)]
    ab2 = [Buf(cl.f32(NTOK)) for _ in range(2)]
    hb2 = [Buf(cl.f32(NTOK)) for _ in range(2)]
    slg2 = [Buf(cl.f32(NTOK)) for _ in range(2)]
    mo = Buf(cl.bf(NTOK), "mo")
    lru_end = cl.o
    ca.o = max(ssd_end, lru_end)
    dtf = {k: Buf(ca.f32(NCH * 64)) for k in ("dt", "adt", "w", "expa", "dA")}
    tmp64 = [Buf(ca.f32(64)) for _ in range(4)]
    acs_sb = Buf(ca.f32(64))
    lws = [Buf(ca.bf(1024), f"lw{i}") for i in range(2)]
    S = [Buf(ca.f32(512)) for _ in range(G)]
    a_end = ca.o
    assert a_end <= ARENA, a_end
    cb_ = Carve()
    mixT_sb = Buf(cb_.bf(64 * NTOK), "mixld")
    wo = [Buf(cb_.bf(64 * 256), f"wo{i}") for i in range(2)]
    xr = [Buf(cb_.f32(256), f"xr{i}") for i in range(2)]
    stg = [Buf(cb_.f32(256), f"stg{i}") for i in range(2)]
    assert cb_.o <= ARENA, cb_.o
    cc = Carve()
    lnr = [Buf(cc.f32(D), f"lnr{i}") for i in range(2)]
    lno = [Buf(cc.f32(D), f"lno{i}") for i in range(2)]
    lngb_sb = Buf(cc.f32(2 * D), "lngb")
    assert cc.o <= 64 * NTOK

    params_sb = Buf(small[:, 0:NPAR], "params")
    consts_sb = Buf(small[:, 724:724 + 512], "consts")
    so = 724 + 512
    hist = Buf(small[:, so:so + 240]); so += 240
    hstate = Buf(small[:, so:so + 32]); so += 32
    Aneg = Buf(small[:, so:so + 64]); so += 64
    coef = Buf(small[:, so:so + 32]); so += 32
    coef2 = Buf(small[:, so:so + 32]); so += 32
    ssb = Buf(small[:, so:so + 8]); so += 8
    stats = Buf(small[:, so:so + 64]); so += 64
    assert so <= 1728, so
    idb = Buf(idb_t[:])
    pp = params_sb.ap
    ident_f, Uincl, Ustrict, ones_f = (consts_sb.ap[:, i * 128:(i + 1) * 128] for i in range(4))

    psA = [Buf(ps_t[0][:]), Buf(ps_t[1][:])]
    psY1 = Buf(ps_t[4][:])
    psY2 = Buf(ps_t[5][:])
    psSn = Buf(ps_t[6][:])
    psD = [Buf(ps_t[2][:]), psA[1]]
    psTPx = Buf(ps_t[3][:])
    psTPo = psA[0]
    psBt = Buf(ps_t[7][:, 0:128])
    psCB = Buf(ps_t[7][:, 128:256])
    ps64 = [Buf(ps_t[7][:, 256:320])]
    ps_acs = psSn
    ps_tot = psY2
    psO = [Buf(ps_t[i][:]) for i in (2, 3, 4, 5)]

    pr = Prog()
    pr.limit = limit
    hd = lambda a, g8: a.rearrange("p (h q) -> p h q", h=8) if g8 else a

    pr.dma("sp", lambda e: e.dma_start(out=params_sb.ap, in_=params[:, :]), "params", writes=[params_sb])
    pr.dma("sp", lambda e: e.dma_start(out=consts_sb.ap, in_=consts[:, :]), "consts", writes=[consts_sb])
    pr.op("dve", lambda e: e.tensor_copy(out=idb.ap, in_=ident_f), reads=[consts_sb], writes=[idb])
    for b_ in pre:
        pr.op("dve", lambda e, b_=b_: e.memset(b_.ap, 0.0), writes=[b_])
    pr.op("dve", lambda e: e.memset(hstate.ap, 0.0), writes=[hstate])
    pr.op("dve", lambda e: e.memset(hist.ap, 0.0), writes=[hist])
    for g in range(G):
        pr.op("dve", lambda e, g=g: e.memset(S[g].ap, 0.0), writes=[S[g]])
    pr.op("act", lambda e: e.activation(out=Aneg.ap, in_=pp[:, P_ALOG:P_ALOG + 64], func=AF.Exp), reads=[params_sb], writes=[Aneg])
    pr.op("dve", lambda e: e.tensor_scalar_mul(out=Aneg.ap, in0=Aneg.ap, scalar1=-1.0), reads=[Aneg], writes=[Aneg])
    pr.op("act", lambda e: e.activation(out=coef.ap, in_=pp[:, P_LAM:P_LAM + 32], func=AF.Exp, scale=-1.0), reads=[params_sb], writes=[coef])
    pr.op("act", lambda e: e.activation(out=coef.ap, in_=coef.ap, func=AF.Ln, bias=1.0, scale=1.0), reads=[coef], writes=[coef])
    pr.op("dve", lambda e: e.tensor_scalar_mul(out=coef2.ap, in0=coef.ap, scalar1=-16.0), reads=[coef], writes=[coef2])
    pr.op("dve", lambda e: e.tensor_scalar_mul(out=coef.ap, in0=coef.ap, scalar1=-8.0), reads=[coef, coef2], writes=[coef])

    pr.marks.append(("setup_end", pr.seq))
    xT3 = xT_sb.ap.rearrange("p (k t) -> p k t", k=KC)
    state = {"slot": 0, "psa": 0, "pa": 0}

    bq = {"order": [], "next": 0, "loaded": {}}

    def prefetch(n):
        for _ in range(n):
            if bq["next"] >= len(bq["order"]):
                return
            name = bq["order"][bq["next"]]
            bq["next"] += 1
            off, W = BLK_OFF[name]
            sl = ws[state["slot"]]
            state["slot"] ^= 1
            pr.dma("pool", lambda e, sl=sl, off=off, W=W: e.dma_start(out=sl.ap[:, 0:KC * W], in_=win[:, off:off + KC * W]),
                   sl.sem, writes=[sl])
            bq["loaded"][name] = (sl, W)

    def load_block(name):
        while name not in bq["loaded"]:
            prefetch(1)
        return bq["loaded"].pop(name)

    def inproj_cm(sl, W, j, tt):
        ps = psA[state["psa"]]
        state["psa"] ^= 1
        w3 = sl.ap[:, 0:KC * W].rearrange("p (k c) -> p k c", k=KC)

        def fn(e, ps=ps, w3=w3, j=j, tt=tt):
            ins = None
            for kc in range(KC):
                ins = e.matmul(ps.ap[:, 0:512], lhsT=w3[:, kc, j * 128:(j + 1) * 128], rhs=xT3[:, kc, tt * 512:(tt + 1) * 512],
                               start=(kc == 0), stop=(kc == KC - 1))
            return ins
        pr.op("pe", fn, reads=[sl, xT_sb], writes=[ps])
        return ps

    def conv_tile(ph, sl, W, j, tile_id, accb):
        pb = pre[state["pa"]]
        state["pa"] ^= 1
        if ph == 1:
            pr.op("dve", lambda e: e.tensor_copy(out=pb.ap[:, 0:3], in_=hist.ap[:, tile_id * 3:tile_id * 3 + 3]), reads=[hist], writes=[pb])
        for tt in range(2):
            ps = inproj_cm(sl, W, j, tt)
            pr.op("act", lambda e, ps=ps, tt=tt: e.activation(out=pb.ap[:, 3 + tt * 512:3 + (tt + 1) * 512], in_=ps.ap[:, 0:512], func=AF.Copy),
                  reads=[ps], writes=[pb])
        cw = pp[:, P_CONV + tile_id * 5:P_CONV + tile_id * 5 + 5]
        pr.op("dve", lambda e: e.tensor_scalar(out=accb.ap, in0=pb.ap[:, 3:1027], scalar1=cw[:, 3:4], scalar2=cw[:, 4:5], op0=ALU.mult, op1=ALU.add),
              reads=[pb, params_sb], writes=[accb])
        for k in (2, 1, 0):
            pr.op("dve", lambda e, k=k: e.scalar_tensor_tensor(out=accb.ap, in0=pb.ap[:, k:k + 1024], scalar=cw[:, k:k + 1], in1=accb.ap, op0=ALU.mult, op1=ALU.add),
                  reads=[pb, accb], writes=[accb])
        if ph == 0:
            pr.op("dve", lambda e: e.tensor_copy(out=hist.ap[:, tile_id * 3:tile_id * 3 + 3], in_=pb.ap[:, 1024:1027]), reads=[pb], writes=[hist])

    def ssd_conv_tile(ph, sl, W, j, tile_id, dst):
        accb = acc[state["pa"]]
        conv_tile(ph, sl, W, j, tile_id, accb)
        pr.op("act", lambda e: e.activation(out=dst.ap, in_=accb.ap, func=AF.Silu), reads=[accb], writes=[dst])

    for ph in range(2):
        main = ph == 1
        order = ["dt"]
        for g_ in range(G):
            order += [f"B{g_}", f"C{g_}", f"xs{g_}_0", f"xs{g_}_1"] + ([f"z{g_}_0", f"z{g_}_1"] if main else [])
        for k_ in range(16):
            order += [f"lx{k_}"] + ([f"lg{k_}"] if main else [])
        bq["order"], bq["next"], bq["loaded"] = order, 0, {}
        pr.dma("pool", lambda e, ph=ph: e.dma_start(out=xT_sb.ap, in_=xT[ph]), "xT", writes=[xT_sb])
        sl, W = load_block("dt")
        w3 = sl.ap[:, 0:KC * 64].rearrange("p (k c) -> p k c", k=KC)
        dtv = {k: v.ap.rearrange("p (c h) -> p c h", c=NCH) for k, v in dtf.items()}
        for c in range(NCH):
            dps = ps64[0]

            def fn(e, c=c, w3=w3, dps=dps):
                ins = None
                for kc in range(KC):
                    ins = e.matmul(dps.ap, lhsT=xT3[:, kc, c * 128:(c + 1) * 128], rhs=w3[:, kc, :], start=(kc == 0), stop=(kc == KC - 1))
                return ins
            pr.op("pe", fn, reads=[sl, xT_sb], writes=[dps])
            tA, tB, tC, tD = tmp64
            pr.op("dve", lambda e: e.tensor_tensor(out=tA.ap, in0=dps.ap, in1=pp[:, P_DTB:P_DTB + 64], op=ALU.add), reads=[dps, params_sb], writes=[tA])
            pr.op("dve", lambda e: e.tensor_scalar_mul(out=tB.ap, in0=tA.ap, scalar1=-1.0), reads=[tA], writes=[tB])
            pr.op("dve", lambda e: e.tensor_tensor(out=tB.ap, in0=tB.ap, in1=tA.ap, op=ALU.max), reads=[tA, tB], writes=[tB])
            pr.op("act", lambda e: e.activation(out=tB.ap, in_=tB.ap, func=AF.Exp, scale=-1.0), reads=[tB], writes=[tB])
            pr.op("act", lambda e: e.activation(out=tB.ap, in_=tB.ap, func=AF.Ln, bias=1.0, scale=1.0), reads=[tB], writes=[tB])
            pr.op("dve", lambda e, c=c: e.scalar_tensor_tensor(out=dtv["dt"][:, c, :], in0=tA.ap, scalar=0.0, in1=tB.ap, op0=ALU.max, op1=ALU.add),
                  reads=[tA, tB], writes=[dtf["dt"]])
            pr.op("dve", lambda e, c=c: e.tensor_tensor(out=dtv["adt"][:, c, :], in0=dtv["dt"][:, c, :], in1=Aneg.ap, op=ALU.mult),
                  reads=[dtf["dt"], Aneg], writes=[dtf["adt"]])
            pr.op("pe", lambda e, c=c: e.matmul(ps_acs.ap[:, 0:64], lhsT=Uincl, rhs=dtv["adt"][:, c, :], start=True, stop=True),
                  reads=[dtf["adt"], consts_sb], writes=[ps_acs])
            pr.op("pe", lambda e, c=c: e.matmul(ps_tot.ap[:, 0:64], lhsT=ones_f, rhs=dtv["adt"][:, c, :], start=True, stop=True),
                  reads=[dtf["adt"], consts_sb], writes=[ps_tot])
            pr.op("act", lambda e, c=c: e.activation(out=dtv["expa"][:, c, :], in_=ps_acs.ap[:, 0:64], func=AF.Exp), reads=[ps_acs], writes=[dtf["expa"]])
            pr.op("act", lambda e, c=c: e.activation(out=dtv["dA"][:, c, :], in_=ps_tot.ap[:, 0:64], func=AF.Exp), reads=[ps_tot], writes=[dtf["dA"]])
            pr.op("act", lambda e: e.activation(out=acs_sb.ap, in_=ps_acs.ap[:, 0:64], func=AF.Copy), reads=[ps_acs], writes=[acs_sb])
            pr.op("dve", lambda e: e.tensor_tensor(out=tC.ap, in0=ps_tot.ap[:, 0:64], in1=acs_sb.ap, op=ALU.subtract), reads=[ps_tot, acs_sb], writes=[tC])
            pr.op("act", lambda e: e.activation(out=tC.ap, in_=tC.ap, func=AF.Exp), reads=[tC], writes=[tC])
            pr.op("dve", lambda e, c=c: e.tensor_tensor(out=dtv["w"][:, c, :], in0=dtv["dt"][:, c, :], in1=tC.ap, op=ALU.mult),
                  reads=[dtf["dt"], tC], writes=[dtf["w"]])

        pr.marks.append((f"dtprep_end_ph{ph}", pr.seq))
        for g in range(G):
            sl, W = load_block(f"B{g}")
            ssd_conv_tile(ph, sl, W, 0, 32 + g, BT)
            sl, W = load_block(f"C{g}")
            if main:
                ssd_conv_tile(ph, sl, W, 0, 40 + g, CT)
            else:
                ps = psA[state["psa"]]
                state["psa"] ^= 1
                w3c = sl.ap[:, 0:KC * W].rearrange("p (k c) -> p k c", k=KC)

                def fn(e, ps=ps, w3c=w3c):
                    ins = None
                    for kc in range(KC):
                        ins = e.matmul(ps.ap[:, 0:3], lhsT=w3c[:, kc, 0:128], rhs=xT3[:, kc, NTOK - 3:NTOK], start=(kc == 0), stop=(kc == KC - 1))
                    return ins
                pr.op("pe", fn, reads=[sl, xT_sb], writes=[ps])
                pr.op("act", lambda e, ps=ps, g=g: e.activation(out=hist.ap[:, (40 + g) * 3:(40 + g) * 3 + 3], in_=ps.ap[:, 0:3], func=AF.Copy),
                      reads=[ps], writes=[hist])
            for hb_ in range(2):
                sl, W = load_block(f"xs{g}_{hb_}")
                for j in range(2):
                    ssd_conv_tile(ph, sl, W, j, g * 4 + hb_ * 2 + j, xsT[hb_ * 2 + j])
            zsl = []
            if main:
                for zb in range(2):
                    sl, W = load_block(f"z{g}_{zb}")
                    zsl.append(sl)

            def zstep(c, zsl=zsl):
                for zb in range(2):
                    sl = zsl[zb]
                    w3 = sl.ap[:, 0:KC * 256].rearrange("p (k c) -> p k c", k=KC)
                    ps = psA[state["psa"]]
                    state["psa"] ^= 1

                    def fn(e, w3=w3, ps=ps):
                        ins = None
                        for kc in range(KC):
                            ins = e.matmul(ps.ap[:, 0:256], lhsT=xT3[:, kc, c * 128:(c + 1) * 128], rhs=w3[:, kc, :], start=(kc == 0), stop=(kc == KC - 1))
                        return ins
                    pr.op("pe", fn, reads=[sl, xT_sb], writes=[ps])
                    pr.op("act", lambda e, zb=zb, ps=ps: e.activation(out=szc[c].ap[:, zb * 256:(zb + 1) * 256], in_=ps.ap[:, 0:256], func=AF.Silu),
                          reads=[ps], writes=[szc[c]])
            pr.marks.append((f"inproj_end_ph{ph}_g{g}", pr.seq))
            hs = slice(g * 8, g * 8 + 8)
            ms3 = mixstage.ap.rearrange("p (j t) -> p j t", j=4)

            def bcast(k, c, hs=hs):
                return dtv[k][:, c, hs].unsqueeze(2).to_broadcast([128, 8, 64])

            def head(c, g=g, hs=hs, bcast=bcast):
                cs = slice(c * 128, (c + 1) * 128)
                xDc = xD2[c % 2]
                if main:
                    pr.op("pool", lambda e: e.tensor_tensor(
                        out=hd(rhs2.ap, 1), in0=Uincl.unsqueeze(1).to_broadcast([128, 8, 128]),
                        in1=dtv["adt"][:, c, hs].unsqueeze(2).to_broadcast([128, 8, 128]), op=ALU.mult),
                        reads=[consts_sb, dtf["adt"]], writes=[rhs2])
                    for hh in range(2):
                        pr.op("pe", lambda e, hh=hh: e.matmul(psD[hh].ap, lhsT=Ustrict, rhs=rhs2.ap[:, hh * 512:(hh + 1) * 512], start=True, stop=True),
                              reads=[rhs2, consts_sb], writes=[psD[hh]])
                    pr.op("pe", lambda e: e.matmul(psCB.ap, lhsT=BT.ap[:, cs], rhs=CT.ap[:, cs], start=True, stop=True),
                          reads=[BT, CT], writes=[psCB])

                def fn(e):
                    ins = None
                    for j in range(4):
                        ins = e.matmul(psTPx.ap[:, j * 128:(j + 1) * 128], lhsT=xsT[j].ap[:, cs], rhs=idb.ap, start=True, stop=True)
                    return ins
                pr.op("pe", fn, reads=xsT + [idb], writes=[psTPx])
                pr.op("pe", lambda e: e.matmul(psBt.ap, lhsT=BT.ap[:, cs], rhs=idb.ap, start=True, stop=True), reads=[BT, idb], writes=[psBt])
                pr.op("act", lambda e: e.activation(out=x_tok.ap, in_=psTPx.ap, func=AF.Copy), reads=[psTPx], writes=[x_tok])
                pr.op("act", lambda e: e.activation(out=B_tok.ap, in_=psBt.ap, func=AF.Copy), reads=[psBt], writes=[B_tok])
                if main:
                    for hh in range(2):
                        pr.op("act", lambda e, hh=hh: e.activation(out=Eb.ap[:, hh * 512:(hh + 1) * 512], in_=psD[hh].ap, func=AF.Exp),
                              reads=[psD[hh]], writes=[Eb])
                    pr.op("dve", lambda e: e.tensor_tensor(out=hd(xdt.ap, 1), in0=hd(x_tok.ap, 1), in1=bcast("dt", c), op=ALU.mult),
                          reads=[x_tok, dtf["dt"]], writes=[xdt])
                pr.op("dve", lambda e: e.tensor_tensor(out=hd(xdd.ap, 1), in0=hd(x_tok.ap, 1), in1=bcast("w", c), op=ALU.mult),
                      reads=[x_tok, dtf["w"]], writes=[xdd])
                if main:
                    pr.op("pool", lambda e: e.tensor_tensor(out=hd(xDc.ap, 1), in0=hd(x_tok.ap, 1),
                                                            in1=pp[:, P_DH + hs.start:P_DH + hs.stop].unsqueeze(2).to_broadcast([128, 8, 64]), op=ALU.mult),
                          reads=[x_tok, params_sb], writes=[xDc])
                    pr.op("dve", lambda e: e.tensor_tensor(out=Mcb.ap, in0=psCB.ap, in1=Uincl, op=ALU.mult), reads=[psCB, consts_sb], writes=[Mcb])
                    pr.op("dve", lambda e: e.tensor_tensor(out=hd(Mb.ap, 1), in0=hd(Eb.ap, 1), in1=Mcb.ap.unsqueeze(1).to_broadcast([128, 8, 128]), op=ALU.mult),
                          reads=[Eb, Mcb], writes=[Mb])

            def mid(c, g=g, hs=hs, bcast=bcast):
                cs = slice(c * 128, (c + 1) * 128)
                pr.op("pe", lambda e: e.matmul(psSn.ap, lhsT=B_tok.ap, rhs=xdd.ap, start=True, stop=True), reads=[B_tok, xdd], writes=[psSn])
                if main:
                    def fn(e):
                        ins = None
                        for h in range(8):
                            ins = e.matmul(psY1.ap[:, h * 64:(h + 1) * 64], lhsT=Mb.ap[:, h * 128:(h + 1) * 128], rhs=xdt.ap[:, h * 64:(h + 1) * 64],
                                           start=True, stop=True)
                        return ins
                    pr.op("pe", fn, reads=[Mb, xdt], writes=[psY1])
                    pr.op("pe", lambda e: e.matmul(psY2.ap, lhsT=CT.ap[:, cs], rhs=S_bf.ap, start=True, stop=True), reads=[CT, S_bf], writes=[psY2])
                pr.op("pool", lambda e: e.tensor_tensor(out=hd(S[g].ap, 1), in0=hd(S[g].ap, 1), in1=bcast("dA", c), op=ALU.mult),
                      reads=[S[g], dtf["dA"]], writes=[S[g]])
                pr.op("dve", lambda e: e.tensor_tensor(out=S[g].ap, in0=psSn.ap, in1=S[g].ap, op=ALU.add), reads=[psSn, S[g]], writes=[S[g]])
                if main and c < NCH - 1:
                    pr.op("act", lambda e: e.activation(out=S_bf.ap, in_=S[g].ap, func=AF.Copy), reads=[S[g]], writes=[S_bf])

            def tail(c, g=g, hs=hs, bcast=bcast):
                cs = slice(c * 128, (c + 1) * 128)
                xDc = xD2[c % 2]
                pr.op("dve", lambda e: e.tensor_tensor(out=hd(t1.ap, 1), in0=hd(psY2.ap, 1), in1=bcast("expa", c), op=ALU.mult),
                      reads=[psY2, dtf["expa"]], writes=[t1])
                pr.op("dve", lambda e: e.tensor_tensor(out=t2.ap, in0=psY1.ap, in1=t1.ap, op=ALU.add), reads=[psY1, t1], writes=[t2])
                pr.op("dve", lambda e: e.tensor_tensor(out=t2.ap, in0=t2.ap, in1=xDc.ap, op=ALU.add), reads=[t2, xDc], writes=[t2])
                pr.op("dve", lambda e: e.tensor_tensor(out=t2.ap, in0=t2.ap, in1=szc[c].ap, op=ALU.mult), reads=[t2, szc[c]], writes=[t2])
                pr.op("dve", lambda e: e.memset(ssb.ap[:, 0:1], 0.0), writes=[ssb])
                pr.op("act", lambda e: e.activation(out=t1.ap, in_=t2.ap, func=AF.Square, accum_out=ssb.ap[:, 0:1]), reads=[t2], writes=[t1, ssb])
                pr.op("act", lambda e: e.activation(out=ssb.ap[:, 1:2], in_=ssb.ap[:, 0:1], func=AF.Ln, bias=EPS, scale=1.0 / 512.0), reads=[ssb], writes=[ssb])
                pr.op("act", lambda e: e.activation(out=ssb.ap[:, 2:3], in_=ssb.ap[:, 1:2], func=AF.Exp, scale=-0.5), reads=[ssb], writes=[ssb])
                pr.op("dve", lambda e: e.tensor_scalar_mul(out=ob.ap, in0=t2.ap, scalar1=ssb.ap[:, 2:3]), reads=[t2, ssb], writes=[ob])

                def fn(e):
                    ins = None
                    for j in range(4):
                        ins = e.matmul(psTPo.ap[:, j * 128:(j + 1) * 128], lhsT=ob.ap[:, j * 128:(j + 1) * 128], rhs=idb.ap, start=True, stop=True)
                    return ins
                pr.op("pe", fn, reads=[ob, idb], writes=[psTPo])
                for j in range(4):
                    nw = pp[:, P_NORMW + g * 4 + j:P_NORMW + g * 4 + j + 1]
                    pr.op("act", lambda e, j=j, nw=nw: e.activation(out=ms3[:, j, cs], in_=psTPo.ap[:, j * 128:(j + 1) * 128], func=AF.Identity, scale=nw),
                          reads=[psTPo, params_sb], writes=[mixstage])

            head(0)
            if not main:
                prefetch(2)
            for c in range(NCH):
                mid(c)
                if c + 1 < NCH:
                    head(c + 1)
                if main:
                    if c < 4:
                        zstep(2 * c)
                        zstep(2 * c + 1)
                    if c == 3:
                        prefetch(2)
                    tail(c)
            if main:
                pr.dma("sp", lambda e, g=g: e.dma_start(out=mixT[g * 4:(g + 1) * 4].rearrange("j p t -> p j t"),
                                                          in_=mixstage.ap.rearrange("p (j t) -> p j t", j=4)), "mixst", reads=[mixstage])
            else:
                pr.op("dve", lambda e, g=g: e.tensor_scalar_mul(out=S[g].ap, in0=S[g].ap, scalar1=pp[:, P_FLAG:P_FLAG + 1]), reads=[S[g], params_sb], writes=[S[g]])
            if main and g < G - 1:
                pr.op("act", lambda e, g=g: e.activation(out=S_bf.ap, in_=S[g + 1].ap, func=AF.Copy), reads=[S[g + 1]], writes=[S_bf])
        if not main:
            pr.op("act", lambda e: e.activation(out=S_bf.ap, in_=S[0].ap, func=AF.Copy), reads=[S[0]], writes=[S_bf])

        pr.marks.append((f"ssd_end_ph{ph}", pr.seq))
        for k in range(16):
            sl, W = load_block(f"lx{k}")
            lw = lws[k % 2]
            pr.dma("pool", lambda e, k=k, lw=lw: e.dma_start(out=lw.ap, in_=lruw[k]), lw.sem, writes=[lw])
            lw4 = lw.ap.rearrange("p (a i j) -> p a i j", a=2, i=2)
            for i in range(2):
                conv_tile(ph, sl, W, i, 48 + 2 * k + i, u[i])
            if main:
                slg_sl, Wg = load_block(f"lg{k}")
                for j in range(2):
                    for tt in range(2):
                        ps = inproj_cm(slg_sl, Wg, j, tt)
                        pr.op("act", lambda e, ps=ps, tt=tt, j=j: e.activation(out=slg2[j].ap[:, tt * 512:(tt + 1) * 512], in_=ps.ap[:, 0:512], func=AF.Silu),
                              reads=[ps], writes=[slg2[j]])
            for i in range(2):
                pr.op("act", lambda e, i=i: e.activation(out=u_bf[i].ap, in_=u[i].ap, func=AF.Copy), reads=[u[i]], writes=[u_bf[i]])
            for j in range(2):
                tl = 2 * k + j
                gr, gi, ab, hb, slg = gr2[j], gi2[j], ab2[j], hb2[j], slg2[j]
                for tt in range(2):
                    ts_ = slice(tt * 512, (tt + 1) * 512)
                    for a_, (psg, dst, bcol) in enumerate(((psD[0], gr, P_LBA), (psTPx, gi, P_LBX))):
                        def fn(e, a_=a_, psg=psg, ts_=ts_, j=j, lw4=lw4):
                            ins = None
                            for i in range(2):
                                ins = e.matmul(psg.ap, lhsT=lw4[:, a_, i, j * 128:(j + 1) * 128], rhs=u_bf[i].ap[:, ts_], start=(i == 0), stop=(i == 1))
                            return ins
                        pr.op("pe", fn, reads=[lw, u_bf[0], u_bf[1]], writes=[psg])
                        pr.op("act", lambda e, psg=psg, dst=dst, bcol=bcol, ts_=ts_, tl=tl: e.activation(
                            out=dst.ap[:, ts_], in_=psg.ap, func=AF.Sigmoid, bias=pp[:, bcol + tl:bcol + tl + 1], scale=1.0),
                            reads=[psg, params_sb], writes=[dst])
                pr.op("act", lambda e, tl=tl, gr=gr, ab=ab: e.activation(out=ab.ap, in_=gr.ap, func=AF.Exp, scale=coef.ap[:, tl:tl + 1]), reads=[gr, coef], writes=[ab])
                pr.op("act", lambda e, tl=tl, gr=gr: e.activation(out=gr.ap, in_=gr.ap, func=AF.Exp, scale=coef2.ap[:, tl:tl + 1]), reads=[gr, coef2], writes=[gr])
                pr.op("act", lambda e, gr=gr: e.activation(out=gr.ap, in_=gr.ap, func=AF.Sqrt, bias=1.0, scale=-1.0), reads=[gr], writes=[gr])
                pr.op("dve", lambda e, j=j, gi=gi: e.tensor_tensor(out=gi.ap, in0=gi.ap, in1=u[j].ap, op=ALU.mult), reads=[gi, u[j]], writes=[gi])
                pr.op("dve", lambda e, gi=gi, gr=gr: e.tensor_tensor(out=gi.ap, in0=gi.ap, in1=gr.ap, op=ALU.mult), reads=[gi, gr], writes=[gi])
                pr.op("dve", lambda e, tl=tl, hb=hb, ab=ab, gi=gi: e.tensor_tensor_scan(out=hb.ap, data0=ab.ap, data1=gi.ap, initial=hstate.ap[:, tl:tl + 1], op0=ALU.mult, op1=ALU.add),
                      reads=[ab, gi, hstate], writes=[hb])
                if main:
                    pr.op("dve", lambda e, hb=hb, slg=slg: e.tensor_tensor(out=mo.ap, in0=hb.ap, in1=slg.ap, op=ALU.mult), reads=[hb, slg], writes=[mo])
                    pr.dma("sp", lambda e, tl=tl: e.dma_start(out=mixT[32 + tl], in_=mo.ap), "mo", reads=[mo])
                else:
                    pr.op("dve", lambda e, tl=tl, hb=hb: e.tensor_scalar_mul(out=hstate.ap[:, tl:tl + 1], in0=hb.ap[:, 1023:1024], scalar1=pp[:, P_FLAG:P_FLAG + 1]),
                          reads=[hb, params_sb], writes=[hstate])

    pr.marks.append(("phaseA_end", pr.seq))
    bar = [(e, pr.cnt[e]) for e in ("pe", "act", "dve", "pool")] + [("d:" + s, n) for s, n in pr.dcnt.items()]
    m3 = mixT_sb.ap.rearrange("p (k t) -> p k t", k=64)
    mparts = [Buf(m3[:, q * 8:(q + 1) * 8, :], f"mixld{q}") for q in range(8)]
    for q in range(8):
        pr.dma("sp", lambda e, q=q: e.dma_start(out=mparts[q].ap, in_=mixT[q * 8:(q + 1) * 8].rearrange("k p t -> p k t")),
               mparts[q].sem, writes=[mparts[q]], extra=bar if q == 0 else ())
    pso_i = 0
    xi = 0
    for cbk in range(16):
        wsl = wo[cbk % 2]
        pr.dma("pool", lambda e, cbk=cbk, wsl=wsl: e.dma_start(out=wsl.ap, in_=wout[cbk]), wsl.sem, writes=[wsl], extra=bar if cbk < 2 else ())
        wo3 = wsl.ap.rearrange("p (k c) -> p k c", k=64)
        for tk in range(8):
            ps = psO[pso_i % 4]
            pso_i += 1
            xb, sb = xr[xi % 2], stg[xi % 2]
            xi += 1
            pr.dma("sp", lambda e, xb=xb, tk=tk, cbk=cbk: e.dma_start(out=xb.ap, in_=xres[tk * 128:(tk + 1) * 128, cbk * 256:(cbk + 1) * 256]),
                   xb.sem, writes=[xb], extra=bar if xi <= 2 else ())

            def fn(e, ps=ps, tk=tk, wo3=wo3):
                ins = None
                for kc in range(64):
                    ins = e.matmul(ps.ap[:, 0:256], lhsT=m3[:, kc, tk * 128:(tk + 1) * 128], rhs=wo3[:, kc, :], start=(kc == 0), stop=(kc == 63))
                return ins
            pr.op("pe", fn, reads=mparts + [wsl], writes=[ps])
            pr.op("dve", lambda e, ps=ps, xb=xb, sb=sb: e.scalar_tensor_tensor(out=sb.ap, in0=xb.ap, scalar=ALPHA, in1=ps.ap[:, 0:256], op0=ALU.mult, op1=ALU.add),
                  reads=[xb, ps], writes=[sb], extra=bar if xi <= 2 else ())
            pr.dma("sp", lambda e, sb=sb, tk=tk, cbk=cbk: e.dma_start(out=pre_scr[tk * 128:(tk + 1) * 128, cbk * 256:(cbk + 1) * 256], in_=sb.ap),
                   sb.sem, reads=[sb])
    pr.marks.append(("outproj_end", pr.seq))
    bar2 = [(e, pr.cnt[e]) for e in ("pe", "dve")] + [("d:" + s, n) for s, n in pr.dcnt.items()]
    pr.dma("sp", lambda e: e.dma_start(out=lngb_sb.ap, in_=lngb[:, :]), "lngb", writes=[lngb_sb], extra=bar2)
    out_toks = []
    for tk in range(8):
        rb, ob_ = lnr[tk % 2], lno[tk % 2]
        pr.dma("sp", lambda e, rb=rb, tk=tk: e.dma_start(out=rb.ap, in_=pre_scr[tk * 128:(tk + 1) * 128, :]), rb.sem, writes=[rb], extra=bar2 if tk < 2 else ())
        st6 = stats.ap[:, 0:48].rearrange("p (c s) -> p c s", c=8)
        for q in range(8):
            pr.op("dve", lambda e, q=q, rb=rb: e.bn_stats(out=st6[:, q, :], in_=rb.ap[:, q * 512:(q + 1) * 512]), reads=[rb], writes=[stats])
        pr.op("dve", lambda e: e.bn_aggr(out=stats.ap[:, 48:50], in_=st6), reads=[stats], writes=[stats])
        pr.op("act", lambda e: e.activation(out=stats.ap[:, 50:51], in_=stats.ap[:, 49:50], func=AF.Sqrt, bias=EPS, scale=1.0), reads=[stats], writes=[stats])
        pr.op("dve", lambda e: e.reciprocal(out=stats.ap[:, 51:52], in_=stats.ap[:, 50:51]), reads=[stats], writes=[stats])
        pr.op("dve", lambda e, rb=rb, ob_=ob_: e.tensor_scalar(out=ob_.ap, in0=rb.ap, scalar1=stats.ap[:, 48:49], scalar2=stats.ap[:, 51:52], op0=ALU.subtract, op1=ALU.mult),
              reads=[rb, stats], writes=[ob_], extra=bar2 if tk < 2 else ())
        pr.op("dve", lambda e, ob_=ob_: e.tensor_tensor(out=ob_.ap, in0=ob_.ap, in1=lngb_sb.ap[:, 0:D], op=ALU.mult), reads=[ob_, lngb_sb], writes=[ob_])
        pr.op("dve", lambda e, ob_=ob_: e.tensor_tensor(out=ob_.ap, in0=ob_.ap, in1=lngb_sb.ap[:, D:2 * D], op=ALU.add), reads=[ob_, lngb_sb], writes=[ob_])
        out_toks.append(pr.dma("sp", lambda e, ob_=ob_, tk=tk: e.dma_start(out=out[tk * 128:(tk + 1) * 128, :], in_=ob_.ap), ob_.sem, reads=[ob_]))

    pr.marks.append(("end", pr.seq))
    nc._marks = pr.marks
    import contextlib
    with contextlib.ExitStack() as es:
        sems = {e: es.enter_context(nc.semaphore("s_" + e)) for e in ("pe", "act", "dve", "pool")}
        for s in pr.dcnt:
            sems["d:" + s] = es.enter_context(nc.semaphore("d_" + s))
        block = es.enter_context(nc.Block())

        def run(engname):
            def body(eng):
                waited = {}
                for fn, waits, tok in pr.ops[engname]:
                    for (sn, val) in waits:
                        if sn == "pe" and engname == "pe":
                            continue
                        if waited.get(sn, 0) >= val:
                            continue
                        waited[sn] = val
                        eng.wait_ge(sems[sn], val)
                    ins = fn(eng)
                    ins.then_inc(sems[tok[0]], 16 if tok[0].startswith("d:") else 1)
                if engname == "sp":
                    fin = [(e_, pr.cnt[e_]) for e_ in ("pe", "act", "dve", "pool")] + [("d:" + s_, n_) for s_, n_ in pr.dcnt.items()]
                    for (sn, val) in fin:
                        if val > 0:
                            eng.wait_ge(sems[sn], val)
            return body
        block.tensor(run("pe"))
        block.scalar(run("act"))
        block.vector(run("dve"))
        block.gpsimd(run("pool"))
        block.sync(run("sp"))
    return nc


_CACHE = {}


def _host_prep(inp):
    f = np.float32
    w_in = np.asarray(inp["w_in"][0], f)
    w3 = w_in.reshape(KC, 128, -1)
    win = np.empty((128, WIN_COLS), f)
    for name, cols in BLOCKS:
        off, W = BLK_OFF[name]
        blk = w3[:, :, cols[0]:cols[-1] + 1]
        win[:, off:off + KC * W] = blk.transpose(1, 0, 2).reshape(128, KC * W)
    w_out = np.asarray(inp["w_out"][0], f)
    wout = np.ascontiguousarray(w_out.reshape(64, 128, 16, 256).transpose(2, 1, 0, 3).reshape(16, 128, 64 * 256))
    wa = np.asarray(inp["lru_wa"][0], f).reshape(16, 2, 128, 256)
    wx = np.asarray(inp["lru_wx"][0], f).reshape(16, 2, 128, 256)
    lruw = np.ascontiguousarray(np.stack([wa, wx], 1).transpose(0, 3, 1, 2, 4).reshape(16, 128, 1024))
    params = np.zeros((128, NPAR), f)
    params[:, P_DTB:P_DTB + 64] = np.asarray(inp["ssd_dt_bias"][0], f)[None, :]
    params[:, P_ALOG:P_ALOG + 64] = np.asarray(inp["ssd_a_log"][0], f)[None, :]
    params[:, P_DH:P_DH + 64] = np.asarray(inp["ssd_d"][0], f)[None, :]
    cw = np.concatenate([np.asarray(inp["ssd_conv_w"][0], f), np.asarray(inp["lru_conv_w"][0], f)], 1)
    cbias = np.concatenate([np.asarray(inp["ssd_conv_b"][0], f), np.asarray(inp["lru_conv_b"][0], f)], 0)
    cp = np.concatenate([cw, cbias[None, :]], 0)
    params[:, P_CONV:P_CONV + 400] = cp.reshape(5, 80, 128).transpose(2, 1, 0).reshape(128, 400)
    params[:, P_NORMW:P_NORMW + 32] = np.asarray(inp["ssd_norm_w"][0], f).reshape(32, 128).T
    params[:, P_LBA:P_LBA + 32] = np.asarray(inp["lru_ba"][0], f).reshape(32, 128).T
    params[:, P_LBX:P_LBX + 32] = np.asarray(inp["lru_bx"][0], f).reshape(32, 128).T
    params[:, P_LAM:P_LAM + 32] = np.asarray(inp["lru_lambda"][0], f).reshape(32, 128).T
    consts = np.zeros((128, 512), f)
    j = np.arange(128)
    consts[:, 0:128] = np.eye(128, dtype=f)
    consts[:, 128:256] = (j[:, None] <= j[None, :]).astype(f)
    consts[:, 256:384] = (j[:, None] > j[None, :]).astype(f)
    consts[:, 384:512] = 1.0
    lngb = np.concatenate([np.broadcast_to(np.asarray(inp["ln_g"][0], f)[None, :], (128, D)),
                           np.broadcast_to(np.asarray(inp["ln_b"][0], f)[None, :], (128, D))], 1)
    lngb = np.ascontiguousarray(lngb)
    x = np.asarray(inp["x"], f)
    in_maps = []
    for core in range(8):
        b, h = core // 2, core % 2
        xm = x[b, h * NTOK:(h + 1) * NTOK]
        xw = x[b, 0:NTOK] if h == 1 else np.zeros((NTOK, D), f)

        def tr(a):
            return a.T.reshape(KC, 128, NTOK).transpose(1, 0, 2).reshape(128, KC * NTOK)
        xT = np.ascontiguousarray(np.stack([tr(xw), tr(xm)], 0))
        p = params.copy()
        p[:, P_FLAG] = float(h)
        in_maps.append({"xT": xT, "xres": np.ascontiguousarray(xm), "win": win, "wout": wout, "lruw": lruw,
                        "params": p, "consts": consts, "lngb": lngb})
    return in_maps


def kernel(**inputs):
    if "nc" not in _CACHE:
        _CACHE["nc"] = build_nc()
    nc = _CACHE["nc"]
    in_maps = _host_prep(inputs)
    res = run_bass_kernel_spmd(nc, in_maps, core_ids=list(range(8)))
    outp = np.empty((4, 2048, D), np.float32)
    for core in range(8):
        b, h = core // 2, core % 2
        outp[b, h * NTOK:(h + 1) * NTOK] = res.results[core]["out"]
    return outp
```

```python
import numpy as np
import ml_dtypes
import concourse.bass as bass
import concourse.mybir as mybir
from concourse.bass_utils import run_bass_kernel_spmd

F32, BF16 = mybir.dt.float32, mybir.dt.bfloat16
AF = mybir.ActivationFunctionType
ALU = mybir.AluOpType
AX = mybir.AxisListType

D = 4096
NTOK = 1024
NCH = 8
KC = 32
G = 8
NLT = 32
ALPHA = 2.0 ** 0.25
EPS = 1e-5
ENGS = ("pe", "act", "dve", "pool", "sp")

P_DTB, P_ALOG, P_DH, P_CONV, P_NORMW, P_LBA, P_LBX, P_LAM, P_FLAG, NPAR = 0, 64, 128, 192, 592, 624, 656, 688, 720, 721

def _blocks():
    blks = [("dt", list(range(10240, 10304)))]
    for g in range(G):
        blks.append((f"B{g}", list(range(8192 + 128 * g, 8192 + 128 * (g + 1)))))
        blks.append((f"C{g}", list(range(9216 + 128 * g, 9216 + 128 * (g + 1)))))
        for hb in range(2):
            blks.append((f"xs{g}_{hb}", list(range(4096 + 512 * g + 256 * hb, 4096 + 512 * g + 256 * (hb + 1)))))
        for hb in range(2):
            blks.append((f"z{g}_{hb}", list(range(512 * g + 256 * hb, 512 * g + 256 * (hb + 1)))))
    for k in range(16):
        blks.append((f"lx{k}", list(range(10304 + 256 * k, 10304 + 256 * (k + 1)))))
        blks.append((f"lg{k}", list(range(14400 + 256 * k, 14400 + 256 * (k + 1)))))
    return blks

BLOCKS = _blocks()
BLK_OFF = {}
_o = 0
for _n, _c in BLOCKS:
    BLK_OFF[_n] = (_o, len(_c))
    _o += KC * len(_c)
WIN_COLS = _o


class Buf:
    __slots__ = ("ap", "w", "r", "sem")

    def __init__(self, ap, sem=None):
        self.ap, self.w, self.r, self.sem = ap, None, [], sem


class Prog:
    def __init__(self):
        self.ops = {e: [] for e in ENGS}
        self.cnt = {e: 0 for e in ENGS}
        self.dcnt = {}
        self.seq = 0
        self.limit = None
        self.marks = []

    def _hz(self, reads, writes, extra):
        waits = [b.w for b in reads if b.w is not None]
        for b in writes:
            waits += b.r
            if b.w is not None:
                waits.append(b.w)
        waits += [t for t in extra if t is not None]
        return waits

    def op(self, eng, fn, reads=(), writes=(), extra=()):
        self.seq += 1
        if self.limit is not None and self.seq > self.limit:
            return None
        waits = self._hz(reads, writes, extra)
        self.cnt[eng] += 1
        tok = (eng, self.cnt[eng])
        self.ops[eng].append((fn, waits, tok))
        for b in reads:
            b.r.append(tok)
        for b in writes:
            b.w, b.r = tok, []
        return tok

    def dma(self, eng, fn, sem, reads=(), writes=(), extra=()):
        self.seq += 1
        if self.limit is not None and self.seq > self.limit:
            return None
        waits = self._hz(reads, writes, extra)
        self.dcnt[sem] = self.dcnt.get(sem, 0) + 16
        tok = ("d:" + sem, self.dcnt[sem])
        self.ops[eng].append((fn, waits, tok))
        for b in reads:
            b.r.append(tok)
        for b in writes:
            b.w, b.r = tok, []
        return tok


def build_nc(limit=None):
    nc = bass.Bass("TRN2", target_bir_lowering=False)
    dram = {}

    def din(name, shape, dt=F32):
        dram[name] = nc.dram_tensor(name, list(shape), dt, kind="ExternalInput").ap()
        return dram[name]

    xT = din("xT", [2, 128, KC * NTOK])
    xres = din("xres", [NTOK, D])
    win = din("win", [128, WIN_COLS])
    wout = din("wout", [16, 128, 64 * 256])
    lruw = din("lruw", [16, 128, 1024])
    params = din("params", [128, NPAR])
    consts = din("consts", [128, 512])
    lngb = din("lngb", [128, 2 * D])
    out = nc.dram_tensor("out", [NTOK, D], F32, kind="ExternalOutput").ap()
    mixT = nc.dram_tensor("mixT_scr", [64, 128, NTOK], BF16, kind="Internal").ap()
    pre_scr = nc.dram_tensor("pre_scr", [NTOK, D], F32, kind="Internal").ap()

    ARENA = 102784
    arena_t = nc.alloc_sbuf_tensor("arena", [128, ARENA], BF16)
    small_t = nc.alloc_sbuf_tensor("small", [128, 1728], F32)
    idb_t = nc.alloc_sbuf_tensor("idb", [128, 128], BF16)
    ps_t = [nc.alloc_psum_tensor(f"ps{i}", [128, 512], F32) for i in range(8)]
    arena = arena_t[:]
    small = small_t[:]

    class Carve:
        def __init__(self, base=0):
            self.o = base

        def bf(self, n):
            a = arena[:, self.o:self.o + n]
            self.o += n
            return a

        def f32(self, n):
            a = arena[:, self.o:self.o + 2 * n].bitcast(F32)
            self.o += 2 * n
            return a

    ca = Carve()
    xT_sb = Buf(ca.bf(KC * NTOK), "xT")
    ws = [Buf(ca.bf(KC * 256), f"ws{i}") for i in range(2)]
    pre = [Buf(ca.f32(1028)) for _ in range(2)]
    acc = [Buf(ca.f32(NTOK)) for _ in range(2)]
    mix_base = ca.o
    xsT = [Buf(ca.bf(NTOK)) for _ in range(4)]
    BT = Buf(ca.bf(NTOK))
    CT = Buf(ca.bf(NTOK))
    szc = [Buf(ca.bf(512)) for _ in range(NCH)]
    mixstage = Buf(ca.bf(4 * NTOK), "mixst")
    rhs2 = Buf(ca.f32(1024))
    Eb = Buf(ca.f32(1024))
    Mb = Buf(ca.bf(1024))
    t1 = Buf(ca.f32(512))
    t2 = Buf(ca.f32(512))
    xD2 = [Buf(ca.f32(512)) for _ in range(2)]
    x_tok2 = [Buf(ca.bf(512)) for _ in range(2)]
    xdt2 = [Buf(ca.bf(512)) for _ in range(2)]
    xdd2 = [Buf(ca.bf(512)) for _ in range(2)]
    ob = Buf(ca.bf(512))
    Mcb = Buf(ca.f32(128))
    B_tok2 = [Buf(ca.bf(128)) for _ in range(2)]
    S_bf = Buf(ca.bf(512))
    ssd_end = ca.o
    cl = Carve(mix_base)
    u = [Buf(cl.f32(NTOK)) for _ in range(2)]
    u_bf = [Buf(cl.bf(NTOK)) for _ in range(2)]
    gr2 = [Buf(cl.f32(NTOK)) for _ in range(2)]
    gi2 = [Buf(cl.f32(NTOK)) for _ in range(2)]
    ab2 = [Buf(cl.f32(NTOK)) for _ in range(2)]
    hb2 = [Buf(cl.f32(NTOK)) for _ in range(2)]
    slg2 = [Buf(cl.f32(NTOK)) for _ in range(2)]
    mo = Buf(cl.bf(NTOK), "mo")
    lru_end = cl.o
    ca.o = max(ssd_end, lru_end)
    dtf = {k: Buf(ca.f32(NCH * 64)) for k in ("dt", "adt", "w", "expa", "dA")}
    tmp64 = [Buf(ca.f32(64)) for _ in range(4)]
    acs_sb = Buf(ca.f32(64))
    lws = [Buf(ca.bf(1024), f"lw{i}") for i in range(2)]
    S = [Buf(ca.f32(512)) for _ in range(G)]
    a_end = ca.o
    assert a_end <= ARENA, a_end
    cb_ = Carve()
    mixT_sb = Buf(cb_.bf(64 * NTOK), "mixld")
    wo = [Buf(cb_.bf(64 * 256), f"wo{i}") for i in range(2)]
    xr = [Buf(cb_.f32(256), f"xr{i}") for i in range(2)]
    stg = [Buf(cb_.f32(256), f"stg{i}") for i in range(2)]
    assert cb_.o <= ARENA, cb_.o
    cc = Carve()
    lnr = [Buf(cc.f32(D), f"lnr{i}") for i in range(2)]
    lno = [Buf(cc.f32(D), f"lno{i}") for i in range(2)]
    lngb_sb = Buf(cc.f32(2 * D), "lngb")
    assert cc.o <= 64 * NTOK

    params_sb = Buf(small[:, 0:NPAR], "params")
    consts_sb = Buf(small[:, 724:724 + 512], "consts")
    so = 724 + 512
    hist = Buf(small[:, so:so + 240]); so += 240
    hstate = Buf(small[:, so:so + 32]); so += 32
    Aneg = Buf(small[:, so:so + 64]); so += 64
    coef = Buf(small[:, so:so + 32]); so += 32
    coef2 = Buf(small[:, so:so + 32]); so += 32
    ssb = Buf(small[:, so:so + 8]); so += 8
    stats = Buf(small[:, so:so + 64]); so += 64
    assert so <= 1728, so
    idb = Buf(idb_t[:])
    pp = params_sb.ap
    ident_f, Uincl, Ustrict, ones_f = (consts_sb.ap[:, i * 128:(i + 1) * 128] for i in range(4))

    psA = [Buf(ps_t[0][:]), Buf(ps_t[1][:])]
    psY1 = Buf(ps_t[4][:])
    psY2 = Buf(ps_t[5][:])
    psSn = Buf(ps_t[6][:])
    psD = [Buf(ps_t[2][:]), psA[1]]
    psTPx = Buf(ps_t[3][:])
    psTPo = psA[0]
    psBt = Buf(ps_t[7][:, 0:128])
    psCB = Buf(ps_t[7][:, 128:256])
    ps64 = [Buf(ps_t[7][:, 256:320])]
    ps_acs = psSn
    ps_tot = psY2
    psO = [Buf(ps_t[i][:]) for i in (2, 3, 4, 5)]

    pr = Prog()
    pr.limit = limit
    hd = lambda a, g8: a.rearrange("p (h q) -> p h q", h=8) if g8 else a

    pr.dma("sp", lambda e: e.dma_start(out=params_sb.ap, in_=params[:, :]), "params", writes=[params_sb])
    pr.dma("sp", lambda e: e.dma_start(out=consts_sb.ap, in_=consts[:, :]), "consts", writes=[consts_sb])
    pr.op("dve", lambda e: e.tensor_copy(out=idb.ap, in_=ident_f), reads=[consts_sb], writes=[idb])
    for b_ in pre:
        pr.op("dve", lambda e, b_=b_: e.memset(b_.ap, 0.0), writes=[b_])
    pr.op("dve", lambda e: e.memset(hstate.ap, 0.0), writes=[hstate])
    pr.op("dve", lambda e: e.memset(hist.ap, 0.0), writes=[hist])
    for g in range(G):
        pr.op("dve", lambda e, g=g: e.memset(S[g].ap, 0.0), writes=[S[g]])
    pr.op("act", lambda e: e.activation(out=Aneg.ap, in_=pp[:, P_ALOG:P_ALOG + 64], func=AF.Exp), reads=[params_sb], writes=[Aneg])
    pr.op("dve", lambda e: e.tensor_scalar_mul(out=Aneg.ap, in0=Aneg.ap, scalar1=-1.0), reads=[Aneg], writes=[Aneg])
    pr.op("act", lambda e: e.activation(out=coef.ap, in_=pp[:, P_LAM:P_LAM + 32], func=AF.Exp, scale=-1.0), reads=[params_sb], writes=[coef])
    pr.op("act", lambda e: e.activation(out=coef.ap, in_=coef.ap, func=AF.Ln, bias=1.0, scale=1.0), reads=[coef], writes=[coef])
    pr.op("dve", lambda e: e.tensor_scalar_mul(out=coef2.ap, in0=coef.ap, scalar1=-16.0), reads=[coef], writes=[coef2])
    pr.op("dve", lambda e: e.tensor_scalar_mul(out=coef.ap, in0=coef.ap, scalar1=-8.0), reads=[coef, coef2], writes=[coef])

    pr.marks.append(("setup_end", pr.seq))
    xT3 = xT_sb.ap.rearrange("p (k t) -> p k t", k=KC)
    state = {"slot": 0, "psa": 0, "pa": 0}

    bq = {"order": [], "next": 0, "loaded": {}}

    def prefetch(n):
        for _ in range(n):
            if bq["next"] >= len(bq["order"]):
                return
            name = bq["order"][bq["next"]]
            bq["next"] += 1
            off, W = BLK_OFF[name]
            sl = ws[state["slot"]]
            state["slot"] ^= 1
            pr.dma("pool", lambda e, sl=sl, off=off, W=W: e.dma_start(out=sl.ap[:, 0:KC * W], in_=win[:, off:off + KC * W]),
                   sl.sem, writes=[sl])
            bq["loaded"][name] = (sl, W)

    def load_block(name):
        while name not in bq["loaded"]:
            prefetch(1)
        return bq["loaded"].pop(name)

    def inproj_cm(sl, W, j, tt):
        ps = psA[state["psa"]]
        state["psa"] ^= 1
        w3 = sl.ap[:, 0:KC * W].rearrange("p (k c) -> p k c", k=KC)

        def fn(e, ps=ps, w3=w3, j=j, tt=tt):
            ins = None
            for kc in range(KC):
                ins = e.matmul(ps.ap[:, 0:512], lhsT=w3[:, kc, j * 128:(j + 1) * 128], rhs=xT3[:, kc, tt * 512:(tt + 1) * 512],
                               start=(kc == 0), stop=(kc == KC - 1))
            return ins
        pr.op("pe", fn, reads=[sl, xT_sb], writes=[ps])
        return ps

    def conv_tile(ph, sl, W, j, tile_id, accb):
        pb = pre[state["pa"]]
        state["pa"] ^= 1
        if ph == 1:
            pr.op("dve", lambda e: e.tensor_copy(out=pb.ap[:, 0:3], in_=hist.ap[:, tile_id * 3:tile_id * 3 + 3]), reads=[hist], writes=[pb])
        for tt in range(2):
            ps = inproj_cm(sl, W, j, tt)
            pr.op("act", lambda e, ps=ps, tt=tt: e.activation(out=pb.ap[:, 3 + tt * 512:3 + (tt + 1) * 512], in_=ps.ap[:, 0:512], func=AF.Copy),
                  reads=[ps], writes=[pb])
        cw = pp[:, P_CONV + tile_id * 5:P_CONV + tile_id * 5 + 5]
        pr.op("dve", lambda e: e.tensor_scalar(out=accb.ap, in0=pb.ap[:, 3:1027], scalar1=cw[:, 3:4], scalar2=cw[:, 4:5], op0=ALU.mult, op1=ALU.add),
              reads=[pb, params_sb], writes=[accb])
        for k in (2, 1, 0):
            pr.op("dve", lambda e, k=k: e.scalar_tensor_tensor(out=accb.ap, in0=pb.ap[:, k:k + 1024], scalar=cw[:, k:k + 1], in1=accb.ap, op0=ALU.mult, op1=ALU.add),
                  reads=[pb, accb], writes=[accb])
        if ph == 0:
            pr.op("dve", lambda e: e.tensor_copy(out=hist.ap[:, tile_id * 3:tile_id * 3 + 3], in_=pb.ap[:, 1024:1027]), reads=[pb], writes=[hist])

    def ssd_conv_tile(ph, sl, W, j, tile_id, dst):
        accb = acc[state["pa"]]
        conv_tile(ph, sl, W, j, tile_id, accb)
        pr.op("act", lambda e: e.activation(out=dst.ap, in_=accb.ap, func=AF.Silu), reads=[accb], writes=[dst])

    for ph in range(2):
        main = ph == 1
        order = ["dt"]
        for g_ in range(G):
            order += [f"B{g_}", f"C{g_}", f"xs{g_}_0", f"xs{g_}_1"] + ([f"z{g_}_0", f"z{g_}_1"] if main else [])
        for k_ in range(16):
            order += [f"lx{k_}"] + ([f"lg{k_}"] if main else [])
        bq["order"], bq["next"], bq["loaded"] = order, 0, {}
        pr.dma("pool", lambda e, ph=ph: e.dma_start(out=xT_sb.ap, in_=xT[ph]), "xT", writes=[xT_sb])
        sl, W = load_block("dt")
        w3 = sl.ap[:, 0:KC * 64].rearrange("p (k c) -> p k c", k=KC)
        dtv = {k: v.ap.rearrange("p (c h) -> p c h", c=NCH) for k, v in dtf.items()}
        for c in range(NCH):
            dps = ps64[0]

            def fn(e, c=c, w3=w3, dps=dps):
                ins = None
                for kc in range(KC):
                    ins = e.matmul(dps.ap, lhsT=xT3[:, kc, c * 128:(c + 1) * 128], rhs=w3[:, kc, :], start=(kc == 0), stop=(kc == KC - 1))
                return ins
            pr.op("pe", fn, reads=[sl, xT_sb], writes=[dps])
            tA, tB, tC, tD = tmp64
            pr.op("dve", lambda e: e.tensor_tensor(out=tA.ap, in0=dps.ap, in1=pp[:, P_DTB:P_DTB + 64], op=ALU.add), reads=[dps, params_sb], writes=[tA])
            pr.op("dve", lambda e: e.tensor_scalar_mul(out=tB.ap, in0=tA.ap, scalar1=-1.0), reads=[tA], writes=[tB])
            pr.op("dve", lambda e: e.tensor_tensor(out=tB.ap, in0=tB.ap, in1=tA.ap, op=ALU.max), reads=[tA, tB], writes=[tB])
            pr.op("act", lambda e: e.activation(out=tB.ap, in_=tB.ap, func=AF.Exp, scale=-1.0), reads=[tB], writes=[tB])
            pr.op("act", lambda e: e.activation(out=tB.ap, in_=tB.ap, func=AF.Ln, bias=1.0, scale=1.0), reads=[tB], writes=[tB])
            pr.op("dve", lambda e, c=c: e.scalar_tensor_tensor(out=dtv["dt"][:, c, :], in0=tA.ap, scalar=0.0, in1=tB.ap, op0=ALU.max, op1=ALU.add),
                  reads=[tA, tB], writes=[dtf["dt"]])
            pr.op("dve", lambda e, c=c: e.tensor_tensor(out=dtv["adt"][:, c, :], in0=dtv["dt"][:, c, :], in1=Aneg.ap, op=ALU.mult),
                  reads=[dtf["dt"], Aneg], writes=[dtf["adt"]])
            pr.op("pe", lambda e, c=c: e.matmul(ps_acs.ap[:, 0:64], lhsT=Uincl, rhs=dtv["adt"][:, c, :], start=True, stop=True),
                  reads=[dtf["adt"], consts_sb], writes=[ps_acs])
            pr.op("pe", lambda e, c=c: e.matmul(ps_tot.ap[:, 0:64], lhsT=ones_f, rhs=dtv["adt"][:, c, :], start=True, stop=True),
                  reads=[dtf["adt"], consts_sb], writes=[ps_tot])
            pr.op("act", lambda e, c=c: e.activation(out=dtv["expa"][:, c, :], in_=ps_acs.ap[:, 0:64], func=AF.Exp), reads=[ps_acs], writes=[dtf["expa"]])
            pr.op("act", lambda e, c=c: e.activation(out=dtv["dA"][:, c, :], in_=ps_tot.ap[:, 0:64], func=AF.Exp), reads=[ps_tot], writes=[dtf["dA"]])
            pr.op("act", lambda e: e.activation(out=acs_sb.ap, in_=ps_acs.ap[:, 0:64], func=AF.Copy), reads=[ps_acs], writes=[acs_sb])
            pr.op("dve", lambda e: e.tensor_tensor(out=tC.ap, in0=ps_tot.ap[:, 0:64], in1=acs_sb.ap, op=ALU.subtract), reads=[ps_tot, acs_sb], writes=[tC])
            pr.op("act", lambda e: e.activation(out=tC.ap, in_=tC.ap, func=AF.Exp), reads=[tC], writes=[tC])
            pr.op("dve", lambda e, c=c: e.tensor_tensor(out=dtv["w"][:, c, :], in0=dtv["dt"][:, c, :], in1=tC.ap, op=ALU.mult),
                  reads=[dtf["dt"], tC], writes=[dtf["w"]])

        pr.marks.append((f"dtprep_end_ph{ph}", pr.seq))
        for g in range(G):
            sl, W = load_block(f"B{g}")
            ssd_conv_tile(ph, sl, W, 0, 32 + g, BT)
            sl, W = load_block(f"C{g}")
            if main:
                ssd_conv_tile(ph, sl, W, 0, 40 + g, CT)
            else:
                ps = psA[state["psa"]]
                state["psa"] ^= 1
                w3c = sl.ap[:, 0:KC * W].rearrange("p (k c) -> p k c", k=KC)

                def fn(e, ps=ps, w3c=w3c):
                    ins = None
                    for kc in range(KC):
                        ins = e.matmul(ps.ap[:, 0:3], lhsT=w3c[:, kc, 0:128], rhs=xT3[:, kc, NTOK - 3:NTOK], start=(kc == 0), stop=(kc == KC - 1))
                    return ins
                pr.op("pe", fn, reads=[sl, xT_sb], writes=[ps])
                pr.op("act", lambda e, ps=ps, g=g: e.activation(out=hist.ap[:, (40 + g) * 3:(40 + g) * 3 + 3], in_=ps.ap[:, 0:3], func=AF.Copy),
                      reads=[ps], writes=[hist])
            for hb_ in range(2):
                sl, W = load_block(f"xs{g}_{hb_}")
                for j in range(2):
                    ssd_conv_tile(ph, sl, W, j, g * 4 + hb_ * 2 + j, xsT[hb_ * 2 + j])
            zsl = []
            if main:
                for zb in range(2):
                    sl, W = load_block(f"z{g}_{zb}")
                    zsl.append(sl)

            def zstep(c, zsl=zsl):
                for zb in range(2):
                    sl = zsl[zb]
                    w3 = sl.ap[:, 0:KC * 256].rearrange("p (k c) -> p k c", k=KC)
                    ps = psA[state["psa"]]
                    state["psa"] ^= 1

                    def fn(e, w3=w3, ps=ps):
                        ins = None
                        for kc in range(KC):
                            ins = e.matmul(ps.ap[:, 0:256], lhsT=xT3[:, kc, c * 128:(c + 1) * 128], rhs=w3[:, kc, :], start=(kc == 0), stop=(kc == KC - 1))
                        return ins
                    pr.op("pe", fn, reads=[sl, xT_sb], writes=[ps])
                    pr.op("act", lambda e, zb=zb, ps=ps: e.activation(out=szc[c].ap[:, zb * 256:(zb + 1) * 256], in_=ps.ap[:, 0:256], func=AF.Silu),
                          reads=[ps], writes=[szc[c]])
            pr.marks.append((f"inproj_end_ph{ph}_g{g}", pr.seq))
            hs = slice(g * 8, g * 8 + 8)
            ms3 = mixstage.ap.rearrange("p (j t) -> p j t", j=4)

            def bcast(k, c, hs=hs):
                return dtv[k][:, c, hs].unsqueeze(2).to_broadcast([128, 8, 64])

            def head_a(c, g=g, hs=hs, bcast=bcast):
                cs = slice(c * 128, (c + 1) * 128)
                x_tok, B_tok = x_tok2[c % 2], B_tok2[c % 2]
                if main:
                    pr.op("pool", lambda e: e.tensor_tensor(
                        out=hd(rhs2.ap, 1), in0=Uincl.unsqueeze(1).to_broadcast([128, 8, 128]),
                        in1=dtv["adt"][:, c, hs].unsqueeze(2).to_broadcast([128, 8, 128]), op=ALU.mult),
                        reads=[consts_sb, dtf["adt"]], writes=[rhs2])
                    for hh in range(2):
                        pr.op("pe", lambda e, hh=hh: e.matmul(psD[hh].ap, lhsT=Ustrict, rhs=rhs2.ap[:, hh * 512:(hh + 1) * 512], start=True, stop=True),
                              reads=[rhs2, consts_sb], writes=[psD[hh]])
                    pr.op("pe", lambda e: e.matmul(psCB.ap, lhsT=BT.ap[:, cs], rhs=CT.ap[:, cs], start=True, stop=True),
                          reads=[BT, CT], writes=[psCB])

                def fn(e):
                    ins = None
                    for j in range(4):
                        ins = e.matmul(psTPx.ap[:, j * 128:(j + 1) * 128], lhsT=xsT[j].ap[:, cs], rhs=idb.ap, start=True, stop=True)
                    return ins
                pr.op("pe", fn, reads=xsT + [idb], writes=[psTPx])
                pr.op("pe", lambda e: e.matmul(psBt.ap, lhsT=BT.ap[:, cs], rhs=idb.ap, start=True, stop=True), reads=[BT, idb], writes=[psBt])
                pr.op("act", lambda e: e.activation(out=x_tok.ap, in_=psTPx.ap, func=AF.Copy), reads=[psTPx], writes=[x_tok])
                pr.op("act", lambda e: e.activation(out=B_tok.ap, in_=psBt.ap, func=AF.Copy), reads=[psBt], writes=[B_tok])
                if main:
                    for hh in range(2):
                        pr.op("act", lambda e, hh=hh: e.activation(out=Eb.ap[:, hh * 512:(hh + 1) * 512], in_=psD[hh].ap, func=AF.Exp),
                              reads=[psD[hh]], writes=[Eb])

            def head_b(c, g=g, hs=hs, bcast=bcast):
                x_tok, xdt, xdd = x_tok2[c % 2], xdt2[c % 2], xdd2[c % 2]
                xDc = xD2[c % 2]
                if main:
                    pr.op("dve", lambda e: e.tensor_tensor(out=hd(xdt.ap, 1), in0=hd(x_tok.ap, 1), in1=bcast("dt", c), op=ALU.mult),
                          reads=[x_tok, dtf["dt"]], writes=[xdt])
                pr.op("dve", lambda e: e.tensor_tensor(out=hd(xdd.ap, 1), in0=hd(x_tok.ap, 1), in1=bcast("w", c), op=ALU.mult),
                      reads=[x_tok, dtf["w"]], writes=[xdd])
                if main:
                    pr.op("pool", lambda e: e.tensor_tensor(out=hd(xDc.ap, 1), in0=hd(x_tok.ap, 1),
                                                            in1=pp[:, P_DH + hs.start:P_DH + hs.stop].unsqueeze(2).to_broadcast([128, 8, 64]), op=ALU.mult),
                          reads=[x_tok, params_sb], writes=[xDc])
                    pr.op("dve", lambda e: e.tensor_tensor(out=Mcb.ap, in0=psCB.ap, in1=Uincl, op=ALU.mult), reads=[psCB, consts_sb], writes=[Mcb])
                    pr.op("dve", lambda e: e.tensor_tensor(out=hd(Mb.ap, 1), in0=hd(Eb.ap, 1), in1=Mcb.ap.unsqueeze(1).to_broadcast([128, 8, 128]), op=ALU.mult),
                          reads=[Eb, Mcb], writes=[Mb])

            def mid(c, g=g, hs=hs, bcast=bcast):
                cs = slice(c * 128, (c + 1) * 128)
                B_tok, xdt, xdd = B_tok2[c % 2], xdt2[c % 2], xdd2[c % 2]
                pr.op("pe", lambda e: e.matmul(psSn.ap, lhsT=B_tok.ap, rhs=xdd.ap, start=True, stop=True), reads=[B_tok, xdd], writes=[psSn])
                if main:
                    def fn(e):
                        ins = None
                        for h in range(8):
                            ins = e.matmul(psY1.ap[:, h * 64:(h + 1) * 64], lhsT=Mb.ap[:, h * 128:(h + 1) * 128], rhs=xdt.ap[:, h * 64:(h + 1) * 64],
                                           start=True, stop=True)
                        return ins
                    pr.op("pe", fn, reads=[Mb, xdt], writes=[psY1])
                    pr.op("pe", lambda e: e.matmul(psY2.ap, lhsT=CT.ap[:, cs], rhs=S_bf.ap, start=True, stop=True), reads=[CT, S_bf], writes=[psY2])
                pr.op("pool", lambda e: e.tensor_tensor(out=hd(S[g].ap, 1), in0=hd(S[g].ap, 1), in1=bcast("dA", c), op=ALU.mult),
                      reads=[S[g], dtf["dA"]], writes=[S[g]])
                pr.op("dve", lambda e: e.tensor_tensor(out=S[g].ap, in0=psSn.ap, in1=S[g].ap, op=ALU.add), reads=[psSn, S[g]], writes=[S[g]])
                if main and c < NCH - 1:
                    pr.op("act", lambda e: e.activation(out=S_bf.ap, in_=S[g].ap, func=AF.Copy), reads=[S[g]], writes=[S_bf])

            def tail_a(c, g=g, hs=hs, bcast=bcast):
                cs = slice(c * 128, (c + 1) * 128)
                xDc = xD2[c % 2]
                pr.op("dve", lambda e: e.tensor_tensor(out=hd(t1.ap, 1), in0=hd(psY2.ap, 1), in1=bcast("expa", c), op=ALU.mult),
                      reads=[psY2, dtf["expa"]], writes=[t1])
                pr.op("dve", lambda e: e.tensor_tensor(out=t2.ap, in0=psY1.ap, in1=t1.ap, op=ALU.add), reads=[psY1, t1], writes=[t2])
                pr.op("dve", lambda e: e.tensor_tensor(out=t2.ap, in0=t2.ap, in1=xDc.ap, op=ALU.add), reads=[t2, xDc], writes=[t2])
                pr.op("dve", lambda e: e.tensor_tensor(out=t2.ap, in0=t2.ap, in1=szc[c].ap, op=ALU.mult), reads=[t2, szc[c]], writes=[t2])
                pr.op("dve", lambda e: e.memset(ssb.ap[:, 0:1], 0.0), writes=[ssb])
                pr.op("act", lambda e: e.activation(out=t1.ap, in_=t2.ap, func=AF.Square, accum_out=ssb.ap[:, 0:1]), reads=[t2], writes=[t1, ssb])
                pr.op("act", lambda e: e.activation(out=ssb.ap[:, 1:2], in_=ssb.ap[:, 0:1], func=AF.Ln, bias=EPS, scale=1.0 / 512.0), reads=[ssb], writes=[ssb])
                pr.op("act", lambda e: e.activation(out=ssb.ap[:, 2:3], in_=ssb.ap[:, 1:2], func=AF.Exp, scale=-0.5), reads=[ssb], writes=[ssb])
                pr.op("dve", lambda e: e.tensor_scalar_mul(out=ob.ap, in0=t2.ap, scalar1=ssb.ap[:, 2:3]), reads=[t2, ssb], writes=[ob])

            def tail_b(c, g=g, hs=hs, bcast=bcast):
                cs = slice(c * 128, (c + 1) * 128)

                def fn(e):
                    ins = None
                    for j in range(4):
                        ins = e.matmul(psTPo.ap[:, j * 128:(j + 1) * 128], lhsT=ob.ap[:, j * 128:(j + 1) * 128], rhs=idb.ap, start=True, stop=True)
                    return ins
                pr.op("pe", fn, reads=[ob, idb], writes=[psTPo])
                for j in range(4):
                    nw = pp[:, P_NORMW + g * 4 + j:P_NORMW + g * 4 + j + 1]
                    pr.op("act", lambda e, j=j, nw=nw: e.activation(out=ms3[:, j, cs], in_=psTPo.ap[:, j * 128:(j + 1) * 128], func=AF.Identity, scale=nw),
                          reads=[psTPo, params_sb], writes=[mixstage])

            if main:
                head_a(0)
                head_b(0)
                for c in range(NCH):
                    mid(c)
                    if c + 1 < NCH:
                        head_a(c + 1)
                    if c < 4:
                        zstep(2 * c)
                        zstep(2 * c + 1)
                    if c == 3:
                        prefetch(2)
                    tail_a(c)
                    tail_b(c)
                    if c + 1 < NCH:
                        head_b(c + 1)
            else:
                prefetch(2)
                for c in range(2):
                    head_a(c)
                    head_b(c)
                for c in range(NCH):
                    mid(c)
                    if c + 2 < NCH:
                        head_a(c + 2)
                        head_b(c + 2)
            if main:
                pr.dma("sp", lambda e, g=g: e.dma_start(out=mixT[g * 4:(g + 1) * 4].rearrange("j p t -> p j t"),
                                                          in_=mixstage.ap.rearrange("p (j t) -> p j t", j=4)), "mixst", reads=[mixstage])
            else:
                pr.op("dve", lambda e, g=g: e.tensor_scalar_mul(out=S[g].ap, in0=S[g].ap, scalar1=pp[:, P_FLAG:P_FLAG + 1]), reads=[S[g], params_sb], writes=[S[g]])
            if main and g < G - 1:
                pr.op("act", lambda e, g=g: e.activation(out=S_bf.ap, in_=S[g + 1].ap, func=AF.Copy), reads=[S[g + 1]], writes=[S_bf])
        if not main:
            pr.op("act", lambda e: e.activation(out=S_bf.ap, in_=S[0].ap, func=AF.Copy), reads=[S[0]], writes=[S_bf])

        pr.marks.append((f"ssd_end_ph{ph}", pr.seq))
        for k in range(16):
            sl, W = load_block(f"lx{k}")
            lw = lws[k % 2]
            pr.dma("pool", lambda e, k=k, lw=lw: e.dma_start(out=lw.ap, in_=lruw[k]), lw.sem, writes=[lw])
            lw4 = lw.ap.rearrange("p (a i j) -> p a i j", a=2, i=2)
            for i in range(2):
                conv_tile(ph, sl, W, i, 48 + 2 * k + i, u[i])
            if main:
                slg_sl, Wg = load_block(f"lg{k}")
                for j in range(2):
                    for tt in range(2):
                        ps = inproj_cm(slg_sl, Wg, j, tt)
                        pr.op("act", lambda e, ps=ps, tt=tt, j=j: e.activation(out=slg2[j].ap[:, tt * 512:(tt + 1) * 512], in_=ps.ap[:, 0:512], func=AF.Silu),
                              reads=[ps], writes=[slg2[j]])
            for i in range(2):
                pr.op("act", lambda e, i=i: e.activation(out=u_bf[i].ap, in_=u[i].ap, func=AF.Copy), reads=[u[i]], writes=[u_bf[i]])
            for j in range(2):
                tl = 2 * k + j
                gr, gi, ab, hb, slg = gr2[j], gi2[j], ab2[j], hb2[j], slg2[j]
                for tt in range(2):
                    ts_ = slice(tt * 512, (tt + 1) * 512)
                    for a_, (psg, dst, bcol) in enumerate(((psD[0], gr, P_LBA), (psTPx, gi, P_LBX))):
                        def fn(e, a_=a_, psg=psg, ts_=ts_, j=j, lw4=lw4):
                            ins = None
                            for i in range(2):
                                ins = e.matmul(psg.ap, lhsT=lw4[:, a_, i, j * 128:(j + 1) * 128], rhs=u_bf[i].ap[:, ts_], start=(i == 0), stop=(i == 1))
                            return ins
                        pr.op("pe", fn, reads=[lw, u_bf[0], u_bf[1]], writes=[psg])
                        pr.op("act", lambda e, psg=psg, dst=dst, bcol=bcol, ts_=ts_, tl=tl: e.activation(
                            out=dst.ap[:, ts_], in_=psg.ap, func=AF.Sigmoid, bias=pp[:, bcol + tl:bcol + tl + 1], scale=1.0),
                            reads=[psg, params_sb], writes=[dst])
                pr.op("act", lambda e, tl=tl, gr=gr, ab=ab: e.activation(out=ab.ap, in_=gr.ap, func=AF.Exp, scale=coef.ap[:, tl:tl + 1]), reads=[gr, coef], writes=[ab])
                pr.op("act", lambda e, tl=tl, gr=gr: e.activation(out=gr.ap, in_=gr.ap, func=AF.Exp, scale=coef2.ap[:, tl:tl + 1]), reads=[gr, coef2], writes=[gr])
                pr.op("act", lambda e, gr=gr: e.activation(out=gr.ap, in_=gr.ap, func=AF.Sqrt, bias=1.0, scale=-1.0), reads=[gr], writes=[gr])
                pr.op("dve", lambda e, j=j, gi=gi: e.tensor_tensor(out=gi.ap, in0=gi.ap, in1=u[j].ap, op=ALU.mult), reads=[gi, u[j]], writes=[gi])
                pr.op("dve", lambda e, gi=gi, gr=gr: e.tensor_tensor(out=gi.ap, in0=gi.ap, in1=gr.ap, op=ALU.mult), reads=[gi, gr], writes=[gi])
                pr.op("dve", lambda e, tl=tl, hb=hb, ab=ab, gi=gi: e.tensor_tensor_scan(out=hb.ap, data0=ab.ap, data1=gi.ap, initial=hstate.ap[:, tl:tl + 1], op0=ALU.mult, op1=ALU.add),
                      reads=[ab, gi, hstate], writes=[hb])
                if main:
                    pr.op("dve", lambda e, hb=hb, slg=slg: e.tensor_tensor(out=mo.ap, in0=hb.ap, in1=slg.ap, op=ALU.mult), reads=[hb, slg], writes=[mo])
                    pr.dma("sp", lambda e, tl=tl: e.dma_start(out=mixT[32 + tl], in_=mo.ap), "mo", reads=[mo])
                else:
                    pr.op("dve", lambda e, tl=tl, hb=hb: e.tensor_scalar_mul(out=hstate.ap[:, tl:tl + 1], in0=hb.ap[:, 1023:1024], scalar1=pp[:, P_FLAG:P_FLAG + 1]),
                          reads=[hb, params_sb], writes=[hstate])

    pr.marks.append(("phaseA_end", pr.seq))
    bar = [(e, pr.cnt[e]) for e in ("pe", "act", "dve", "pool")] + [("d:" + s, n) for s, n in pr.dcnt.items()]
    m3 = mixT_sb.ap.rearrange("p (k t) -> p k t", k=64)
    mparts = [Buf(m3[:, q * 8:(q + 1) * 8, :], f"mixld{q}") for q in range(8)]
    for q in range(8):
        pr.dma("sp", lambda e, q=q: e.dma_start(out=mparts[q].ap, in_=mixT[q * 8:(q + 1) * 8].rearrange("k p t -> p k t")),
               mparts[q].sem, writes=[mparts[q]], extra=bar if q == 0 else ())
    pso_i = 0
    xi = 0
    for cbk in range(16):
        wsl = wo[cbk % 2]
        pr.dma("pool", lambda e, cbk=cbk, wsl=wsl: e.dma_start(out=wsl.ap, in_=wout[cbk]), wsl.sem, writes=[wsl], extra=bar if cbk < 2 else ())
        wo3 = wsl.ap.rearrange("p (k c) -> p k c", k=64)
        for tk in range(8):
            ps = psO[pso_i % 4]
            pso_i += 1
            xb, sb = xr[xi % 2], stg[xi % 2]
            xi += 1
            pr.dma("sp", lambda e, xb=xb, tk=tk, cbk=cbk: e.dma_start(out=xb.ap, in_=xres[tk * 128:(tk + 1) * 128, cbk * 256:(cbk + 1) * 256]),
                   xb.sem, writes=[xb], extra=bar if xi <= 2 else ())

            def fn(e, ps=ps, tk=tk, wo3=wo3):
                ins = None
                for kc in range(64):
                    ins = e.matmul(ps.ap[:, 0:256], lhsT=m3[:, kc, tk * 128:(tk + 1) * 128], rhs=wo3[:, kc, :], start=(kc == 0), stop=(kc == 63))
                return ins
            pr.op("pe", fn, reads=mparts + [wsl], writes=[ps])
            pr.op("dve", lambda e, ps=ps, xb=xb, sb=sb: e.scalar_tensor_tensor(out=sb.ap, in0=xb.ap, scalar=ALPHA, in1=ps.ap[:, 0:256], op0=ALU.mult, op1=ALU.add),
                  reads=[xb, ps], writes=[sb], extra=bar if xi <= 2 else ())
            pr.dma("sp", lambda e, sb=sb, tk=tk, cbk=cbk: e.dma_start(out=pre_scr[tk * 128:(tk + 1) * 128, cbk * 256:(cbk + 1) * 256], in_=sb.ap),
                   sb.sem, reads=[sb])
    pr.marks.append(("outproj_end", pr.seq))
    bar2 = [(e, pr.cnt[e]) for e in ("pe", "dve")] + [("d:" + s, n) for s, n in pr.dcnt.items()]
    pr.dma("sp", lambda e: e.dma_start(out=lngb_sb.ap, in_=lngb[:, :]), "lngb", writes=[lngb_sb], extra=bar2)
    out_toks = []
    for tk in range(8):
        rb, ob_ = lnr[tk % 2], lno[tk % 2]
        pr.dma("sp", lambda e, rb=rb, tk=tk: e.dma_start(out=rb.ap, in_=pre_scr[tk * 128:(tk + 1) * 128, :]), rb.sem, writes=[rb], extra=bar2 if tk < 2 else ())
        st6 = stats.ap[:, 0:48].rearrange("p (c s) -> p c s", c=8)
        for q in range(8):
            pr.op("dve", lambda e, q=q, rb=rb: e.bn_stats(out=st6[:, q, :], in_=rb.ap[:, q * 512:(q + 1) * 512]), reads=[rb], writes=[stats])
        pr.op("dve", lambda e: e.bn_aggr(out=stats.ap[:, 48:50], in_=st6), reads=[stats], writes=[stats])
        pr.op("act", lambda e: e.activation(out=stats.ap[:, 50:51], in_=stats.ap[:, 49:50], func=AF.Sqrt, bias=EPS, scale=1.0), reads=[stats], writes=[stats])
        pr.op("dve", lambda e: e.reciprocal(out=stats.ap[:, 51:52], in_=stats.ap[:, 50:51]), reads=[stats], writes=[stats])
        pr.op("dve", lambda e, rb=rb, ob_=ob_: e.tensor_scalar(out=ob_.ap, in0=rb.ap, scalar1=stats.ap[:, 48:49], scalar2=stats.ap[:, 51:52], op0=ALU.subtract, op1=ALU.mult),
              reads=[rb, stats], writes=[ob_], extra=bar2 if tk < 2 else ())
        pr.op("dve", lambda e, ob_=ob_: e.tensor_tensor(out=ob_.ap, in0=ob_.ap, in1=lngb_sb.ap[:, 0:D], op=ALU.mult), reads=[ob_, lngb_sb], writes=[ob_])
        pr.op("dve", lambda e, ob_=ob_: e.tensor_tensor(out=ob_.ap, in0=ob_.ap, in1=lngb_sb.ap[:, D:2 * D], op=ALU.add), reads=[ob_, lngb_sb], writes=[ob_])
        out_toks.append(pr.dma("sp", lambda e, ob_=ob_, tk=tk: e.dma_start(out=out[tk * 128:(tk + 1) * 128, :], in_=ob_.ap), ob_.sem, reads=[ob_]))

    pr.marks.append(("end", pr.seq))
    nc._marks = pr.marks
    import contextlib
    with contextlib.ExitStack() as es:
        sems = {e: es.enter_context(nc.semaphore("s_" + e)) for e in ("pe", "act", "dve", "pool")}
        for s in pr.dcnt:
            sems["d:" + s] = es.enter_context(nc.semaphore("d_" + s))
        block = es.enter_context(nc.Block())

        def run(engname):
            def body(eng):
                waited = {}
                for fn, waits, tok in pr.ops[engname]:
                    for (sn, val) in waits:
                        if sn == "pe" and engname == "pe":
                            continue
                        if waited.get(sn, 0) >= val:
                            continue
                        waited[sn] = val
                        eng.wait_ge(sems[sn], val)
                    ins = fn(eng)
                    ins.then_inc(sems[tok[0]], 16 if tok[0].startswith("d:") else 1)
                if engname == "sp":
                    fin = [(e_, pr.cnt[e_]) for e_ in ("pe", "act", "dve", "pool")] + [("d:" + s_, n_) for s_, n_ in pr.dcnt.items()]
                    for (sn, val) in fin:
                        if val > 0:
                            eng.wait_ge(sems[sn], val)
            return body
        block.tensor(run("pe"))
        block.scalar(run("act"))
        block.vector(run("dve"))
        block.gpsimd(run("pool"))
        block.sync(run("sp"))
    return nc


_CACHE = {}


def _host_prep(inp):
    f = np.float32
    w_in = np.asarray(inp["w_in"][0], f)
    w3 = w_in.reshape(KC, 128, -1)
    win = np.empty((128, WIN_COLS), f)
    for name, cols in BLOCKS:
        off, W = BLK_OFF[name]
        blk = w3[:, :, cols[0]:cols[-1] + 1]
        win[:, off:off + KC * W] = blk.transpose(1, 0, 2).reshape(128, KC * W)
    w_out = np.asarray(inp["w_out"][0], f)
    wout = np.ascontiguousarray(w_out.reshape(64, 128, 16, 256).transpose(2, 1, 0, 3).reshape(16, 128, 64 * 256))
    wa = np.asarray(inp["lru_wa"][0], f).reshape(16, 2, 128, 256)
    wx = np.asarray(inp["lru_wx"][0], f).reshape(16, 2, 128, 256)
    lruw = np.ascontiguousarray(np.stack([wa, wx], 1).transpose(0, 3, 1, 2, 4).reshape(16, 128, 1024))
    params = np.zeros((128, NPAR), f)
    params[:, P_DTB:P_DTB + 64] = np.asarray(inp["ssd_dt_bias"][0], f)[None, :]
    params[:, P_ALOG:P_ALOG + 64] = np.asarray(inp["ssd_a_log"][0], f)[None, :]
    params[:, P_DH:P_DH + 64] = np.asarray(inp["ssd_d"][0], f)[None, :]
    cw = np.concatenate([np.asarray(inp["ssd_conv_w"][0], f), np.asarray(inp["lru_conv_w"][0], f)], 1)
    cbias = np.concatenate([np.asarray(inp["ssd_conv_b"][0], f), np.asarray(inp["lru_conv_b"][0], f)], 0)
    cp = np.concatenate([cw, cbias[None, :]], 0)
    params[:, P_CONV:P_CONV + 400] = cp.reshape(5, 80, 128).transpose(2, 1, 0).reshape(128, 400)
    params[:, P_NORMW:P_NORMW + 32] = np.asarray(inp["ssd_norm_w"][0], f).reshape(32, 128).T
    params[:, P_LBA:P_LBA + 32] = np.asarray(inp["lru_ba"][0], f).reshape(32, 128).T
    params[:, P_LBX:P_LBX + 32] = np.asarray(inp["lru_bx"][0], f).reshape(32, 128).T
    params[:, P_LAM:P_LAM + 32] = np.asarray(inp["lru_lambda"][0], f).reshape(32, 128).T
    consts = np.zeros((128, 512), f)
    j = np.arange(128)
    consts[:, 0:128] = np.eye(128, dtype=f)
    consts[:, 128:256] = (j[:, None] <= j[None, :]).astype(f)
    consts[:, 256:384] = (j[:, None] > j[None, :]).astype(f)
    consts[:, 384:512] = 1.0
    lngb = np.concatenate([np.broadcast_to(np.asarray(inp["ln_g"][0], f)[None, :], (128, D)),
                           np.broadcast_to(np.asarray(inp["ln_b"][0], f)[None, :], (128, D))], 1)
    lngb = np.ascontiguousarray(lngb)
    x = np.asarray(inp["x"], f)
    in_maps = []
    for core in range(8):
        b, h = core // 2, core % 2
        xm = x[b, h * NTOK:(h + 1) * NTOK]
        xw = x[b, 0:NTOK] if h == 1 else np.zeros((NTOK, D), f)

        def tr(a):
            return a.T.reshape(KC, 128, NTOK).transpose(1, 0, 2).reshape(128, KC * NTOK)
        xT = np.ascontiguousarray(np.stack([tr(xw), tr(xm)], 0))
        p = params.copy()
        p[:, P_FLAG] = float(h)
        in_maps.append({"xT": xT, "xres": np.ascontiguousarray(xm), "win": win, "wout": wout, "lruw": lruw,
                        "params": p, "consts": consts, "lngb": lngb})
    return in_maps


def kernel(**inputs):
    if "nc" not in _CACHE:
        _CACHE["nc"] = build_nc()
    nc = _CACHE["nc"]
    in_maps = _host_prep(inputs)
    res = run_bass_kernel_spmd(nc, in_maps, core_ids=list(range(8)))
    outp = np.empty((4, 2048, D), np.float32)
    for core in range(8):
        b, h = core // 2, core % 2
        outp[b, h * NTOK:(h + 1) * NTOK] = res.results[core]["out"]
    return outp
```

```python
import numpy as np
import ml_dtypes
import concourse.bass as bass
import concourse.mybir as mybir
from concourse.bass_utils import run_bass_kernel_spmd

F32, BF16 = mybir.dt.float32, mybir.dt.bfloat16
AF = mybir.ActivationFunctionType
ALU = mybir.AluOpType
AX = mybir.AxisListType

D = 4096
NTOK = 1024
NCH = 8
KC = 32
G = 8
NLT = 32
ALPHA = 2.0 ** 0.25
EPS = 1e-5
ENGS = ("pe", "act", "dve", "pool", "sp")

P_DTB, P_ALOG, P_DH, P_CONV, P_NORMW, P_LBA, P_LBX, P_LAM, P_FLAG, NPAR = 0, 64, 128, 192, 592, 624, 656, 688, 720, 721

def _blocks():
    blks = [("dt", list(range(10240, 10304)))]
    for g in range(G):
        blks.append((f"B{g}", list(range(8192 + 128 * g, 8192 + 128 * (g + 1)))))
        blks.append((f"C{g}", list(range(9216 + 128 * g, 9216 + 128 * (g + 1)))))
        for hb in range(2):
            blks.append((f"xs{g}_{hb}", list(range(4096 + 512 * g + 256 * hb, 4096 + 512 * g + 256 * (hb + 1)))))
        for hb in range(2):
            blks.append((f"z{g}_{hb}", list(range(512 * g + 256 * hb, 512 * g + 256 * (hb + 1)))))
    for k in range(16):
        blks.append((f"lx{k}", list(range(10304 + 256 * k, 10304 + 256 * (k + 1)))))
        blks.append((f"lg{k}", list(range(14400 + 256 * k, 14400 + 256 * (k + 1)))))
    return blks

BLOCKS = _blocks()
BLK_OFF = {}
_o = 0
for _n, _c in BLOCKS:
    BLK_OFF[_n] = (_o, len(_c))
    _o += KC * len(_c)
WIN_COLS = _o


class Buf:
    __slots__ = ("ap", "w", "r", "sem")

    def __init__(self, ap, sem=None):
        self.ap, self.w, self.r, self.sem = ap, None, [], sem


class Prog:
    def __init__(self):
        self.ops = {e: [] for e in ENGS}
        self.cnt = {e: 0 for e in ENGS}
        self.dcnt = {}
        self.seq = 0
        self.limit = None
        self.marks = []

    def _hz(self, reads, writes, extra):
        waits = [b.w for b in reads if b.w is not None]
        for b in writes:
            waits += b.r
            if b.w is not None:
                waits.append(b.w)
        waits += [t for t in extra if t is not None]
        return waits

    def op(self, eng, fn, reads=(), writes=(), extra=()):
        self.seq += 1
        if self.limit is not None and self.seq > self.limit:
            return None
        waits = self._hz(reads, writes, extra)
        self.cnt[eng] += 1
        tok = (eng, self.cnt[eng])
        self.ops[eng].append((fn, waits, tok))
        for b in reads:
            b.r.append(tok)
        for b in writes:
            b.w, b.r = tok, []
        return tok

    def dma(self, eng, fn, sem, reads=(), writes=(), extra=()):
        self.seq += 1
        if self.limit is not None and self.seq > self.limit:
            return None
        waits = self._hz(reads, writes, extra)
        self.dcnt[sem] = self.dcnt.get(sem, 0) + 16
        tok = ("d:" + sem, self.dcnt[sem])
        self.ops[eng].append((fn, waits, tok))
        for b in reads:
            b.r.append(tok)
        for b in writes:
            b.w, b.r = tok, []
        return tok


def build_nc(limit=None):
    nc = bass.Bass("TRN2", target_bir_lowering=False)
    dram = {}

    def din(name, shape, dt=F32):
        dram[name] = nc.dram_tensor(name, list(shape), dt, kind="ExternalInput").ap()
        return dram[name]

    xT = din("xT", [2, 128, KC * NTOK])
    xres = din("xres", [NTOK, D])
    win = din("win", [128, WIN_COLS])
    wout = din("wout", [16, 128, 64 * 256])
    lruw = din("lruw", [16, 128, 1024])
    params = din("params", [128, NPAR])
    consts = din("consts", [128, 512])
    lngb = din("lngb", [128, 2 * D])
    out = nc.dram_tensor("out", [NTOK, D], F32, kind="ExternalOutput").ap()
    mixT = nc.dram_tensor("mixT_scr", [64, 128, NTOK], BF16, kind="Internal").ap()
    pre_scr = nc.dram_tensor("pre_scr", [NTOK, D], F32, kind="Internal").ap()

    ARENA = 102784
    arena_t = nc.alloc_sbuf_tensor("arena", [128, ARENA], BF16)
    small_t = nc.alloc_sbuf_tensor("small", [128, 1728], F32)
    idb_t = nc.alloc_sbuf_tensor("idb", [128, 128], BF16)
    ps_t = [nc.alloc_psum_tensor(f"ps{i}", [128, 512], F32) for i in range(8)]
    arena = arena_t[:]
    small = small_t[:]

    class Carve:
        def __init__(self, base=0):
            self.o = base

        def bf(self, n):
            a = arena[:, self.o:self.o + n]
            self.o += n
            return a

        def f32(self, n):
            a = arena[:, self.o:self.o + 2 * n].bitcast(F32)
            self.o += 2 * n
            return a

    ca = Carve()
    xT_sb = Buf(ca.bf(KC * NTOK), "xT")
    ws = [Buf(ca.bf(KC * 256), f"ws{i}") for i in range(2)]
    pre = [Buf(ca.f32(1028)) for _ in range(2)]
    acc = [Buf(ca.f32(NTOK)) for _ in range(2)]
    mix_base = ca.o
    xsT = [Buf(ca.bf(NTOK)) for _ in range(4)]
    BT = Buf(ca.bf(NTOK))
    CT = Buf(ca.bf(NTOK))
    szc = [Buf(ca.bf(512)) for _ in range(NCH)]
    mixstage = Buf(ca.bf(4 * NTOK), "mixst")
    rhs2 = Buf(ca.f32(1024))
    Eb = Buf(ca.f32(1024))
    Mb = Buf(ca.bf(1024))
    t1 = Buf(ca.f32(512))
    t2 = Buf(ca.f32(512))
    xD2 = [Buf(ca.f32(512)) for _ in range(2)]
    x_tok2 = [Buf(ca.bf(512)) for _ in range(2)]
    xdt2 = [Buf(ca.bf(512)) for _ in range(2)]
    xdd2 = [Buf(ca.bf(512)) for _ in range(2)]
    ob = Buf(ca.bf(512))
    Mcb = Buf(ca.f32(128))
    B_tok2 = [Buf(ca.bf(128)) for _ in range(2)]
    S_bf = Buf(ca.bf(512))
    ssd_end = ca.o
    cl = Carve(mix_base)
    u = [Buf(cl.f32(NTOK)) for _ in range(2)]
    u_bf = [Buf(cl.bf(NTOK)) for _ in range(2)]
    gr2 = [Buf(cl.f32(NTOK)) for _ in range(2)]
    gi2 = [Buf(cl.f32(NTOK)) for _ in range(2)]
    ab2 = [Buf(cl.f32(NTOK)) for _ in range(2)]
    hb2 = [Buf(cl.f32(NTOK)) for _ in range(2)]
    slg2 = [Buf(cl.f32(NTOK)) for _ in range(2)]
    mo = Buf(cl.bf(NTOK), "mo")
    lru_end = cl.o
    ca.o = max(ssd_end, lru_end)
    dtf = {k: Buf(ca.f32(NCH * 64)) for k in ("dt", "adt", "w", "expa", "dA")}
    tmp64 = [Buf(ca.f32(64)) for _ in range(4)]
    acs_sb = Buf(ca.f32(64))
    lws = [Buf(ca.bf(1024), f"lw{i}") for i in range(2)]
    S = [Buf(ca.f32(512)) for _ in range(G)]
    a_end = ca.o
    assert a_end <= ARENA, a_end
    cb_ = Carve()
    mixT_sb = Buf(cb_.bf(64 * NTOK), "mixld")
    wo = [Buf(cb_.bf(64 * 256), f"wo{i}") for i in range(2)]
    xr = [Buf(cb_.f32(256), f"xr{i}") for i in range(2)]
    stg = [Buf(cb_.f32(256), f"stg{i}") for i in range(2)]
    assert cb_.o <= ARENA, cb_.o
    cc = Carve()
    lnr = [Buf(cc.f32(D), f"lnr{i}") for i in range(2)]
    lno = [Buf(cc.f32(D), f"lno{i}") for i in range(2)]
    lngb_sb = Buf(cc.f32(2 * D), "lngb")
    assert cc.o <= 64 * NTOK

    params_sb = Buf(small[:, 0:NPAR], "params")
    consts_sb = Buf(small[:, 724:724 + 512], "consts")
    so = 724 + 512
    hist = Buf(small[:, so:so + 240]); so += 240
    hstate = Buf(small[:, so:so + 32]); so += 32
    Aneg = Buf(small[:, so:so + 64]); so += 64
    coef = Buf(small[:, so:so + 32]); so += 32
    coef2 = Buf(small[:, so:so + 32]); so += 32
    ssb = Buf(small[:, so:so + 8]); so += 8
    stats = Buf(small[:, so:so + 64]); so += 64
    assert so <= 1728, so
    idb = Buf(idb_t[:])
    pp = params_sb.ap
    ident_f, Uincl, Ustrict, ones_f = (consts_sb.ap[:, i * 128:(i + 1) * 128] for i in range(4))

    psA = [Buf(ps_t[0][:]), Buf(ps_t[1][:])]
    psY1 = Buf(ps_t[4][:])
    psY2 = Buf(ps_t[5][:])
    psSn = Buf(ps_t[6][:])
    psD = [Buf(ps_t[2][:]), psA[1]]
    psTPx = Buf(ps_t[3][:])
    psTPo = psA[0]
    psBt = Buf(ps_t[7][:, 0:128])
    psCB = Buf(ps_t[7][:, 128:256])
    ps64 = [Buf(ps_t[7][:, 256:320])]
    ps_acs = psSn
    ps_tot = psY2
    psO = [Buf(ps_t[i][:]) for i in (2, 3, 4, 5)]

    pr = Prog()
    pr.limit = limit
    hd = lambda a, g8: a.rearrange("p (h q) -> p h q", h=8) if g8 else a

    pr.dma("sp", lambda e: e.dma_start(out=params_sb.ap, in_=params[:, :]), "params", writes=[params_sb])
    pr.dma("sp", lambda e: e.dma_start(out=consts_sb.ap, in_=consts[:, :]), "consts", writes=[consts_sb])
    pr.op("dve", lambda e: e.tensor_copy(out=idb.ap, in_=ident_f), reads=[consts_sb], writes=[idb])
    for b_ in pre:
        pr.op("dve", lambda e, b_=b_: e.memset(b_.ap, 0.0), writes=[b_])
    pr.op("dve", lambda e: e.memset(hstate.ap, 0.0), writes=[hstate])
    pr.op("dve", lambda e: e.memset(hist.ap, 0.0), writes=[hist])
    for g in range(G):
        pr.op("dve", lambda e, g=g: e.memset(S[g].ap, 0.0), writes=[S[g]])
    pr.op("act", lambda e: e.activation(out=Aneg.ap, in_=pp[:, P_ALOG:P_ALOG + 64], func=AF.Exp), reads=[params_sb], writes=[Aneg])
    pr.op("dve", lambda e: e.tensor_scalar_mul(out=Aneg.ap, in0=Aneg.ap, scalar1=-1.0), reads=[Aneg], writes=[Aneg])
    pr.op("act", lambda e: e.activation(out=coef.ap, in_=pp[:, P_LAM:P_LAM + 32], func=AF.Exp, scale=-1.0), reads=[params_sb], writes=[coef])
    pr.op("act", lambda e: e.activation(out=coef.ap, in_=coef.ap, func=AF.Ln, bias=1.0, scale=1.0), reads=[coef], writes=[coef])
    pr.op("dve", lambda e: e.tensor_scalar_mul(out=coef2.ap, in0=coef.ap, scalar1=-16.0), reads=[coef], writes=[coef2])
    pr.op("dve", lambda e: e.tensor_scalar_mul(out=coef.ap, in0=coef.ap, scalar1=-8.0), reads=[coef, coef2], writes=[coef])

    pr.marks.append(("setup_end", pr.seq))
    xT3 = xT_sb.ap.rearrange("p (k t) -> p k t", k=KC)
    state = {"slot": 0, "psa": 0, "pa": 0}

    bq = {"order": [], "next": 0, "loaded": {}}

    def prefetch(n):
        for _ in range(n):
            if bq["next"] >= len(bq["order"]):
                return
            name = bq["order"][bq["next"]]
            bq["next"] += 1
            off, W = BLK_OFF[name]
            sl = ws[state["slot"]]
            state["slot"] ^= 1
            pr.dma("pool", lambda e, sl=sl, off=off, W=W: e.dma_start(out=sl.ap[:, 0:KC * W], in_=win[:, off:off + KC * W]),
                   sl.sem, writes=[sl])
            bq["loaded"][name] = (sl, W)

    def load_block(name):
        while name not in bq["loaded"]:
            prefetch(1)
        return bq["loaded"].pop(name)

    def inproj_cm(sl, W, j, tt):
        ps = psA[state["psa"]]
        state["psa"] ^= 1
        w3 = sl.ap[:, 0:KC * W].rearrange("p (k c) -> p k c", k=KC)

        def fn(e, ps=ps, w3=w3, j=j, tt=tt):
            ins = None
            for kc in range(KC):
                ins = e.matmul(ps.ap[:, 0:512], lhsT=w3[:, kc, j * 128:(j + 1) * 128], rhs=xT3[:, kc, tt * 512:(tt + 1) * 512],
                               start=(kc == 0), stop=(kc == KC - 1))
            return ins
        pr.op("pe", fn, reads=[sl, xT_sb], writes=[ps])
        return ps

    def conv_tile(ph, sl, W, j, tile_id, accb):
        pb = pre[state["pa"]]
        state["pa"] ^= 1
        if ph == 1:
            pr.op("dve", lambda e: e.tensor_copy(out=pb.ap[:, 0:3], in_=hist.ap[:, tile_id * 3:tile_id * 3 + 3]), reads=[hist], writes=[pb])
        for tt in range(2):
            ps = inproj_cm(sl, W, j, tt)
            pr.op("act", lambda e, ps=ps, tt=tt: e.activation(out=pb.ap[:, 3 + tt * 512:3 + (tt + 1) * 512], in_=ps.ap[:, 0:512], func=AF.Copy),
                  reads=[ps], writes=[pb])
        cw = pp[:, P_CONV + tile_id * 5:P_CONV + tile_id * 5 + 5]
        pr.op("dve", lambda e: e.tensor_scalar(out=accb.ap, in0=pb.ap[:, 3:1027], scalar1=cw[:, 3:4], scalar2=cw[:, 4:5], op0=ALU.mult, op1=ALU.add),
              reads=[pb, params_sb], writes=[accb])
        for k in (2, 1, 0):
            pr.op("dve", lambda e, k=k: e.scalar_tensor_tensor(out=accb.ap, in0=pb.ap[:, k:k + 1024], scalar=cw[:, k:k + 1], in1=accb.ap, op0=ALU.mult, op1=ALU.add),
                  reads=[pb, accb], writes=[accb])
        if ph == 0:
            pr.op("dve", lambda e: e.tensor_copy(out=hist.ap[:, tile_id * 3:tile_id * 3 + 3], in_=pb.ap[:, 1024:1027]), reads=[pb], writes=[hist])

    def ssd_conv_tile(ph, sl, W, j, tile_id, dst):
        accb = acc[state["pa"]]
        conv_tile(ph, sl, W, j, tile_id, accb)
        pr.op("act", lambda e: e.activation(out=dst.ap, in_=accb.ap, func=AF.Silu), reads=[accb], writes=[dst])

    for ph in range(2):
        main = ph == 1
        order = ["dt"]
        for g_ in range(G):
            order += [f"B{g_}", f"C{g_}", f"xs{g_}_0", f"xs{g_}_1"] + ([f"z{g_}_0", f"z{g_}_1"] if main else [])
        for k_ in range(16):
            order += [f"lx{k_}"] + ([f"lg{k_}"] if main else [])
        bq["order"], bq["next"], bq["loaded"] = order, 0, {}
        pr.dma("pool", lambda e, ph=ph: e.dma_start(out=xT_sb.ap, in_=xT[ph]), "xT", writes=[xT_sb])
        sl, W = load_block("dt")
        w3 = sl.ap[:, 0:KC * 64].rearrange("p (k c) -> p k c", k=KC)
        dtv = {k: v.ap.rearrange("p (c h) -> p c h", c=NCH) for k, v in dtf.items()}
        for c in range(NCH):
            dps = ps64[0]

            def fn(e, c=c, w3=w3, dps=dps):
                ins = None
                for kc in range(KC):
                    ins = e.matmul(dps.ap, lhsT=xT3[:, kc, c * 128:(c + 1) * 128], rhs=w3[:, kc, :], start=(kc == 0), stop=(kc == KC - 1))
                return ins
            pr.op("pe", fn, reads=[sl, xT_sb], writes=[dps])
            tA, tB, tC, tD = tmp64
            pr.op("dve", lambda e: e.tensor_tensor(out=tA.ap, in0=dps.ap, in1=pp[:, P_DTB:P_DTB + 64], op=ALU.add), reads=[dps, params_sb], writes=[tA])
            pr.op("dve", lambda e: e.tensor_scalar_mul(out=tB.ap, in0=tA.ap, scalar1=-1.0), reads=[tA], writes=[tB])
            pr.op("dve", lambda e: e.tensor_tensor(out=tB.ap, in0=tB.ap, in1=tA.ap, op=ALU.max), reads=[tA, tB], writes=[tB])
            pr.op("act", lambda e: e.activation(out=tB.ap, in_=tB.ap, func=AF.Exp, scale=-1.0), reads=[tB], writes=[tB])
            pr.op("act", lambda e: e.activation(out=tB.ap, in_=tB.ap, func=AF.Ln, bias=1.0, scale=1.0), reads=[tB], writes=[tB])
            pr.op("dve", lambda e, c=c: e.scalar_tensor_tensor(out=dtv["dt"][:, c, :], in0=tA.ap, scalar=0.0, in1=tB.ap, op0=ALU.max, op1=ALU.add),
                  reads=[tA, tB], writes=[dtf["dt"]])
            pr.op("dve", lambda e, c=c: e.tensor_tensor(out=dtv["adt"][:, c, :], in0=dtv["dt"][:, c, :], in1=Aneg.ap, op=ALU.mult),
                  reads=[dtf["dt"], Aneg], writes=[dtf["adt"]])
            pr.op("pe", lambda e, c=c: e.matmul(ps_acs.ap[:, 0:64], lhsT=Uincl, rhs=dtv["adt"][:, c, :], start=True, stop=True),
                  reads=[dtf["adt"], consts_sb], writes=[ps_acs])
            pr.op("pe", lambda e, c=c: e.matmul(ps_tot.ap[:, 0:64], lhsT=ones_f, rhs=dtv["adt"][:, c, :], start=True, stop=True),
                  reads=[dtf["adt"], consts_sb], writes=[ps_tot])
            pr.op("act", lambda e, c=c: e.activation(out=dtv["expa"][:, c, :], in_=ps_acs.ap[:, 0:64], func=AF.Exp), reads=[ps_acs], writes=[dtf["expa"]])
            pr.op("act", lambda e, c=c: e.activation(out=dtv["dA"][:, c, :], in_=ps_tot.ap[:, 0:64], func=AF.Exp), reads=[ps_tot], writes=[dtf["dA"]])
            pr.op("act", lambda e: e.activation(out=acs_sb.ap, in_=ps_acs.ap[:, 0:64], func=AF.Copy), reads=[ps_acs], writes=[acs_sb])
            pr.op("dve", lambda e: e.tensor_tensor(out=tC.ap, in0=ps_tot.ap[:, 0:64], in1=acs_sb.ap, op=ALU.subtract), reads=[ps_tot, acs_sb], writes=[tC])
            pr.op("act", lambda e: e.activation(out=tC.ap, in_=tC.ap, func=AF.Exp), reads=[tC], writes=[tC])
            pr.op("dve", lambda e, c=c: e.tensor_tensor(out=dtv["w"][:, c, :], in0=dtv["dt"][:, c, :], in1=tC.ap, op=ALU.mult),
                  reads=[dtf["dt"], tC], writes=[dtf["w"]])

        pr.marks.append((f"dtprep_end_ph{ph}", pr.seq))
        for g in range(G):
            sl, W = load_block(f"B{g}")
            ssd_conv_tile(ph, sl, W, 0, 32 + g, BT)
            sl, W = load_block(f"C{g}")
            if main:
                ssd_conv_tile(ph, sl, W, 0, 40 + g, CT)
            else:
                ps = psA[state["psa"]]
                state["psa"] ^= 1
                w3c = sl.ap[:, 0:KC * W].rearrange("p (k c) -> p k c", k=KC)

                def fn(e, ps=ps, w3c=w3c):
                    ins = None
                    for kc in range(KC):
                        ins = e.matmul(ps.ap[:, 0:3], lhsT=w3c[:, kc, 0:128], rhs=xT3[:, kc, NTOK - 3:NTOK], start=(kc == 0), stop=(kc == KC - 1))
                    return ins
                pr.op("pe", fn, reads=[sl, xT_sb], writes=[ps])
                pr.op("act", lambda e, ps=ps, g=g: e.activation(out=hist.ap[:, (40 + g) * 3:(40 + g) * 3 + 3], in_=ps.ap[:, 0:3], func=AF.Copy),
                      reads=[ps], writes=[hist])
            for hb_ in range(2):
                sl, W = load_block(f"xs{g}_{hb_}")
                for j in range(2):
                    ssd_conv_tile(ph, sl, W, j, g * 4 + hb_ * 2 + j, xsT[hb_ * 2 + j])
            zsl = []
            if main:
                for zb in range(2):
                    sl, W = load_block(f"z{g}_{zb}")
                    zsl.append(sl)

            def zstep(c, zsl=zsl):
                for zb in range(2):
                    sl = zsl[zb]
                    w3 = sl.ap[:, 0:KC * 256].rearrange("p (k c) -> p k c", k=KC)
                    ps = psA[state["psa"]]
                    state["psa"] ^= 1

                    def fn(e, w3=w3, ps=ps):
                        ins = None
                        for kc in range(KC):
                            ins = e.matmul(ps.ap[:, 0:256], lhsT=xT3[:, kc, c * 128:(c + 1) * 128], rhs=w3[:, kc, :], start=(kc == 0), stop=(kc == KC - 1))
                        return ins
                    pr.op("pe", fn, reads=[sl, xT_sb], writes=[ps])
                    pr.op("act", lambda e, zb=zb, ps=ps: e.activation(out=szc[c].ap[:, zb * 256:(zb + 1) * 256], in_=ps.ap[:, 0:256], func=AF.Silu),
                          reads=[ps], writes=[szc[c]])
            pr.marks.append((f"inproj_end_ph{ph}_g{g}", pr.seq))
            hs = slice(g * 8, g * 8 + 8)
            ms3 = mixstage.ap.rearrange("p (j t) -> p j t", j=4)

            def bcast(k, c, hs=hs):
                return dtv[k][:, c, hs].unsqueeze(2).to_broadcast([128, 8, 64])

            def head_a(c, g=g, hs=hs, bcast=bcast):
                cs = slice(c * 128, (c + 1) * 128)
                x_tok, B_tok = x_tok2[c % 2], B_tok2[c % 2]
                if main:
                    pr.op("pool", lambda e: e.tensor_tensor(
                        out=hd(rhs2.ap, 1), in0=Uincl.unsqueeze(1).to_broadcast([128, 8, 128]),
                        in1=dtv["adt"][:, c, hs].unsqueeze(2).to_broadcast([128, 8, 128]), op=ALU.mult),
                        reads=[consts_sb, dtf["adt"]], writes=[rhs2])
                    for hh in range(2):
                        pr.op("pe", lambda e, hh=hh: e.matmul(psD[hh].ap, lhsT=Ustrict, rhs=rhs2.ap[:, hh * 512:(hh + 1) * 512], start=True, stop=True),
                              reads=[rhs2, consts_sb], writes=[psD[hh]])
                    pr.op("pe", lambda e: e.matmul(psCB.ap, lhsT=BT.ap[:, cs], rhs=CT.ap[:, cs], start=True, stop=True),
                          reads=[BT, CT], writes=[psCB])

                def fn(e):
                    ins = None
                    for j in range(4):
                        ins = e.matmul(psTPx.ap[:, j * 128:(j + 1) * 128], lhsT=xsT[j].ap[:, cs], rhs=idb.ap, start=True, stop=True)
                    return ins
                pr.op("pe", fn, reads=xsT + [idb], writes=[psTPx])
                pr.op("pe", lambda e: e.matmul(psBt.ap, lhsT=BT.ap[:, cs], rhs=idb.ap, start=True, stop=True), reads=[BT, idb], writes=[psBt])
                pr.op("act", lambda e: e.activation(out=x_tok.ap, in_=psTPx.ap, func=AF.Copy), reads=[psTPx], writes=[x_tok])
                pr.op("act", lambda e: e.activation(out=B_tok.ap, in_=psBt.ap, func=AF.Copy), reads=[psBt], writes=[B_tok])
                if main:
                    for hh in range(2):
                        pr.op("act", lambda e, hh=hh: e.activation(out=Eb.ap[:, hh * 512:(hh + 1) * 512], in_=psD[hh].ap, func=AF.Exp),
                              reads=[psD[hh]], writes=[Eb])

            def head_b(c, g=g, hs=hs, bcast=bcast):
                x_tok, xdt, xdd = x_tok2[c % 2], xdt2[c % 2], xdd2[c % 2]
                xDc = xD2[c % 2]
                if main:
                    pr.op("dve", lambda e: e.tensor_tensor(out=hd(xdt.ap, 1), in0=hd(x_tok.ap, 1), in1=bcast("dt", c), op=ALU.mult),
                          reads=[x_tok, dtf["dt"]], writes=[xdt])
                pr.op("dve", lambda e: e.tensor_tensor(out=hd(xdd.ap, 1), in0=hd(x_tok.ap, 1), in1=bcast("w", c), op=ALU.mult),
                      reads=[x_tok, dtf["w"]], writes=[xdd])
                if main:
                    pr.op("pool", lambda e: e.tensor_tensor(out=hd(xDc.ap, 1), in0=hd(x_tok.ap, 1),
                                                            in1=pp[:, P_DH + hs.start:P_DH + hs.stop].unsqueeze(2).to_broadcast([128, 8, 64]), op=ALU.mult),
                          reads=[x_tok, params_sb], writes=[xDc])
                    pr.op("dve", lambda e: e.tensor_tensor(out=Mcb.ap, in0=psCB.ap, in1=Uincl, op=ALU.mult), reads=[psCB, consts_sb], writes=[Mcb])
                    pr.op("dve", lambda e: e.tensor_tensor(out=hd(Mb.ap, 1), in0=hd(Eb.ap, 1), in1=Mcb.ap.unsqueeze(1).to_broadcast([128, 8, 128]), op=ALU.mult),
                          reads=[Eb, Mcb], writes=[Mb])

            def mid(c, g=g, hs=hs, bcast=bcast):
                cs = slice(c * 128, (c + 1) * 128)
                B_tok, xdt, xdd = B_tok2[c % 2], xdt2[c % 2], xdd2[c % 2]
                pr.op("pe", lambda e: e.matmul(psSn.ap, lhsT=B_tok.ap, rhs=xdd.ap, start=True, stop=True), reads=[B_tok, xdd], writes=[psSn])
                if main:
                    def fn(e):
                        ins = None
                        for h in range(8):
                            ins = e.matmul(psY1.ap[:, h * 64:(h + 1) * 64], lhsT=Mb.ap[:, h * 128:(h + 1) * 128], rhs=xdt.ap[:, h * 64:(h + 1) * 64],
                                           start=True, stop=True)
                        return ins
                    pr.op("pe", fn, reads=[Mb, xdt], writes=[psY1])
                    pr.op("pe", lambda e: e.matmul(psY2.ap, lhsT=CT.ap[:, cs], rhs=S_bf.ap, start=True, stop=True), reads=[CT, S_bf], writes=[psY2])
                pr.op("pool", lambda e: e.tensor_tensor(out=hd(S[g].ap, 1), in0=hd(S[g].ap, 1), in1=bcast("dA", c), op=ALU.mult),
                      reads=[S[g], dtf["dA"]], writes=[S[g]])
                pr.op("dve", lambda e: e.tensor_tensor(out=S[g].ap, in0=psSn.ap, in1=S[g].ap, op=ALU.add), reads=[psSn, S[g]], writes=[S[g]])
                if main and c < NCH - 1:
                    pr.op("act", lambda e: e.activation(out=S_bf.ap, in_=S[g].ap, func=AF.Copy), reads=[S[g]], writes=[S_bf])

            def tail_a(c, g=g, hs=hs, bcast=bcast):
                cs = slice(c * 128, (c + 1) * 128)
                xDc = xD2[c % 2]
                pr.op("dve", lambda e: e.tensor_tensor(out=hd(t1.ap, 1), in0=hd(psY2.ap, 1), in1=bcast("expa", c), op=ALU.mult),
                      reads=[psY2, dtf["expa"]], writes=[t1])
                pr.op("dve", lambda e: e.tensor_tensor(out=t2.ap, in0=psY1.ap, in1=t1.ap, op=ALU.add), reads=[psY1, t1], writes=[t2])
                pr.op("dve", lambda e: e.tensor_tensor(out=t2.ap, in0=t2.ap, in1=xDc.ap, op=ALU.add), reads=[t2, xDc], writes=[t2])
                pr.op("dve", lambda e: e.tensor_tensor(out=t2.ap, in0=t2.ap, in1=szc[c].ap, op=ALU.mult), reads=[t2, szc[c]], writes=[t2])
                pr.op("dve", lambda e: e.memset(ssb.ap[:, 0:1], 0.0), writes=[ssb])
                pr.op("act", lambda e: e.activation(out=t1.ap, in_=t2.ap, func=AF.Square, accum_out=ssb.ap[:, 0:1]), reads=[t2], writes=[t1, ssb])
                pr.op("act", lambda e: e.activation(out=ssb.ap[:, 1:2], in_=ssb.ap[:, 0:1], func=AF.Ln, bias=EPS, scale=1.0 / 512.0), reads=[ssb], writes=[ssb])
                pr.op("act", lambda e: e.activation(out=ssb.ap[:, 2:3], in_=ssb.ap[:, 1:2], func=AF.Exp, scale=-0.5), reads=[ssb], writes=[ssb])
                pr.op("dve", lambda e: e.tensor_scalar_mul(out=ob.ap, in0=t2.ap, scalar1=ssb.ap[:, 2:3]), reads=[t2, ssb], writes=[ob])

            def tail_b(c, g=g, hs=hs, bcast=bcast):
                cs = slice(c * 128, (c + 1) * 128)

                def fn(e):
                    ins = None
                    for j in range(4):
                        ins = e.matmul(psTPo.ap[:, j * 128:(j + 1) * 128], lhsT=ob.ap[:, j * 128:(j + 1) * 128], rhs=idb.ap, start=True, stop=True)
                    return ins
                pr.op("pe", fn, reads=[ob, idb], writes=[psTPo])
                for j in range(4):
                    nw = pp[:, P_NORMW + g * 4 + j:P_NORMW + g * 4 + j + 1]
                    pr.op("act", lambda e, j=j, nw=nw: e.activation(out=ms3[:, j, cs], in_=psTPo.ap[:, j * 128:(j + 1) * 128], func=AF.Identity, scale=nw),
                          reads=[psTPo, params_sb], writes=[mixstage])

            if main:
                zstep(0)
                head_a(0)
                head_b(0)
                for c in range(NCH):
                    mid(c)
                    if c + 1 < NCH:
                        head_a(c + 1)
                    tail_a(c)
                    if c + 1 < NCH:
                        zstep(c + 1)
                    if c == NCH - 2:
                        prefetch(2)
                    tail_b(c)
                    if c + 1 < NCH:
                        head_b(c + 1)
            else:
                prefetch(2)
                for c in range(2):
                    head_a(c)
                    head_b(c)
                for c in range(NCH):
                    mid(c)
                    if c + 2 < NCH:
                        head_a(c + 2)
                        head_b(c + 2)
            if main:
                pr.dma("sp", lambda e, g=g: e.dma_start(out=mixT[g * 4:(g + 1) * 4].rearrange("j p t -> p j t"),
                                                          in_=mixstage.ap.rearrange("p (j t) -> p j t", j=4)), "mixst", reads=[mixstage])
            else:
                pr.op("dve", lambda e, g=g: e.tensor_scalar_mul(out=S[g].ap, in0=S[g].ap, scalar1=pp[:, P_FLAG:P_FLAG + 1]), reads=[S[g], params_sb], writes=[S[g]])
            if main and g < G - 1:
                pr.op("act", lambda e, g=g: e.activation(out=S_bf.ap, in_=S[g + 1].ap, func=AF.Copy), reads=[S[g + 1]], writes=[S_bf])
        if not main:
            pr.op("act", lambda e: e.activation(out=S_bf.ap, in_=S[0].ap, func=AF.Copy), reads=[S[0]], writes=[S_bf])

        pr.marks.append((f"ssd_end_ph{ph}", pr.seq))
        for k in range(16):
            sl, W = load_block(f"lx{k}")
            lw = lws[k % 2]
            pr.dma("pool", lambda e, k=k, lw=lw: e.dma_start(out=lw.ap, in_=lruw[k]), lw.sem, writes=[lw])
            lw4 = lw.ap.rearrange("p (a i j) -> p a i j", a=2, i=2)
            for i in range(2):
                conv_tile(ph, sl, W, i, 48 + 2 * k + i, u[i])
            if main:
                slg_sl, Wg = load_block(f"lg{k}")
                for j in range(2):
                    for tt in range(2):
                        ps = inproj_cm(slg_sl, Wg, j, tt)
                        pr.op("act", lambda e, ps=ps, tt=tt, j=j: e.activation(out=slg2[j].ap[:, tt * 512:(tt + 1) * 512], in_=ps.ap[:, 0:512], func=AF.Silu),
                              reads=[ps], writes=[slg2[j]])
            for i in range(2):
                pr.op("act", lambda e, i=i: e.activation(out=u_bf[i].ap, in_=u[i].ap, func=AF.Copy), reads=[u[i]], writes=[u_bf[i]])
            for j in range(2):
                tl = 2 * k + j
                gr, gi, ab, hb, slg = gr2[j], gi2[j], ab2[j], hb2[j], slg2[j]
                for tt in range(2):
                    ts_ = slice(tt * 512, (tt + 1) * 512)
                    for a_, (psg, dst, bcol) in enumerate(((psD[0], gr, P_LBA), (psTPx, gi, P_LBX))):
                        def fn(e, a_=a_, psg=psg, ts_=ts_, j=j, lw4=lw4):
                            ins = None
                            for i in range(2):
                                ins = e.matmul(psg.ap, lhsT=lw4[:, a_, i, j * 128:(j + 1) * 128], rhs=u_bf[i].ap[:, ts_], start=(i == 0), stop=(i == 1))
                            return ins
                        pr.op("pe", fn, reads=[lw, u_bf[0], u_bf[1]], writes=[psg])
                        pr.op("act", lambda e, psg=psg, dst=dst, bcol=bcol, ts_=ts_, tl=tl: e.activation(
                            out=dst.ap[:, ts_], in_=psg.ap, func=AF.Sigmoid, bias=pp[:, bcol + tl:bcol + tl + 1], scale=1.0),
                            reads=[psg, params_sb], writes=[dst])
                pr.op("act", lambda e, tl=tl, gr=gr, ab=ab: e.activation(out=ab.ap, in_=gr.ap, func=AF.Exp, scale=coef.ap[:, tl:tl + 1]), reads=[gr, coef], writes=[ab])
                pr.op("act", lambda e, tl=tl, gr=gr: e.activation(out=gr.ap, in_=gr.ap, func=AF.Exp, scale=coef2.ap[:, tl:tl + 1]), reads=[gr, coef2], writes=[gr])
                pr.op("act", lambda e, gr=gr: e.activation(out=gr.ap, in_=gr.ap, func=AF.Sqrt, bias=1.0, scale=-1.0), reads=[gr], writes=[gr])
                pr.op("dve", lambda e, j=j, gi=gi: e.tensor_tensor(out=gi.ap, in0=gi.ap, in1=u[j].ap, op=ALU.mult), reads=[gi, u[j]], writes=[gi])
                pr.op("dve", lambda e, gi=gi, gr=gr: e.tensor_tensor(out=gi.ap, in0=gi.ap, in1=gr.ap, op=ALU.mult), reads=[gi, gr], writes=[gi])
                pr.op("dve", lambda e, tl=tl, hb=hb, ab=ab, gi=gi: e.tensor_tensor_scan(out=hb.ap, data0=ab.ap, data1=gi.ap, initial=hstate.ap[:, tl:tl + 1], op0=ALU.mult, op1=ALU.add),
                      reads=[ab, gi, hstate], writes=[hb])
                if main:
                    pr.op("dve", lambda e, hb=hb, slg=slg: e.tensor_tensor(out=mo.ap, in0=hb.ap, in1=slg.ap, op=ALU.mult), reads=[hb, slg], writes=[mo])
                    pr.dma("sp", lambda e, tl=tl: e.dma_start(out=mixT[32 + tl], in_=mo.ap), "mo", reads=[mo])
                else:
                    pr.op("dve", lambda e, tl=tl, hb=hb: e.tensor_scalar_mul(out=hstate.ap[:, tl:tl + 1], in0=hb.ap[:, 1023:1024], scalar1=pp[:, P_FLAG:P_FLAG + 1]),
                          reads=[hb, params_sb], writes=[hstate])

    pr.marks.append(("phaseA_end", pr.seq))
    bar = [(e, pr.cnt[e]) for e in ("pe", "act", "dve", "pool")] + [("d:" + s, n) for s, n in pr.dcnt.items()]
    m3 = mixT_sb.ap.rearrange("p (k t) -> p k t", k=64)
    mparts = [Buf(m3[:, q * 8:(q + 1) * 8, :], f"mixld{q}") for q in range(8)]
    for q in range(8):
        pr.dma("sp", lambda e, q=q: e.dma_start(out=mparts[q].ap, in_=mixT[q * 8:(q + 1) * 8].rearrange("k p t -> p k t")),
               mparts[q].sem, writes=[mparts[q]], extra=bar if q == 0 else ())
    pso_i = 0
    xi = 0
    for cbk in range(16):
        wsl = wo[cbk % 2]
        pr.dma("pool", lambda e, cbk=cbk, wsl=wsl: e.dma_start(out=wsl.ap, in_=wout[cbk]), wsl.sem, writes=[wsl], extra=bar if cbk < 2 else ())
        wo3 = wsl.ap.rearrange("p (k c) -> p k c", k=64)
        for tk in range(8):
            ps = psO[pso_i % 4]
            pso_i += 1
            xb, sb = xr[xi % 2], stg[xi % 2]
            xi += 1
            pr.dma("sp", lambda e, xb=xb, tk=tk, cbk=cbk: e.dma_start(out=xb.ap, in_=xres[tk * 128:(tk + 1) * 128, cbk * 256:(cbk + 1) * 256]),
                   xb.sem, writes=[xb], extra=bar if xi <= 2 else ())

            def fn(e, ps=ps, tk=tk, wo3=wo3):
                ins = None
                for kc in range(64):
                    ins = e.matmul(ps.ap[:, 0:256], lhsT=m3[:, kc, tk * 128:(tk + 1) * 128], rhs=wo3[:, kc, :], start=(kc == 0), stop=(kc == 63))
                return ins
            pr.op("pe", fn, reads=mparts + [wsl], writes=[ps])
            pr.op("dve", lambda e, ps=ps, xb=xb, sb=sb: e.scalar_tensor_tensor(out=sb.ap, in0=xb.ap, scalar=ALPHA, in1=ps.ap[:, 0:256], op0=ALU.mult, op1=ALU.add),
                  reads=[xb, ps], writes=[sb], extra=bar if xi <= 2 else ())
            pr.dma("sp", lambda e, sb=sb, tk=tk, cbk=cbk: e.dma_start(out=pre_scr[tk * 128:(tk + 1) * 128, cbk * 256:(cbk + 1) * 256], in_=sb.ap),
                   sb.sem, reads=[sb])
    pr.marks.append(("outproj_end", pr.seq))
    bar2 = [(e, pr.cnt[e]) for e in ("pe", "dve")] + [("d:" + s, n) for s, n in pr.dcnt.items()]
    pr.dma("sp", lambda e: e.dma_start(out=lngb_sb.ap, in_=lngb[:, :]), "lngb", writes=[lngb_sb], extra=bar2)
    out_toks = []
    for tk in range(8):
        rb, ob_ = lnr[tk % 2], lno[tk % 2]
        pr.dma("sp", lambda e, rb=rb, tk=tk: e.dma_start(out=rb.ap, in_=pre_scr[tk * 128:(tk + 1) * 128, :]), rb.sem, writes=[rb], extra=bar2 if tk < 2 else ())
        st6 = stats.ap[:, 0:48].rearrange("p (c s) -> p c s", c=8)
        for q in range(8):
            pr.op("dve", lambda e, q=q, rb=rb: e.bn_stats(out=st6[:, q, :], in_=rb.ap[:, q * 512:(q + 1) * 512]), reads=[rb], writes=[stats])
        pr.op("dve", lambda e: e.bn_aggr(out=stats.ap[:, 48:50], in_=st6), reads=[stats], writes=[stats])
        pr.op("act", lambda e: e.activation(out=stats.ap[:, 50:51], in_=stats.ap[:, 49:50], func=AF.Sqrt, bias=EPS, scale=1.0), reads=[stats], writes=[stats])
        pr.op("dve", lambda e: e.reciprocal(out=stats.ap[:, 51:52], in_=stats.ap[:, 50:51]), reads=[stats], writes=[stats])
        pr.op("dve", lambda e, rb=rb, ob_=ob_: e.tensor_scalar(out=ob_.ap, in0=rb.ap, scalar1=stats.ap[:, 48:49], scalar2=stats.ap[:, 51:52], op0=ALU.subtract, op1=ALU.mult),
              reads=[rb, stats], writes=[ob_], extra=bar2 if tk < 2 else ())
        pr.op("dve", lambda e, ob_=ob_: e.tensor_tensor(out=ob_.ap, in0=ob_.ap, in1=lngb_sb.ap[:, 0:D], op=ALU.mult), reads=[ob_, lngb_sb], writes=[ob_])
        pr.op("dve", lambda e, ob_=ob_: e.tensor_tensor(out=ob_.ap, in0=ob_.ap, in1=lngb_sb.ap[:, D:2 * D], op=ALU.add), reads=[ob_, lngb_sb], writes=[ob_])
        out_toks.append(pr.dma("sp", lambda e, ob_=ob_, tk=tk: e.dma_start(out=out[tk * 128:(tk + 1) * 128, :], in_=ob_.ap), ob_.sem, reads=[ob_]))

    pr.marks.append(("end", pr.seq))
    nc._marks = pr.marks
    import contextlib
    with contextlib.ExitStack() as es:
        sems = {e: es.enter_context(nc.semaphore("s_" + e)) for e in ("pe", "act", "dve", "pool")}
        for s in pr.dcnt:
            sems["d:" + s] = es.enter_context(nc.semaphore("d_" + s))
        block = es.enter_context(nc.Block())

        def run(engname):
            def body(eng):
                waited = {}
                for fn, waits, tok in pr.ops[engname]:
                    for (sn, val) in waits:
                        if sn == "pe" and engname == "pe":
                            continue
                        if waited.get(sn, 0) >= val:
                            continue
                        waited[sn] = val
                        eng.wait_ge(sems[sn], val)
                    ins = fn(eng)
                    ins.then_inc(sems[tok[0]], 16 if tok[0].startswith("d:") else 1)
                if engname == "sp":
                    fin = [(e_, pr.cnt[e_]) for e_ in ("pe", "act", "dve", "pool")] + [("d:" + s_, n_) for s_, n_ in pr.dcnt.items()]
                    for (sn, val) in fin:
                        if val > 0:
                            eng.wait_ge(sems[sn], val)
            return body
        block.tensor(run("pe"))
        block.scalar(run("act"))
        block.vector(run("dve"))
        block.gpsimd(run("pool"))
        block.sync(run("sp"))
    return nc


_CACHE = {}


def _host_prep(inp):
    f = np.float32
    w_in = np.asarray(inp["w_in"][0], f)
    w3 = w_in.reshape(KC, 128, -1)
    win = np.empty((128, WIN_COLS), f)
    for name, cols in BLOCKS:
        off, W = BLK_OFF[name]
        blk = w3[:, :, cols[0]:cols[-1] + 1]
        win[:, off:off + KC * W] = blk.transpose(1, 0, 2).reshape(128, KC * W)
    w_out = np.asarray(inp["w_out"][0], f)
    wout = np.ascontiguousarray(w_out.reshape(64, 128, 16, 256).transpose(2, 1, 0, 3).reshape(16, 128, 64 * 256))
    wa = np.asarray(inp["lru_wa"][0], f).reshape(16, 2, 128, 256)
    wx = np.asarray(inp["lru_wx"][0], f).reshape(16, 2, 128, 256)
    lruw = np.ascontiguousarray(np.stack([wa, wx], 1).transpose(0, 3, 1, 2, 4).reshape(16, 128, 1024))
    params = np.zeros((128, NPAR), f)
    params[:, P_DTB:P_DTB + 64] = np.asarray(inp["ssd_dt_bias"][0], f)[None, :]
    params[:, P_ALOG:P_ALOG + 64] = np.asarray(inp["ssd_a_log"][0], f)[None, :]
    params[:, P_DH:P_DH + 64] = np.asarray(inp["ssd_d"][0], f)[None, :]
    cw = np.concatenate([np.asarray(inp["ssd_conv_w"][0], f), np.asarray(inp["lru_conv_w"][0], f)], 1)
    cbias = np.concatenate([np.asarray(inp["ssd_conv_b"][0], f), np.asarray(inp["lru_conv_b"][0], f)], 0)
    cp = np.concatenate([cw, cbias[None, :]], 0)
    params[:, P_CONV:P_CONV + 400] = cp.reshape(5, 80, 128).transpose(2, 1, 0).reshape(128, 400)
    params[:, P_NORMW:P_NORMW + 32] = np.asarray(inp["ssd_norm_w"][0], f).reshape(32, 128).T
    params[:, P_LBA:P_LBA + 32] = np.asarray(inp["lru_ba"][0], f).reshape(32, 128).T
    params[:, P_LBX:P_LBX + 32] = np.asarray(inp["lru_bx"][0], f).reshape(32, 128).T
    params[:, P_LAM:P_LAM + 32] = np.asarray(inp["lru_lambda"][0], f).reshape(32, 128).T
    consts = np.zeros((128, 512), f)
    j = np.arange(128)
    consts[:, 0:128] = np.eye(128, dtype=f)
    consts[:, 128:256] = (j[:, None] <= j[None, :]).astype(f)
    consts[:, 256:384] = (j[:, None] > j[None, :]).astype(f)
    consts[:, 384:512] = 1.0
    lngb = np.concatenate([np.broadcast_to(np.asarray(inp["ln_g"][0], f)[None, :], (128, D)),
                           np.broadcast_to(np.asarray(inp["ln_b"][0], f)[None, :], (128, D))], 1)
    lngb = np.ascontiguousarray(lngb)
    x = np.asarray(inp["x"], f)
    in_maps = []
    for core in range(8):
        b, h = core // 2, core % 2
        xm = x[b, h * NTOK:(h + 1) * NTOK]
        xw = x[b, 0:NTOK] if h == 1 else np.zeros((NTOK, D), f)

        def tr(a):
            return a.T.reshape(KC, 128, NTOK).transpose(1, 0, 2).reshape(128, KC * NTOK)
        xT = np.ascontiguousarray(np.stack([tr(xw), tr(xm)], 0))
        p = params.copy()
        p[:, P_FLAG] = float(h)
        in_maps.append({"xT": xT, "xres": np.ascontiguousarray(xm), "win": win, "wout": wout, "lruw": lruw,
                        "params": p, "consts": consts, "lngb": lngb})
    return in_maps


def kernel(**inputs):
    if "nc" not in _CACHE:
        _CACHE["nc"] = build_nc()
    nc = _CACHE["nc"]
    in_maps = _host_prep(inputs)
    res = run_bass_kernel_spmd(nc, in_maps, core_ids=list(range(8)))
    outp = np.empty((4, 2048, D), np.float32)
    for core in range(8):
        b, h = core // 2, core % 2
        outp[b, h * NTOK:(h + 1) * NTOK] = res.results[core]["out"]
    return outp
```

```python
import numpy as np
import ml_dtypes
import concourse.bass as bass
import concourse.mybir as mybir
from concourse.bass_utils import run_bass_kernel_spmd

F32, BF16 = mybir.dt.float32, mybir.dt.bfloat16
AF = mybir.ActivationFunctionType
ALU = mybir.AluOpType
AX = mybir.AxisListType

D = 4096
NTOK = 1024
NCH = 8
KC = 32
G = 8
NLT = 32
ALPHA = 2.0 ** 0.25
EPS = 1e-5
ENGS = ("pe", "act", "dve", "pool", "sp")

P_DTB, P_ALOG, P_DH, P_CONV, P_NORMW, P_LBA, P_LBX, P_LAM, P_FLAG, NPAR = 0, 64, 128, 192, 592, 624, 656, 688, 720, 721

def _blocks():
    blks = [("dt", list(range(10240, 10304)))]
    for g in range(G):
        blks.append((f"B{g}", list(range(8192 + 128 * g, 8192 + 128 * (g + 1)))))
        blks.append((f"C{g}", list(range(9216 + 128 * g, 9216 + 128 * (g + 1)))))
        for hb in range(2):
            blks.append((f"xs{g}_{hb}", list(range(4096 + 512 * g + 256 * hb, 4096 + 512 * g + 256 * (hb + 1)))))
        for hb in range(2):
            blks.append((f"z{g}_{hb}", list(range(512 * g + 256 * hb, 512 * g + 256 * (hb + 1)))))
    for k in range(16):
        blks.append((f"lx{k}", list(range(10304 + 256 * k, 10304 + 256 * (k + 1)))))
        blks.append((f"lg{k}", list(range(14400 + 256 * k, 14400 + 256 * (k + 1)))))
    return blks

BLOCKS = _blocks()
BLK_OFF = {}
_o = 0
for _n, _c in BLOCKS:
    BLK_OFF[_n] = (_o, len(_c))
    _o += KC * len(_c)
WIN_COLS = _o


class Buf:
    __slots__ = ("ap", "w", "r", "sem")

    def __init__(self, ap, sem=None):
        self.ap, self.w, self.r, self.sem = ap, None, [], sem


class Prog:
    def __init__(self):
        self.ops = {e: [] for e in ENGS}
        self.cnt = {e: 0 for e in ENGS}
        self.dcnt = {}
        self.seq = 0
        self.limit = None
        self.marks = []

    def _hz(self, reads, writes, extra):
        waits = [b.w for b in reads if b.w is not None]
        for b in writes:
            waits += b.r
            if b.w is not None:
                waits.append(b.w)
        waits += [t for t in extra if t is not None]
        return waits

    def op(self, eng, fn, reads=(), writes=(), extra=()):
        self.seq += 1
        if self.limit is not None and self.seq > self.limit:
            return None
        waits = self._hz(reads, writes, extra)
        self.cnt[eng] += 1
        tok = (eng, self.cnt[eng])
        self.ops[eng].append((fn, waits, tok))
        for b in reads:
            b.r.append(tok)
        for b in writes:
            b.w, b.r = tok, []
        return tok

    def dma(self, eng, fn, sem, reads=(), writes=(), extra=()):
        self.seq += 1
        if self.limit is not None and self.seq > self.limit:
            return None
        waits = self._hz(reads, writes, extra)
        self.dcnt[sem] = self.dcnt.get(sem, 0) + 16
        tok = ("d:" + sem, self.dcnt[sem])
        self.ops[eng].append((fn, waits, tok))
        for b in reads:
            b.r.append(tok)
        for b in writes:
            b.w, b.r = tok, []
        return tok


def build_nc(limit=None):
    nc = bass.Bass("TRN2", target_bir_lowering=False)
    dram = {}

    def din(name, shape, dt=F32):
        dram[name] = nc.dram_tensor(name, list(shape), dt, kind="ExternalInput").ap()
        return dram[name]

    xT = din("xT", [2, 128, KC * NTOK])
    xres = din("xres", [NTOK, D])
    win = din("win", [128, WIN_COLS])
    wout = din("wout", [16, 128, 64 * 256])
    lruw = din("lruw", [16, 128, 1024])
    params = din("params", [128, NPAR])
    consts = din("consts", [128, 512])
    lngb = din("lngb", [128, 2 * D])
    out = nc.dram_tensor("out", [NTOK, D], F32, kind="ExternalOutput").ap()
    mixT = nc.dram_tensor("mixT_scr", [64, 128, NTOK], BF16, kind="Internal").ap()
    pre_scr = nc.dram_tensor("pre_scr", [NTOK, D], F32, kind="Internal").ap()

    ARENA = 102784
    arena_t = nc.alloc_sbuf_tensor("arena", [128, ARENA], BF16)
    small_t = nc.alloc_sbuf_tensor("small", [128, 1728], F32)
    idb_t = nc.alloc_sbuf_tensor("idb", [128, 128], BF16)
    ps_t = [nc.alloc_psum_tensor(f"ps{i}", [128, 512], F32) for i in range(8)]
    arena = arena_t[:]
    small = small_t[:]

    class Carve:
        def __init__(self, base=0):
            self.o = base

        def bf(self, n):
            a = arena[:, self.o:self.o + n]
            self.o += n
            return a

        def f32(self, n):
            a = arena[:, self.o:self.o + 2 * n].bitcast(F32)
            self.o += 2 * n
            return a

    ca = Carve()
    xT_sb = Buf(ca.bf(KC * NTOK), "xT")
    ws = [Buf(ca.bf(KC * 256), f"ws{i}") for i in range(2)]
    pre = [Buf(ca.f32(1028)) for _ in range(2)]
    acc = [Buf(ca.f32(NTOK)) for _ in range(2)]
    mix_base = ca.o
    xsT = [Buf(ca.bf(NTOK)) for _ in range(4)]
    BT = Buf(ca.bf(NTOK))
    CT = Buf(ca.bf(NTOK))
    szc = [Buf(ca.bf(512)) for _ in range(NCH)]
    mixstage = Buf(ca.bf(4 * NTOK), "mixst")
    rhs2 = Buf(ca.f32(1024))
    Eb = Buf(ca.f32(1024))
    Mb = Buf(ca.bf(1024))
    t1 = Buf(ca.f32(512))
    t2 = Buf(ca.f32(512))
    xD2 = [Buf(ca.f32(512)) for _ in range(2)]
    x_tok2 = [Buf(ca.bf(512)) for _ in range(2)]
    xdt2 = [Buf(ca.bf(512)) for _ in range(2)]
    xdd2 = [Buf(ca.bf(512)) for _ in range(2)]
    ob = Buf(ca.bf(512))
    Mcb = Buf(ca.f32(128))
    B_tok2 = [Buf(ca.bf(128)) for _ in range(2)]
    S_bf = Buf(ca.bf(512))
    ssd_end = ca.o
    cl = Carve(mix_base)
    u = [Buf(cl.f32(NTOK)) for _ in range(2)]
    u_bf = [Buf(cl.bf(NTOK)) for _ in range(2)]
    gr2 = [Buf(cl.f32(NTOK)) for _ in range(2)]
    gi2 = [Buf(cl.f32(NTOK)) for _ in range(2)]
    ab2 = [Buf(cl.f32(NTOK)) for _ in range(2)]
    hb2 = [Buf(cl.f32(NTOK)) for _ in range(2)]
    slg2 = [Buf(cl.f32(NTOK)) for _ in range(2)]
    mo = Buf(cl.bf(NTOK), "mo")
    lru_end = cl.o
    ca.o = max(ssd_end, lru_end)
    dtf = {k: Buf(ca.f32(NCH * 64)) for k in ("dt", "adt", "w", "expa", "dA")}
    tmp64 = [Buf(ca.f32(64)) for _ in range(4)]
    acs_sb = Buf(ca.f32(64))
    lws = [Buf(ca.bf(1024), f"lw{i}") for i in range(2)]
    S = [Buf(ca.f32(512)) for _ in range(G)]
    a_end = ca.o
    assert a_end <= ARENA, a_end
    cb_ = Carve()
    mixT_sb = Buf(cb_.bf(64 * NTOK), "mixld")
    wo = [Buf(cb_.bf(64 * 256), f"wo{i}") for i in range(2)]
    xr = [Buf(cb_.f32(256), f"xr{i}") for i in range(2)]
    stg = [Buf(cb_.f32(256), f"stg{i}") for i in range(2)]
    assert cb_.o <= ARENA, cb_.o
    cc = Carve()
    lnr = [Buf(cc.f32(D), f"lnr{i}") for i in range(2)]
    lno = [Buf(cc.f32(D), f"lno{i}") for i in range(2)]
    lngb_sb = Buf(cc.f32(2 * D), "lngb")
    assert cc.o <= 64 * NTOK

    params_sb = Buf(small[:, 0:NPAR], "params")
    consts_sb = Buf(small[:, 724:724 + 512], "consts")
    so = 724 + 512
    hist = Buf(small[:, so:so + 240]); so += 240
    hstate = Buf(small[:, so:so + 32]); so += 32
    Aneg = Buf(small[:, so:so + 64]); so += 64
    coef = Buf(small[:, so:so + 32]); so += 32
    coef2 = Buf(small[:, so:so + 32]); so += 32
    ssb = Buf(small[:, so:so + 8]); so += 8
    stats = Buf(small[:, so:so + 64]); so += 64
    assert so <= 1728, so
    idb = Buf(idb_t[:])
    pp = params_sb.ap
    ident_f, Uincl, Ustrict, ones_f = (consts_sb.ap[:, i * 128:(i + 1) * 128] for i in range(4))

    psA = [Buf(ps_t[0][:]), Buf(ps_t[1][:])]
    psY1 = Buf(ps_t[4][:])
    psY2 = Buf(ps_t[5][:])
    psSn = Buf(ps_t[6][:])
    psD = [Buf(ps_t[2][:]), psA[1]]
    psTPx = Buf(ps_t[3][:])
    psTPo = psA[0]
    psBt = Buf(ps_t[7][:, 0:128])
    psCB = Buf(ps_t[7][:, 128:256])
    ps64 = [Buf(ps_t[7][:, 256:320])]
    ps_acs = psSn
    ps_tot = psY2
    psO = [Buf(ps_t[i][:]) for i in (2, 3, 4, 5)]

    pr = Prog()
    pr.limit = limit
    hd = lambda a, g8: a.rearrange("p (h q) -> p h q", h=8) if g8 else a

    pr.dma("sp", lambda e: e.dma_start(out=params_sb.ap, in_=params[:, :]), "params", writes=[params_sb])
    pr.dma("sp", lambda e: e.dma_start(out=consts_sb.ap, in_=consts[:, :]), "consts", writes=[consts_sb])
    pr.op("dve", lambda e: e.tensor_copy(out=idb.ap, in_=ident_f), reads=[consts_sb], writes=[idb])
    for b_ in pre:
        pr.op("dve", lambda e, b_=b_: e.memset(b_.ap, 0.0), writes=[b_])
    pr.op("dve", lambda e: e.memset(hstate.ap, 0.0), writes=[hstate])
    pr.op("dve", lambda e: e.memset(hist.ap, 0.0), writes=[hist])
    for g in range(G):
        pr.op("dve", lambda e, g=g: e.memset(S[g].ap, 0.0), writes=[S[g]])
    pr.op("act", lambda e: e.activation(out=Aneg.ap, in_=pp[:, P_ALOG:P_ALOG + 64], func=AF.Exp), reads=[params_sb], writes=[Aneg])
    pr.op("dve", lambda e: e.tensor_scalar_mul(out=Aneg.ap, in0=Aneg.ap, scalar1=-1.0), reads=[Aneg], writes=[Aneg])
    pr.op("act", lambda e: e.activation(out=coef.ap, in_=pp[:, P_LAM:P_LAM + 32], func=AF.Exp, scale=-1.0), reads=[params_sb], writes=[coef])
    pr.op("act", lambda e: e.activation(out=coef.ap, in_=coef.ap, func=AF.Ln, bias=1.0, scale=1.0), reads=[coef], writes=[coef])
    pr.op("dve", lambda e: e.tensor_scalar_mul(out=coef2.ap, in0=coef.ap, scalar1=-16.0), reads=[coef], writes=[coef2])
    pr.op("dve", lambda e: e.tensor_scalar_mul(out=coef.ap, in0=coef.ap, scalar1=-8.0), reads=[coef, coef2], writes=[coef])

    pr.marks.append(("setup_end", pr.seq))
    xT3 = xT_sb.ap.rearrange("p (k t) -> p k t", k=KC)
    state = {"slot": 0, "psa": 0, "pa": 0}

    bq = {"order": [], "next": 0, "loaded": {}}

    def prefetch(n):
        for _ in range(n):
            if bq["next"] >= len(bq["order"]):
                return
            name = bq["order"][bq["next"]]
            bq["next"] += 1
            off, W = BLK_OFF[name]
            sl = ws[state["slot"]]
            state["slot"] ^= 1
            pr.dma("pool", lambda e, sl=sl, off=off, W=W: e.dma_start(out=sl.ap[:, 0:KC * W], in_=win[:, off:off + KC * W]),
                   sl.sem, writes=[sl])
            bq["loaded"][name] = (sl, W)

    def load_block(name):
        while name not in bq["loaded"]:
            prefetch(1)
        return bq["loaded"].pop(name)

    def inproj_cm(sl, W, j, tt):
        ps = psA[state["psa"]]
        state["psa"] ^= 1
        w3 = sl.ap[:, 0:KC * W].rearrange("p (k c) -> p k c", k=KC)

        def fn(e, ps=ps, w3=w3, j=j, tt=tt):
            ins = None
            for kc in range(KC):
                ins = e.matmul(ps.ap[:, 0:512], lhsT=w3[:, kc, j * 128:(j + 1) * 128], rhs=xT3[:, kc, tt * 512:(tt + 1) * 512],
                               start=(kc == 0), stop=(kc == KC - 1))
            return ins
        pr.op("pe", fn, reads=[sl, xT_sb], writes=[ps])
        return ps

    def conv_tile(ph, sl, W, j, tile_id, accb):
        pb = pre[state["pa"]]
        state["pa"] ^= 1
        if ph == 1:
            pr.op("dve", lambda e: e.tensor_copy(out=pb.ap[:, 0:3], in_=hist.ap[:, tile_id * 3:tile_id * 3 + 3]), reads=[hist], writes=[pb])
        cw = pp[:, P_CONV + tile_id * 5:P_CONV + tile_id * 5 + 5]
        for tt in range(2):
            ps = inproj_cm(sl, W, j, tt)
            pr.op("act", lambda e, ps=ps, tt=tt: e.activation(out=pb.ap[:, 3 + tt * 512:3 + (tt + 1) * 512], in_=ps.ap[:, 0:512], func=AF.Copy),
                  reads=[ps], writes=[pb])
            o_ = tt * 512
            pr.op("dve", lambda e, o_=o_: e.tensor_scalar(out=accb.ap[:, o_:o_ + 512], in0=pb.ap[:, 3 + o_:3 + o_ + 512], scalar1=cw[:, 3:4], scalar2=cw[:, 4:5],
                                                        op0=ALU.mult, op1=ALU.add), reads=[pb, params_sb], writes=[accb])
            for k in (2, 1, 0):
                pr.op("dve", lambda e, k=k, o_=o_: e.scalar_tensor_tensor(out=accb.ap[:, o_:o_ + 512], in0=pb.ap[:, k + o_:k + o_ + 512], scalar=cw[:, k:k + 1],
                                                                         in1=accb.ap[:, o_:o_ + 512], op0=ALU.mult, op1=ALU.add),
                      reads=[pb, accb], writes=[accb])
        if ph == 0:
            pr.op("dve", lambda e: e.tensor_copy(out=hist.ap[:, tile_id * 3:tile_id * 3 + 3], in_=pb.ap[:, 1024:1027]), reads=[pb], writes=[hist])

    def ssd_conv_tile(ph, sl, W, j, tile_id, dst):
        accb = acc[state["pa"]]
        conv_tile(ph, sl, W, j, tile_id, accb)
        pr.op("act", lambda e: e.activation(out=dst.ap, in_=accb.ap, func=AF.Silu), reads=[accb], writes=[dst])

    for ph in range(2):
        main = ph == 1
        order = ["dt"]
        for g_ in range(G):
            order += ([f"B{g_}", f"C{g_}", f"xs{g_}_0", f"xs{g_}_1", f"z{g_}_0", f"z{g_}_1"] if main
                      else [f"B{g_}", f"xs{g_}_0", f"xs{g_}_1", f"C{g_}"])
        for k_ in range(16):
            order += [f"lx{k_}"] + ([f"lg{k_}"] if main else [])
        bq["order"], bq["next"], bq["loaded"] = order, 0, {}
        pr.dma("pool", lambda e, ph=ph: e.dma_start(out=xT_sb.ap, in_=xT[ph]), "xT", writes=[xT_sb])
        sl, W = load_block("dt")
        w3 = sl.ap[:, 0:KC * 64].rearrange("p (k c) -> p k c", k=KC)
        dtv = {k: v.ap.rearrange("p (c h) -> p c h", c=NCH) for k, v in dtf.items()}
        for c in range(NCH):
            dps = ps64[0]

            def fn(e, c=c, w3=w3, dps=dps):
                ins = None
                for kc in range(KC):
                    ins = e.matmul(dps.ap, lhsT=xT3[:, kc, c * 128:(c + 1) * 128], rhs=w3[:, kc, :], start=(kc == 0), stop=(kc == KC - 1))
                return ins
            pr.op("pe", fn, reads=[sl, xT_sb], writes=[dps])
            tA, tB, tC, tD = tmp64
            pr.op("dve", lambda e: e.tensor_tensor(out=tA.ap, in0=dps.ap, in1=pp[:, P_DTB:P_DTB + 64], op=ALU.add), reads=[dps, params_sb], writes=[tA])
            pr.op("dve", lambda e: e.tensor_scalar_mul(out=tB.ap, in0=tA.ap, scalar1=-1.0), reads=[tA], writes=[tB])
            pr.op("dve", lambda e: e.tensor_tensor(out=tB.ap, in0=tB.ap, in1=tA.ap, op=ALU.max), reads=[tA, tB], writes=[tB])
            pr.op("act", lambda e: e.activation(out=tB.ap, in_=tB.ap, func=AF.Exp, scale=-1.0), reads=[tB], writes=[tB])
            pr.op("act", lambda e: e.activation(out=tB.ap, in_=tB.ap, func=AF.Ln, bias=1.0, scale=1.0), reads=[tB], writes=[tB])
            pr.op("dve", lambda e, c=c: e.scalar_tensor_tensor(out=dtv["dt"][:, c, :], in0=tA.ap, scalar=0.0, in1=tB.ap, op0=ALU.max, op1=ALU.add),
                  reads=[tA, tB], writes=[dtf["dt"]])
            pr.op("dve", lambda e, c=c: e.tensor_tensor(out=dtv["adt"][:, c, :], in0=dtv["dt"][:, c, :], in1=Aneg.ap, op=ALU.mult),
                  reads=[dtf["dt"], Aneg], writes=[dtf["adt"]])
            pr.op("pe", lambda e, c=c: e.matmul(ps_acs.ap[:, 0:64], lhsT=Uincl, rhs=dtv["adt"][:, c, :], start=True, stop=True),
                  reads=[dtf["adt"], consts_sb], writes=[ps_acs])
            pr.op("pe", lambda e, c=c: e.matmul(ps_tot.ap[:, 0:64], lhsT=ones_f, rhs=dtv["adt"][:, c, :], start=True, stop=True),
                  reads=[dtf["adt"], consts_sb], writes=[ps_tot])
            pr.op("act", lambda e, c=c: e.activation(out=dtv["expa"][:, c, :], in_=ps_acs.ap[:, 0:64], func=AF.Exp), reads=[ps_acs], writes=[dtf["expa"]])
            pr.op("act", lambda e, c=c: e.activation(out=dtv["dA"][:, c, :], in_=ps_tot.ap[:, 0:64], func=AF.Exp), reads=[ps_tot], writes=[dtf["dA"]])
            pr.op("act", lambda e: e.activation(out=acs_sb.ap, in_=ps_acs.ap[:, 0:64], func=AF.Copy), reads=[ps_acs], writes=[acs_sb])
            pr.op("dve", lambda e: e.tensor_tensor(out=tC.ap, in0=ps_tot.ap[:, 0:64], in1=acs_sb.ap, op=ALU.subtract), reads=[ps_tot, acs_sb], writes=[tC])
            pr.op("act", lambda e: e.activation(out=tC.ap, in_=tC.ap, func=AF.Exp), reads=[tC], writes=[tC])
            pr.op("dve", lambda e, c=c: e.tensor_tensor(out=dtv["w"][:, c, :], in0=dtv["dt"][:, c, :], in1=tC.ap, op=ALU.mult),
                  reads=[dtf["dt"], tC], writes=[dtf["w"]])

        pr.marks.append((f"dtprep_end_ph{ph}", pr.seq))
        for g in range(G):
            sl, W = load_block(f"B{g}")
            ssd_conv_tile(ph, sl, W, 0, 32 + g, BT)
            def c_block(g=g):
                sl, W = load_block(f"C{g}")
                if main:
                    ssd_conv_tile(ph, sl, W, 0, 40 + g, CT)
                else:
                    ps = psA[state["psa"]]
                    state["psa"] ^= 1
                    w3c = sl.ap[:, 0:KC * W].rearrange("p (k c) -> p k c", k=KC)

                    def fn(e, ps=ps, w3c=w3c):
                        ins = None
                        for kc in range(KC):
                            ins = e.matmul(ps.ap[:, 0:3], lhsT=w3c[:, kc, 0:128], rhs=xT3[:, kc, NTOK - 3:NTOK], start=(kc == 0), stop=(kc == KC - 1))
                        return ins
                    pr.op("pe", fn, reads=[sl, xT_sb], writes=[ps])
                    pr.op("act", lambda e, ps=ps, g=g: e.activation(out=hist.ap[:, (40 + g) * 3:(40 + g) * 3 + 3], in_=ps.ap[:, 0:3], func=AF.Copy),
                          reads=[ps], writes=[hist])
            if main:
                c_block()
            for hb_ in range(2):
                sl, W = load_block(f"xs{g}_{hb_}")
                for j in range(2):
                    ssd_conv_tile(ph, sl, W, j, g * 4 + hb_ * 2 + j, xsT[hb_ * 2 + j])
            if not main:
                c_block()
            zsl = []
            if main:
                for zb in range(2):
                    sl, W = load_block(f"z{g}_{zb}")
                    zsl.append(sl)

            def zstep(c, zsl=zsl):
                for zb in range(2):
                    sl = zsl[zb]
                    w3 = sl.ap[:, 0:KC * 256].rearrange("p (k c) -> p k c", k=KC)
                    ps = psA[state["psa"]]
                    state["psa"] ^= 1

                    def fn(e, w3=w3, ps=ps):
                        ins = None
                        for kc in range(KC):
                            ins = e.matmul(ps.ap[:, 0:256], lhsT=xT3[:, kc, c * 128:(c + 1) * 128], rhs=w3[:, kc, :], start=(kc == 0), stop=(kc == KC - 1))
                        return ins
                    pr.op("pe", fn, reads=[sl, xT_sb], writes=[ps])
                    pr.op("act", lambda e, zb=zb, ps=ps: e.activation(out=szc[c].ap[:, zb * 256:(zb + 1) * 256], in_=ps.ap[:, 0:256], func=AF.Silu),
                          reads=[ps], writes=[szc[c]])
            pr.marks.append((f"inproj_end_ph{ph}_g{g}", pr.seq))
            hs = slice(g * 8, g * 8 + 8)
            ms3 = mixstage.ap.rearrange("p (j t) -> p j t", j=4)

            def bcast(k, c, hs=hs):
                return dtv[k][:, c, hs].unsqueeze(2).to_broadcast([128, 8, 64])

            def head_a(c, g=g, hs=hs, bcast=bcast):
                cs = slice(c * 128, (c + 1) * 128)
                x_tok, B_tok = x_tok2[c % 2], B_tok2[c % 2]
                if main:
                    pr.op("pool", lambda e: e.tensor_tensor(
                        out=hd(rhs2.ap, 1), in0=Uincl.unsqueeze(1).to_broadcast([128, 8, 128]),
                        in1=dtv["adt"][:, c, hs].unsqueeze(2).to_broadcast([128, 8, 128]), op=ALU.mult),
                        reads=[consts_sb, dtf["adt"]], writes=[rhs2])
                    for hh in range(2):
                        pr.op("pe", lambda e, hh=hh: e.matmul(psD[hh].ap, lhsT=Ustrict, rhs=rhs2.ap[:, hh * 512:(hh + 1) * 512], start=True, stop=True),
                              reads=[rhs2, consts_sb], writes=[psD[hh]])
                    pr.op("pe", lambda e: e.matmul(psCB.ap, lhsT=BT.ap[:, cs], rhs=CT.ap[:, cs], start=True, stop=True),
                          reads=[BT, CT], writes=[psCB])

                def fn(e):
                    ins = None
                    for j in range(4):
                        ins = e.matmul(psTPx.ap[:, j * 128:(j + 1) * 128], lhsT=xsT[j].ap[:, cs], rhs=idb.ap, start=True, stop=True)
                    return ins
                pr.op("pe", fn, reads=xsT + [idb], writes=[psTPx])
                pr.op("pe", lambda e: e.matmul(psBt.ap, lhsT=BT.ap[:, cs], rhs=idb.ap, start=True, stop=True), reads=[BT, idb], writes=[psBt])
                pr.op("act", lambda e: e.activation(out=x_tok.ap, in_=psTPx.ap, func=AF.Copy), reads=[psTPx], writes=[x_tok])
                pr.op("act", lambda e: e.activation(out=B_tok.ap, in_=psBt.ap, func=AF.Copy), reads=[psBt], writes=[B_tok])
                if main:
                    for hh in range(2):
                        pr.op("act", lambda e, hh=hh: e.activation(out=Eb.ap[:, hh * 512:(hh + 1) * 512], in_=psD[hh].ap, func=AF.Exp),
                              reads=[psD[hh]], writes=[Eb])

            def head_b(c, g=g, hs=hs, bcast=bcast):
                x_tok, xdt, xdd = x_tok2[c % 2], xdt2[c % 2], xdd2[c % 2]
                xDc = xD2[c % 2]
                if main:
                    pr.op("dve", lambda e: e.tensor_tensor(out=hd(xdt.ap, 1), in0=hd(x_tok.ap, 1), in1=bcast("dt", c), op=ALU.mult),
                          reads=[x_tok, dtf["dt"]], writes=[xdt])
                pr.op("dve", lambda e: e.tensor_tensor(out=hd(xdd.ap, 1), in0=hd(x_tok.ap, 1), in1=bcast("w", c), op=ALU.mult),
                      reads=[x_tok, dtf["w"]], writes=[xdd])
                if main:
                    pr.op("pool", lambda e: e.tensor_tensor(out=hd(xDc.ap, 1), in0=hd(x_tok.ap, 1),
                                                            in1=pp[:, P_DH + hs.start:P_DH + hs.stop].unsqueeze(2).to_broadcast([128, 8, 64]), op=ALU.mult),
                          reads=[x_tok, params_sb], writes=[xDc])
                    pr.op("dve", lambda e: e.tensor_tensor(out=Mcb.ap, in0=psCB.ap, in1=Uincl, op=ALU.mult), reads=[psCB, consts_sb], writes=[Mcb])
                    pr.op("dve", lambda e: e.tensor_tensor(out=hd(Mb.ap, 1), in0=hd(Eb.ap, 1), in1=Mcb.ap.unsqueeze(1).to_broadcast([128, 8, 128]), op=ALU.mult),
                          reads=[Eb, Mcb], writes=[Mb])

            def mid(c, g=g, hs=hs, bcast=bcast):
                cs = slice(c * 128, (c + 1) * 128)
                B_tok, xdt, xdd = B_tok2[c % 2], xdt2[c % 2], xdd2[c % 2]
                pr.op("pe", lambda e: e.matmul(psSn.ap, lhsT=B_tok.ap, rhs=xdd.ap, start=True, stop=True), reads=[B_tok, xdd], writes=[psSn])
                if main:
                    def fn(e):
                        ins = None
                        for h in range(8):
                            ins = e.matmul(psY1.ap[:, h * 64:(h + 1) * 64], lhsT=Mb.ap[:, h * 128:(h + 1) * 128], rhs=xdt.ap[:, h * 64:(h + 1) * 64],
                                           start=True, stop=True)
                        return ins
                    pr.op("pe", fn, reads=[Mb, xdt], writes=[psY1])
                    pr.op("pe", lambda e: e.matmul(psY2.ap, lhsT=CT.ap[:, cs], rhs=S_bf.ap, start=True, stop=True), reads=[CT, S_bf], writes=[psY2])
                pr.op("pool", lambda e: e.tensor_tensor(out=hd(S[g].ap, 1), in0=hd(S[g].ap, 1), in1=bcast("dA", c), op=ALU.mult),
                      reads=[S[g], dtf["dA"]], writes=[S[g]])
                pr.op("dve", lambda e: e.tensor_tensor(out=S[g].ap, in0=psSn.ap, in1=S[g].ap, op=ALU.add), reads=[psSn, S[g]], writes=[S[g]])
                if main and c < NCH - 1:
                    pr.op("act", lambda e: e.activation(out=S_bf.ap, in_=S[g].ap, func=AF.Copy), reads=[S[g]], writes=[S_bf])

            def tail_a(c, g=g, hs=hs, bcast=bcast):
                cs = slice(c * 128, (c + 1) * 128)
                xDc = xD2[c % 2]
                pr.op("dve", lambda e: e.tensor_tensor(out=hd(t1.ap, 1), in0=hd(psY2.ap, 1), in1=bcast("expa", c), op=ALU.mult),
                      reads=[psY2, dtf["expa"]], writes=[t1])
                pr.op("dve", lambda e: e.tensor_tensor(out=t2.ap, in0=psY1.ap, in1=t1.ap, op=ALU.add), reads=[psY1, t1], writes=[t2])
                pr.op("dve", lambda e: e.tensor_tensor(out=t2.ap, in0=t2.ap, in1=xDc.ap, op=ALU.add), reads=[t2, xDc], writes=[t2])
                pr.op("dve", lambda e: e.tensor_tensor(out=t2.ap, in0=t2.ap, in1=szc[c].ap, op=ALU.mult), reads=[t2, szc[c]], writes=[t2])
                pr.op("dve", lambda e: e.memset(ssb.ap[:, 0:1], 0.0), writes=[ssb])
                pr.op("act", lambda e: e.activation(out=t1.ap, in_=t2.ap, func=AF.Square, accum_out=ssb.ap[:, 0:1]), reads=[t2], writes=[t1, ssb])
                pr.op("act", lambda e: e.activation(out=ssb.ap[:, 1:2], in_=ssb.ap[:, 0:1], func=AF.Ln, bias=EPS, scale=1.0 / 512.0), reads=[ssb], writes=[ssb])
                pr.op("act", lambda e: e.activation(out=ssb.ap[:, 2:3], in_=ssb.ap[:, 1:2], func=AF.Exp, scale=-0.5), reads=[ssb], writes=[ssb])
                pr.op("dve", lambda e: e.tensor_scalar_mul(out=ob.ap, in0=t2.ap, scalar1=ssb.ap[:, 2:3]), reads=[t2, ssb], writes=[ob])

            def tail_b(c, g=g, hs=hs, bcast=bcast):
                cs = slice(c * 128, (c + 1) * 128)

                def fn(e):
                    ins = None
                    for j in range(4):
                        ins = e.matmul(psTPo.ap[:, j * 128:(j + 1) * 128], lhsT=ob.ap[:, j * 128:(j + 1) * 128], rhs=idb.ap, start=True, stop=True)
                    return ins
                pr.op("pe", fn, reads=[ob, idb], writes=[psTPo])
                for j in range(4):
                    nw = pp[:, P_NORMW + g * 4 + j:P_NORMW + g * 4 + j + 1]
                    pr.op("act", lambda e, j=j, nw=nw: e.activation(out=ms3[:, j, cs], in_=psTPo.ap[:, j * 128:(j + 1) * 128], func=AF.Identity, scale=nw),
                          reads=[psTPo, params_sb], writes=[mixstage])

            if main:
                zstep(0)
                head_a(0)
                head_b(0)
                for c in range(NCH):
                    mid(c)
                    if c + 1 < NCH:
                        head_a(c + 1)
                    tail_a(c)
                    if c + 1 < NCH:
                        zstep(c + 1)
                    if c == NCH - 2:
                        prefetch(2)
                    tail_b(c)
                    if c + 1 < NCH:
                        head_b(c + 1)
            else:
                prefetch(2)
                for c in range(2):
                    head_a(c)
                    head_b(c)
                for c in range(NCH):
                    mid(c)
                    if c + 2 < NCH:
                        head_a(c + 2)
                        head_b(c + 2)
            if main:
                pr.dma("sp", lambda e, g=g: e.dma_start(out=mixT[g * 4:(g + 1) * 4].rearrange("j p t -> p j t"),
                                                          in_=mixstage.ap.rearrange("p (j t) -> p j t", j=4)), "mixst", reads=[mixstage])
            else:
                pr.op("dve", lambda e, g=g: e.tensor_scalar_mul(out=S[g].ap, in0=S[g].ap, scalar1=pp[:, P_FLAG:P_FLAG + 1]), reads=[S[g], params_sb], writes=[S[g]])
            if main and g < G - 1:
                pr.op("act", lambda e, g=g: e.activation(out=S_bf.ap, in_=S[g + 1].ap, func=AF.Copy), reads=[S[g + 1]], writes=[S_bf])
        if not main:
            pr.op("act", lambda e: e.activation(out=S_bf.ap, in_=S[0].ap, func=AF.Copy), reads=[S[0]], writes=[S_bf])

        pr.marks.append((f"ssd_end_ph{ph}", pr.seq))
        for k in range(16):
            sl, W = load_block(f"lx{k}")
            lw = lws[k % 2]
            pr.dma("pool", lambda e, k=k, lw=lw: e.dma_start(out=lw.ap, in_=lruw[k]), lw.sem, writes=[lw])
            lw4 = lw.ap.rearrange("p (a i j) -> p a i j", a=2, i=2)
            for i in range(2):
                conv_tile(ph, sl, W, i, 48 + 2 * k + i, u[i])
            if main:
                slg_sl, Wg = load_block(f"lg{k}")
                for j in range(2):
                    for tt in range(2):
                        ps = inproj_cm(slg_sl, Wg, j, tt)
                        pr.op("act", lambda e, ps=ps, tt=tt, j=j: e.activation(out=slg2[j].ap[:, tt * 512:(tt + 1) * 512], in_=ps.ap[:, 0:512], func=AF.Silu),
                              reads=[ps], writes=[slg2[j]])
            for i in range(2):
                pr.op("act", lambda e, i=i: e.activation(out=u_bf[i].ap, in_=u[i].ap, func=AF.Copy), reads=[u[i]], writes=[u_bf[i]])
            for j in range(2):
                tl = 2 * k + j
                gr, gi, ab, hb, slg = gr2[j], gi2[j], ab2[j], hb2[j], slg2[j]
                for tt in range(2):
                    ts_ = slice(tt * 512, (tt + 1) * 512)
                    for a_, (psg, dst, bcol) in enumerate(((psD[0], gr, P_LBA), (psTPx, gi, P_LBX))):
                        def fn(e, a_=a_, psg=psg, ts_=ts_, j=j, lw4=lw4):
                            ins = None
                            for i in range(2):
                                ins = e.matmul(psg.ap, lhsT=lw4[:, a_, i, j * 128:(j + 1) * 128], rhs=u_bf[i].ap[:, ts_], start=(i == 0), stop=(i == 1))
                            return ins
                        pr.op("pe", fn, reads=[lw, u_bf[0], u_bf[1]], writes=[psg])
                        pr.op("act", lambda e, psg=psg, dst=dst, bcol=bcol, ts_=ts_, tl=tl: e.activation(
                            out=dst.ap[:, ts_], in_=psg.ap, func=AF.Sigmoid, bias=pp[:, bcol + tl:bcol + tl + 1], scale=1.0),
                            reads=[psg, params_sb], writes=[dst])
                pr.op("act", lambda e, tl=tl, gr=gr, ab=ab: e.activation(out=ab.ap, in_=gr.ap, func=AF.Exp, scale=coef.ap[:, tl:tl + 1]), reads=[gr, coef], writes=[ab])
                pr.op("act", lambda e, tl=tl, gr=gr: e.activation(out=gr.ap, in_=gr.ap, func=AF.Exp, scale=coef2.ap[:, tl:tl + 1]), reads=[gr, coef2], writes=[gr])
                pr.op("act", lambda e, gr=gr: e.activation(out=gr.ap, in_=gr.ap, func=AF.Sqrt, bias=1.0, scale=-1.0), reads=[gr], writes=[gr])
                pr.op("dve", lambda e, j=j, gi=gi: e.tensor_tensor(out=gi.ap, in0=gi.ap, in1=u[j].ap, op=ALU.mult), reads=[gi, u[j]], writes=[gi])
                pr.op("dve", lambda e, gi=gi, gr=gr: e.tensor_tensor(out=gi.ap, in0=gi.ap, in1=gr.ap, op=ALU.mult), reads=[gi, gr], writes=[gi])
                pr.op("dve", lambda e, tl=tl, hb=hb, ab=ab, gi=gi: e.tensor_tensor_scan(out=hb.ap, data0=ab.ap, data1=gi.ap, initial=hstate.ap[:, tl:tl + 1], op0=ALU.mult, op1=ALU.add),
                      reads=[ab, gi, hstate], writes=[hb])
                if main:
                    pr.op("dve", lambda e, hb=hb, slg=slg: e.tensor_tensor(out=mo.ap, in0=hb.ap, in1=slg.ap, op=ALU.mult), reads=[hb, slg], writes=[mo])
                    pr.dma("sp", lambda e, tl=tl: e.dma_start(out=mixT[32 + tl], in_=mo.ap), "mo", reads=[mo])
                else:
                    pr.op("dve", lambda e, tl=tl, hb=hb: e.tensor_scalar_mul(out=hstate.ap[:, tl:tl + 1], in0=hb.ap[:, 1023:1024], scalar1=pp[:, P_FLAG:P_FLAG + 1]),
                          reads=[hb, params_sb], writes=[hstate])

    pr.marks.append(("phaseA_end", pr.seq))
    bar = [(e, pr.cnt[e]) for e in ("pe", "act", "dve", "pool")] + [("d:" + s, n) for s, n in pr.dcnt.items()]
    m3 = mixT_sb.ap.rearrange("p (k t) -> p k t", k=64)
    mparts = [Buf(m3[:, q * 8:(q + 1) * 8, :], f"mixld{q}") for q in range(8)]
    for q in range(8):
        pr.dma("sp", lambda e, q=q: e.dma_start(out=mparts[q].ap, in_=mixT[q * 8:(q + 1) * 8].rearrange("k p t -> p k t")),
               mparts[q].sem, writes=[mparts[q]], extra=bar if q == 0 else ())
    pso_i = 0
    xi = 0
    for cbk in range(16):
        wsl = wo[cbk % 2]
        pr.dma("pool", lambda e, cbk=cbk, wsl=wsl: e.dma_start(out=wsl.ap, in_=wout[cbk]), wsl.sem, writes=[wsl], extra=bar if cbk < 2 else ())
        wo3 = wsl.ap.rearrange("p (k c) -> p k c", k=64)
        for tk in range(8):
            ps = psO[pso_i % 4]
            pso_i += 1
            xb, sb = xr[xi % 2], stg[xi % 2]
            xi += 1
            pr.dma("sp", lambda e, xb=xb, tk=tk, cbk=cbk: e.dma_start(out=xb.ap, in_=xres[tk * 128:(tk + 1) * 128, cbk * 256:(cbk + 1) * 256]),
                   xb.sem, writes=[xb], extra=bar if xi <= 2 else ())

            def fn(e, ps=ps, tk=tk, wo3=wo3):
                ins = None
                for kc in range(64):
                    ins = e.matmul(ps.ap[:, 0:256], lhsT=m3[:, kc, tk * 128:(tk + 1) * 128], rhs=wo3[:, kc, :], start=(kc == 0), stop=(kc == 63))
                return ins
            pr.op("pe", fn, reads=mparts + [wsl], writes=[ps])
            pr.op("dve", lambda e, ps=ps, xb=xb, sb=sb: e.scalar_tensor_tensor(out=sb.ap, in0=xb.ap, scalar=ALPHA, in1=ps.ap[:, 0:256], op0=ALU.mult, op1=ALU.add),
                  reads=[xb, ps], writes=[sb], extra=bar if xi <= 2 else ())
            pr.dma("sp", lambda e, sb=sb, tk=tk, cbk=cbk: e.dma_start(out=pre_scr[tk * 128:(tk + 1) * 128, cbk * 256:(cbk + 1) * 256], in_=sb.ap),
                   sb.sem, reads=[sb])
    pr.marks.append(("outproj_end", pr.seq))
    bar2 = [(e, pr.cnt[e]) for e in ("pe", "dve")] + [("d:" + s, n) for s, n in pr.dcnt.items()]
    pr.dma("sp", lambda e: e.dma_start(out=lngb_sb.ap, in_=lngb[:, :]), "lngb", writes=[lngb_sb], extra=bar2)
    out_toks = []
    for tk in range(8):
        rb, ob_ = lnr[tk % 2], lno[tk % 2]
        pr.dma("sp", lambda e, rb=rb, tk=tk: e.dma_start(out=rb.ap, in_=pre_scr[tk * 128:(tk + 1) * 128, :]), rb.sem, writes=[rb], extra=bar2 if tk < 2 else ())
        st6 = stats.ap[:, 0:48].rearrange("p (c s) -> p c s", c=8)
        for q in range(8):
            pr.op("dve", lambda e, q=q, rb=rb: e.bn_stats(out=st6[:, q, :], in_=rb.ap[:, q * 512:(q + 1) * 512]), reads=[rb], writes=[stats])
        pr.op("dve", lambda e: e.bn_aggr(out=stats.ap[:, 48:50], in_=st6), reads=[stats], writes=[stats])
        pr.op("act", lambda e: e.activation(out=stats.ap[:, 50:51], in_=stats.ap[:, 49:50], func=AF.Sqrt, bias=EPS, scale=1.0), reads=[stats], writes=[stats])
        pr.op("dve", lambda e: e.reciprocal(out=stats.ap[:, 51:52], in_=stats.ap[:, 50:51]), reads=[stats], writes=[stats])
        pr.op("dve", lambda e, rb=rb, ob_=ob_: e.tensor_scalar(out=ob_.ap, in0=rb.ap, scalar1=stats.ap[:, 48:49], scalar2=stats.ap[:, 51:52], op0=ALU.subtract, op1=ALU.mult),
              reads=[rb, stats], writes=[ob_], extra=bar2 if tk < 2 else ())
        pr.op("dve", lambda e, ob_=ob_: e.tensor_tensor(out=ob_.ap, in0=ob_.ap, in1=lngb_sb.ap[:, 0:D], op=ALU.mult), reads=[ob_, lngb_sb], writes=[ob_])
        pr.op("dve", lambda e, ob_=ob_: e.tensor_tensor(out=ob_.ap, in0=ob_.ap, in1=lngb_sb.ap[:, D:2 * D], op=ALU.add), reads=[ob_, lngb_sb], writes=[ob_])
        out_toks.append(pr.dma("sp", lambda e, ob_=ob_, tk=tk: e.dma_start(out=out[tk * 128:(tk + 1) * 128, :], in_=ob_.ap), ob_.sem, reads=[ob_]))

    pr.marks.append(("end", pr.seq))
    nc._marks = pr.marks
    import contextlib
    with contextlib.ExitStack() as es:
        sems = {e: es.enter_context(nc.semaphore("s_" + e)) for e in ("pe", "act", "dve", "pool")}
        for s in pr.dcnt:
            sems["d:" + s] = es.enter_context(nc.semaphore("d_" + s))
        block = es.enter_context(nc.Block())

        def run(engname):
            def body(eng):
                waited = {}
                for fn, waits, tok in pr.ops[engname]:
                    for (sn, val) in waits:
                        if sn == "pe" and engname == "pe":
                            continue
                        if waited.get(sn, 0) >= val:
                            continue
                        waited[sn] = val
                        eng.wait_ge(sems[sn], val)
                    ins = fn(eng)
                    ins.then_inc(sems[tok[0]], 16 if tok[0].startswith("d:") else 1)
                if engname == "sp":
                    fin = [(e_, pr.cnt[e_]) for e_ in ("pe", "act", "dve", "pool")] + [("d:" + s_, n_) for s_, n_ in pr.dcnt.items()]
                    for (sn, val) in fin:
                        if val > 0:
                            eng.wait_ge(sems[sn], val)
            return body
        block.tensor(run("pe"))
        block.scalar(run("act"))
        block.vector(run("dve"))
        block.gpsimd(run("pool"))
        block.sync(run("sp"))
    return nc


_CACHE = {}


def _host_prep(inp):
    f = np.float32
    w_in = np.asarray(inp["w_in"][0], f)
    w3 = w_in.reshape(KC, 128, -1)
    win = np.empty((128, WIN_COLS), f)
    for name, cols in BLOCKS:
        off, W = BLK_OFF[name]
        blk = w3[:, :, cols[0]:cols[-1] + 1]
        win[:, off:off + KC * W] = blk.transpose(1, 0, 2).reshape(128, KC * W)
    w_out = np.asarray(inp["w_out"][0], f)
    wout = np.ascontiguousarray(w_out.reshape(64, 128, 16, 256).transpose(2, 1, 0, 3).reshape(16, 128, 64 * 256))
    wa = np.asarray(inp["lru_wa"][0], f).reshape(16, 2, 128, 256)
    wx = np.asarray(inp["lru_wx"][0], f).reshape(16, 2, 128, 256)
    lruw = np.ascontiguousarray(np.stack([wa, wx], 1).transpose(0, 3, 1, 2, 4).reshape(16, 128, 1024))
    params = np.zeros((128, NPAR), f)
    params[:, P_DTB:P_DTB + 64] = np.asarray(inp["ssd_dt_bias"][0], f)[None, :]
    params[:, P_ALOG:P_ALOG + 64] = np.asarray(inp["ssd_a_log"][0], f)[None, :]
    params[:, P_DH:P_DH + 64] = np.asarray(inp["ssd_d"][0], f)[None, :]
    cw = np.concatenate([np.asarray(inp["ssd_conv_w"][0], f), np.asarray(inp["lru_conv_w"][0], f)], 1)
    cbias = np.concatenate([np.asarray(inp["ssd_conv_b"][0], f), np.asarray(inp["lru_conv_b"][0], f)], 0)
    cp = np.concatenate([cw, cbias[None, :]], 0)
    params[:, P_CONV:P_CONV + 400] = cp.reshape(5, 80, 128).transpose(2, 1, 0).reshape(128, 400)
    params[:, P_NORMW:P_NORMW + 32] = np.asarray(inp["ssd_norm_w"][0], f).reshape(32, 128).T
    params[:, P_LBA:P_LBA + 32] = np.asarray(inp["lru_ba"][0], f).reshape(32, 128).T
    params[:, P_LBX:P_LBX + 32] = np.asarray(inp["lru_bx"][0], f).reshape(32, 128).T
    params[:, P_LAM:P_LAM + 32] = np.asarray(inp["lru_lambda"][0], f).reshape(32, 128).T
    consts = np.zeros((128, 512), f)
    j = np.arange(128)
    consts[:, 0:128] = np.eye(128, dtype=f)
    consts[:, 128:256] = (j[:, None] <= j[None, :]).astype(f)
    consts[:, 256:384] = (j[:, None] > j[None, :]).astype(f)
    consts[:, 384:512] = 1.0
    lngb = np.concatenate([np.broadcast_to(np.asarray(inp["ln_g"][0], f)[None, :], (128, D)),
                           np.broadcast_to(np.asarray(inp["ln_b"][0], f)[None, :], (128, D))], 1)
    lngb = np.ascontiguousarray(lngb)
    x = np.asarray(inp["x"], f)
    in_maps = []
    for core in range(8):
        b, h = core // 2, core % 2
        xm = x[b, h * NTOK:(h + 1) * NTOK]
        xw = x[b, 0:NTOK] if h == 1 else np.zeros((NTOK, D), f)

        def tr(a):
            return a.T.reshape(KC, 128, NTOK).transpose(1, 0, 2).reshape(128, KC * NTOK)
        xT = np.ascontiguousarray(np.stack([tr(xw), tr(xm)], 0))
        p = params.copy()
        p[:, P_FLAG] = float(h)
        in_maps.append({"xT": xT, "xres": np.ascontiguousarray(xm), "win": win, "wout": wout, "lruw": lruw,
                        "params": p, "consts": consts, "lngb": lngb})
    return in_maps


def kernel(**inputs):
    if "nc" not in _CACHE:
        _CACHE["nc"] = build_nc()
    nc = _CACHE["nc"]
    in_maps = _host_prep(inputs)
    res = run_bass_kernel_spmd(nc, in_maps, core_ids=list(range(8)))
    outp = np.empty((4, 2048, D), np.float32)
    for core in range(8):
        b, h = core // 2, core % 2
        outp[b, h * NTOK:(h + 1) * NTOK] = res.results[core]["out"]
    return outp
```
